# Optimizing a Trainium2 kernel written in Bass

```python
import jax, jax.numpy as jnp
from jax import lax
import numpy as np

D_MODEL = 1024
BATCH = 8
SEQ = 2048
DEPTH = 2

W_POOL = D_MODEL // 2
W_SSM = D_MODEL // 2
W_LRU = D_MODEL // 2
W_GMLP = D_MODEL // 2
N_BRANCH = 4
W_BRANCH = D_MODEL // 2
POOL_WINDOWS = (2, 4, 8, 16)
POOL_GROUPS = 4
POOL_GC = W_POOL // POOL_GROUPS
SSM_H = 16
SSM_G = W_SSM // SSM_H
SSM_P = 64
SSM_DT_MIN = 1e-3
SSM_DT_MAX = 1e-1
LRU_BLOCKS = 8
LRU_BC = W_LRU // LRU_BLOCKS
LRU_C = 8.0
CONV_W = 4
GMLP_CHUNK = 128
GMLP_GROUPS = 4
GMLP_GC = W_GMLP // GMLP_GROUPS
D_FF = 4 * D_MODEL
D_IN = W_POOL + W_SSM + 2 * W_LRU + 2 * W_GMLP
SPLITS = (W_POOL, W_POOL + W_SSM, W_POOL + W_SSM + W_LRU, W_POOL + W_SSM + 2 * W_LRU)
EPS = 1e-6

kernel_name = 'hybrid_gated_bidir_encoder'


def rmsnorm(x, g):
    xf = x.astype(jnp.float32)
    r = lax.rsqrt(jnp.mean(xf * xf, axis=-1, keepdims=True) + EPS)
    return (xf * r).astype(x.dtype) * g


def _real_combine(e1, e2):
    a1, b1 = e1
    a2, b2 = e2
    return a2 * a1, a2 * b1 + b2


def _cplx_combine(e1, e2):
    a1r, a1i, b1r, b1i = e1
    a2r, a2i, b2r, b2i = e2
    return (a2r * a1r - a2i * a1i,
            a2r * a1i + a2i * a1r,
            a2r * b1r - a2i * b1i + b2r,
            a2r * b1i + a2i * b1r + b2i)


def pool_mixer(u, w_pool, s_pool):
    Bsz, L, _ = u.shape
    ug = u.reshape(Bsz, L, POOL_GROUPS, POOL_GC)
    cs = jnp.cumsum(ug.astype(jnp.float32), axis=1)
    cs = jnp.pad(cs, ((0, 0), (1, 0), (0, 0), (0, 0)))
    t = jnp.arange(L)[:, None]
    win = jnp.array(POOL_WINDOWS, dtype=jnp.int32)[None, :]
    lo = jnp.clip(t - win // 2, 0, L)
    hi = jnp.clip(t - win // 2 + win, 0, L)
    gi = jnp.arange(POOL_GROUPS)[None, :]
    wsum = cs[:, hi, gi] - cs[:, lo, gi]
    cnt = (hi - lo).astype(jnp.float32)[None, :, :, None]
    pooled = (wsum / cnt).astype(u.dtype) - ug
    y = jnp.einsum('blgc,gcd->blgd', pooled, w_pool)
    return y.reshape(Bsz, L, W_POOL) * s_pool


def ssm_mixer(u, lam_re, lam_im, log_dt, b_re, b_im, c_re, c_im, d_skip, glu_w, glu_b):
    Bsz, L, _ = u.shape
    ug = u.reshape(Bsz, L, SSM_G, SSM_H).astype(jnp.float32)
    lr = lam_re.astype(jnp.float32)
    li = lam_im.astype(jnp.float32)
    dt = jnp.exp(log_dt.astype(jnp.float32))[..., None]
    mag = jnp.exp(lr * dt)
    ab_re = mag * jnp.cos(li * dt)
    ab_im = mag * jnp.sin(li * dt)
    den = lr * lr + li * li
    nr = ab_re - 1.0
    f_re = (nr * lr + ab_im * li) / den
    f_im = (ab_im * lr - nr * li) / den
    br = b_re.astype(jnp.float32)[None]
    bi = b_im.astype(jnp.float32)[None]
    bb_re = f_re[..., None] * br - f_im[..., None] * bi
    bb_im = f_re[..., None] * bi + f_im[..., None] * br
    y = ug * d_skip.astype(jnp.float32).reshape(SSM_G, SSM_H)
    for dirn, rev in ((0, False), (1, True)):
        x_re = jnp.einsum('blgh,gph->blgp', ug, bb_re[dirn])
        x_im = jnp.einsum('blgh,gph->blgp', ug, bb_im[dirn])
        a_re = jnp.broadcast_to(ab_re[dirn], x_re.shape)
        a_im = jnp.broadcast_to(ab_im[dirn], x_im.shape)
        _, _, h_re, h_im = lax.associative_scan(_cplx_combine, (a_re, a_im, x_re, x_im), reverse=rev, axis=1)
        y = y + jnp.einsum('blgp,ghp->blgh', h_re, c_re[dirn].astype(jnp.float32)) \
              - jnp.einsum('blgp,ghp->blgh', h_im, c_im[dirn].astype(jnp.float32))
    y = jax.nn.gelu(y.reshape(Bsz, L, W_SSM)).astype(u.dtype)
    return y * jax.nn.sigmoid(y @ glu_w + glu_b)


def rglru_mixer(xr, xg, conv_w, conv_b, w_a, b_a, w_x, b_x, lam):
    Bsz, L, _ = xr.shape
    left = CONV_W // 2
    xp = jnp.pad(xr, ((0, 0), (left, CONV_W - 1 - left), (0, 0)))
    xc = conv_b + sum(xp[:, k:k + L] * conv_w[k] for k in range(CONV_W))
    xb = xc.reshape(Bsz, L, LRU_BLOCKS, LRU_BC)
    xcf = xc.astype(jnp.float32)
    h = jnp.zeros_like(xcf)
    for dirn, rev in ((0, False), (1, True)):
        r = jax.nn.sigmoid(jnp.einsum('blkc,kcd->blkd', xb, w_a[dirn]).reshape(Bsz, L, W_LRU) + b_a[dirn])
        i = jax.nn.sigmoid(jnp.einsum('blkc,kcd->blkd', xb, w_x[dirn]).reshape(Bsz, L, W_LRU) + b_x[dirn])
        log_a = LRU_C * r.astype(jnp.float32) * jax.nn.log_sigmoid(lam[dirn].astype(jnp.float32))
        a = jnp.exp(log_a)
        mult = jnp.sqrt(-jnp.expm1(2.0 * log_a))
        xin = mult * i.astype(jnp.float32) * xcf
        _, hh = lax.associative_scan(_real_combine, (a, xin), reverse=rev, axis=1)
        h = h + hh
    return h.astype(xr.dtype) * jax.nn.gelu(xg)


def gmlp_mixer(z, norm_g, w_s, b_s):
    Bsz, L, _ = z.shape
    u, v = jnp.split(jax.nn.gelu(z), 2, axis=-1)
    v = rmsnorm(v, norm_g)
    vc = v.reshape(Bsz, L // GMLP_CHUNK, GMLP_CHUNK, GMLP_GROUPS, GMLP_GC)
    sv = jnp.einsum('gqp,bnpgc->bnqgc', w_s, vc) + b_s.T[None, None, :, :, None]
    return u * sv.reshape(Bsz, L, W_GMLP)


def setup_inputs(seed: int = 0) -> dict:
    key = jax.random.key(seed)
    it = iter(jax.random.split(key, 48))
    nk = lambda: next(it)
    f32 = jnp.float32

    def nrm(shape, scale):
        return jax.random.normal(nk(), shape, f32) * scale

    def gain(shape):
        return 1.0 + 0.05 * jax.random.normal(nk(), shape, f32)

    Dp = DEPTH
    n_idx = jnp.arange(SSM_P, dtype=f32)
    lam_re = -0.5 + 0.01 * jax.random.normal(nk(), (Dp, 2, SSM_G, SSM_P), f32)
    lam_im = jnp.pi * n_idx + 0.01 * jax.random.normal(nk(), (Dp, 2, SSM_G, SSM_P), f32)
    log_dt = jax.random.uniform(nk(), (Dp, 2, SSM_G), f32, np.log(SSM_DT_MIN), np.log(SSM_DT_MAX))
    a0 = jax.random.uniform(nk(), (Dp, 2, W_LRU), f32, 0.9, 0.999)
    s0 = a0 ** (1.0 / LRU_C)
    lru_lam = jnp.log(s0) - jnp.log1p(-s0)
    return {
        'x': nrm((BATCH, SEQ, D_MODEL), 1.0),
        'c': nrm((BATCH, D_MODEL), 1.0),
        'ada_w': nrm((Dp, D_MODEL, 6 * D_MODEL), 0.5 * D_MODEL ** -0.5),
        'ada_b': nrm((Dp, 6 * D_MODEL), 0.02),
        'norm1_g': gain((Dp, D_MODEL)),
        'w_in': nrm((Dp, D_MODEL, D_IN), D_MODEL ** -0.5),
        'pool_w': nrm((Dp, POOL_GROUPS, POOL_GC, POOL_GC), POOL_GC ** -0.5),
        'pool_scale': gain((Dp, W_POOL)),
        'ssm_lam_re': lam_re,
        'ssm_lam_im': lam_im,
        'ssm_log_dt': log_dt,
        'ssm_b_re': nrm((Dp, SSM_G, SSM_P, SSM_H), (2 * SSM_H) ** -0.5),
        'ssm_b_im': nrm((Dp, SSM_G, SSM_P, SSM_H), (2 * SSM_H) ** -0.5),
        'ssm_c_re': nrm((Dp, 2, SSM_G, SSM_H, SSM_P), (2 * SSM_P) ** -0.5),
        'ssm_c_im': nrm((Dp, 2, SSM_G, SSM_H, SSM_P), (2 * SSM_P) ** -0.5),
        'ssm_d': nrm((Dp, W_SSM), 0.5),
        'ssm_glu_w': nrm((Dp, W_SSM, W_SSM), W_SSM ** -0.5),
        'ssm_glu_b': nrm((Dp, W_SSM), 0.02),
        'lru_conv_w': nrm((Dp, CONV_W, W_LRU), CONV_W ** -0.5),
        'lru_conv_b': nrm((Dp, W_LRU), 0.02),
        'lru_wa': nrm((Dp, 2, LRU_BLOCKS, LRU_BC, LRU_BC), LRU_BC ** -0.5),
        'lru_ba': nrm((Dp, 2, W_LRU), 0.02),
        'lru_wx': nrm((Dp, 2, LRU_BLOCKS, LRU_BC, LRU_BC), LRU_BC ** -0.5),
        'lru_bx': nrm((Dp, 2, W_LRU), 0.02),
        'lru_lam': lru_lam,
        'gmlp_norm_g': gain((Dp, W_GMLP)),
        'gmlp_ws': nrm((Dp, GMLP_GROUPS, GMLP_CHUNK, GMLP_CHUNK), 0.5 * GMLP_CHUNK ** -0.5),
        'gmlp_bs': gain((Dp, GMLP_GROUPS, GMLP_CHUNK)),
        'w_branch': nrm((Dp, N_BRANCH, W_BRANCH, D_MODEL), W_BRANCH ** -0.5),
        'w_gate': nrm((Dp, D_MODEL, N_BRANCH * D_MODEL), D_MODEL ** -0.5),
        'b_gate': nrm((Dp, N_BRANCH * D_MODEL), 0.02),
        'w_out': nrm((Dp, D_MODEL, D_MODEL), D_MODEL ** -0.5),
        'norm2_g': gain((Dp, D_MODEL)),
        'w_ff1': nrm((Dp, D_MODEL, D_FF), D_MODEL ** -0.5),
        'w_ff2': nrm((Dp, D_FF, D_MODEL), D_FF ** -0.5),
        'final_g': gain((D_MODEL,)),
    }


def reference(x, c, ada_w, ada_b, norm1_g, w_in, pool_w, pool_scale,
              ssm_lam_re, ssm_lam_im, ssm_log_dt, ssm_b_re, ssm_b_im, ssm_c_re, ssm_c_im,
              ssm_d, ssm_glu_w, ssm_glu_b,
              lru_conv_w, lru_conv_b, lru_wa, lru_ba, lru_wx, lru_bx, lru_lam,
              gmlp_norm_g, gmlp_ws, gmlp_bs,
              w_branch, w_gate, b_gate, w_out, norm2_g, w_ff1, w_ff2, final_g):
    Bsz, L, _ = x.shape
    cond = jax.nn.silu(c)
    for l in range(DEPTH):
        mod = cond @ ada_w[l] + ada_b[l]
        sh1, sc1, g1, sh2, sc2, g2 = [m[:, None, :] for m in jnp.split(mod, 6, axis=-1)]
        h = rmsnorm(x, norm1_g[l]) * (1.0 + sc1) + sh1
        z = h @ w_in[l]
        zp, zs, zr, zg, zm = jnp.split(z, SPLITS, axis=-1)
        ya = pool_mixer(zp, pool_w[l], pool_scale[l])
        yb = ssm_mixer(zs, ssm_lam_re[l], ssm_lam_im[l], ssm_log_dt[l], ssm_b_re[l], ssm_b_im[l],
                       ssm_c_re[l], ssm_c_im[l], ssm_d[l], ssm_glu_w[l], ssm_glu_b[l])
        yc = rglru_mixer(zr, zg, lru_conv_w[l], lru_conv_b[l], lru_wa[l], lru_ba[l],
                         lru_wx[l], lru_bx[l], lru_lam[l])
        yd = gmlp_mixer(zm, gmlp_norm_g[l], gmlp_ws[l], gmlp_bs[l])
        ys = jnp.stack([ya, yb, yc, yd], axis=2)
        proj = jnp.einsum('blkw,kwd->blkd', ys, w_branch[l])
        gates = jax.nn.sigmoid(h @ w_gate[l] + b_gate[l]).reshape(Bsz, L, N_BRANCH, D_MODEL)
        merged = jnp.einsum('blkd,blkd->bld', gates, proj)
        x = x + g1 * (merged @ w_out[l])
        h2 = rmsnorm(x, norm2_g[l]) * (1.0 + sc2) + sh2
        f = jnp.square(jax.nn.relu(h2 @ w_ff1[l])) @ w_ff2[l]
        x = x + g2 * f
    return rmsnorm(x, final_g)
```

```python
import numpy as np
from contextlib import ExitStack
import concourse.bass as bass
import concourse.mybir as mybir
from concourse.bass_utils import run_bass_kernel_spmd

F32 = mybir.dt.float32
BF16 = mybir.dt.bfloat16
I32 = mybir.dt.int32
AF = mybir.ActivationFunctionType
ALU = mybir.AluOpType

NDMASEM = 8
L = 2048
D = 1024
DEPTH = 2
EPS = 1e-6
import os as _os
DBGL = int(_os.environ.get("DBGL", "0"))
MAGIC = 12582912.0
TWO_PI = float(2 * np.pi)
PVROWS = 160
R_ADAB, R_N1G, R_PSC, R_SSD, R_GLUB, R_CW, R_CB, R_BA, R_BX, R_LAM, R_BG, R_N2G, R_FG = 0, 48, 56, 60, 64, 68, 84, 88, 96, 104, 112, 144, 152


class Sched:
    ENG = ("pe", "act", "dve", "pool", "sp")

    def __init__(self, nc, stack):
        self.nc = nc
        self.prog = {e: [] for e in self.ENG}
        self.count = {e: 0 for e in self.ENG}
        self.sem = {}
        for e in ("pe", "act", "dve", "pool"):
            self.sem[e] = stack.enter_context(nc.semaphore("s_" + e))
        self.dq = {}
        for q in ("sp", "pool"):
            for i in range(NDMASEM):
                self.sem[(q, i)] = stack.enter_context(nc.semaphore(f"d_{q}{i}"))
            self.dq[q] = 0
        self.seen = {e: {} for e in self.ENG}
        self.lastw = {}
        self.readers = {}
        self.ninstr = 0

    def _need(self, eng, tickets):
        waits = {}
        for t in tickets:
            if t is None:
                continue
            k, v = t
            if self.seen[eng].get(k, 0) >= v:
                continue
            if waits.get(k, 0) < v:
                waits[k] = v
        for k, v in waits.items():
            self.seen[eng][k] = v
        return list(waits.items())

    def _deps(self, reads, writes):
        ts = []
        for r in reads:
            ts.append(self.lastw.get(r))
        for w in writes:
            ts.append(self.lastw.get(w))
            ts.extend(self.readers.get(w, ()))
        return ts

    def _commit(self, ticket, reads, writes):
        for r in reads:
            self.readers.setdefault(r, []).append(ticket)
        for w in writes:
            self.lastw[w] = ticket
            self.readers[w] = []

    def op(self, eng, fn, reads=(), writes=(), inc=True):
        deps = self._deps(reads, writes)
        if eng == "pe":
            deps = [t for t in deps if t is not None and t[0] != "pe"]
        waits = self._need(eng, deps)
        if inc:
            self.count[eng] += 1
            ticket = (eng, self.count[eng])
        else:
            ticket = (eng, self.count[eng] + 1)
        self.prog[eng].append((fn, waits, self.sem[eng] if inc else None, 1))
        self._commit(ticket, reads, writes)
        self.ninstr += 1
        return ticket

    def dma(self, q, fn, reads=(), writes=()):
        i = self.dq[q]
        self.dq[q] += 1
        slot, rnd = i % NDMASEM, i // NDMASEM
        key = (q, slot)
        deps = self._deps(reads, writes)
        if rnd > 0:
            deps.append((key, 16 * rnd))
        waits = self._need(q, deps)
        ticket = (key, 16 * (rnd + 1))
        self.prog[q].append((fn, waits, self.sem[key], 16))
        self._commit(ticket, reads, writes)
        self.ninstr += 1
        return ticket

    def all_tickets(self):
        ts = [(e, self.count[e]) for e in ("pe", "act", "dve", "pool") if self.count[e] > 0]
        for q in ("sp", "pool"):
            n = self.dq[q]
            for slot in range(NDMASEM):
                cnt = (n - slot + NDMASEM - 1) // NDMASEM if n > slot else 0
                if cnt > 0:
                    ts.append(((q, slot), 16 * cnt))
        return ts

    def barrier(self):
        ts = self.all_tickets()
        for e in self.ENG:
            waits = self._need(e, ts)
            if waits:
                self.prog[e].append((None, waits, None, 0))
        self.lastw = {}
        self.readers = {}

    def emit(self, block):
        def run(name):
            def body(e):
                for fn, waits, sem, n in self.prog[name]:
                    for k, v in waits:
                        e.wait_ge(self.sem[k], v)
                    if fn is None:
                        continue
                    ins = fn(e)
                    if sem is not None:
                        ins.then_inc(sem, n)
            return body

        block.tensor(run("pe"))
        block.scalar(run("act"))
        block.vector(run("dve"))
        block.gpsimd(run("pool"))
        block.sync(run("sp"))


class Arena:
    def __init__(self, ap, words):
        self.ap = ap
        self.words = words
        self.top = 0
        self.marks = []
        self.peak = 0

    def _take(self, words):
        a = self.ap[:, self.top:self.top + words]
        self.top += words
        self.peak = max(self.peak, self.top)
        assert self.top <= self.words, f"arena overflow {self.top} > {self.words}"
        return a

    def f32(self, cols):
        return self._take(cols)

    def i32(self, cols):
        return self._take(cols).bitcast(I32)

    def bf16(self, cols):
        w = (cols + 1) // 2
        return self._take(w).bitcast(BF16)[:, :cols]

    def push(self):
        self.marks.append(self.top)

    def pop(self):
        self.top = self.marks.pop()


class _Stop(Exception):
    pass


def build_program(dbg=None, stage=99):
    nc = bass.Bass("TRN2", target_bir_lowering=False)
    dram_in = lambda name, shape: nc.dram_tensor(name, list(shape), F32, kind="ExternalInput").ap()
    x_d = dram_in("x", [L, D])
    cb_d = dram_in("cb", [8, 128])
    pm_d = dram_in("pm", [128, 20 * 128])
    Wd = []
    for l in range(DEPTH):
        Wd.append(dict(
            pv=dram_in(f"pv{l}", [PVROWS, 128]),
            adaw=dram_in(f"adaw{l}", [D, 6 * D]),
            win=dram_in(f"win{l}", [D, 3072]),
            poolw=dram_in(f"poolw{l}", [4, 128, 128]),
            ssc=dram_in(f"ssc{l}", [128, 96]),
            ssb=dram_in(f"ssb{l}", [128, 2 * 16 * 16]),
            sscc=dram_in(f"sscc{l}", [128, 2 * 2 * 16 * 16]),
            gluw=dram_in(f"gluw{l}", [512, 512]),
            lrubd=dram_in(f"lrubd{l}", [128, 16 * 128]),
            gng=dram_in(f"gng{l}", [128, 512]),
            gws=dram_in(f"gws{l}", [128, 4 * 128]),
            gbs=dram_in(f"gbs{l}", [1, 512]),
            wbr=dram_in(f"wbr{l}", [4, 512, D]),
            wgate=dram_in(f"wgate{l}", [D, 4 * D]),
            wout=dram_in(f"wout{l}", [D, D]),
            wff1=dram_in(f"wff1{l}", [D, 4 * D]),
            wff2=dram_in(f"wff2{l}", [4 * D, D]),
        ))
    out_d = nc.dram_tensor("out", [L, D], F32, kind="ExternalOutput").ap()
    y_d = nc.dram_tensor("yscr", [4, 128, 4, L], BF16, kind="Internal").ap()
    dbg_d = {}
    if dbg:
        for name, shape in dbg.items():
            dbg_d[name] = nc.dram_tensor("dbg_" + name, list(shape), F32, kind="ExternalOutput").ap()

    with ExitStack() as st:
        S = Sched(nc, st)
        AW = 48600
        arena_t = st.enter_context(nc.sbuf_tensor("arena", [128, AW], F32))
        A = Arena(arena_t[:], AW)
        banks = [st.enter_context(nc.psum_tensor(f"ps{i}", [128, 512], F32)) for i in range(8)]
        block = st.enter_context(nc.Block())
        bank_state = {"i": 0, "free": list(range(8))}

        def PS():
            fr = bank_state["free"]
            b = fr[bank_state["i"] % len(fr)]
            bank_state["i"] += 1
            return banks[b][:], ("ps", b)

        def MM(out, lhsT, rhs, start, stop, R, W, inc):
            return S.op("pe", lambda e: e.matmul(out, lhsT=lhsT, rhs=rhs, start=start, stop=stop), R, W, inc)

        def TR(out, in_, ident, R, W):
            return S.op("pe", lambda e: e.transpose(out=out, in_=in_, identity=ident), R, W, True)

        def ACT(out, in_, func, R, W, bias=None, scale=None, accum=None):
            kw = {}
            if bias is not None:
                kw["bias"] = bias
            if scale is not None:
                kw["scale"] = scale
            if accum is not None:
                kw["accum_out"] = accum
            return S.op("act", lambda e: e.activation(out=out, in_=in_, func=func, **kw), R, W)

        def TT(eng, out, a, b, op, R, W):
            return S.op(eng, lambda e: e.tensor_tensor(out=out, in0=a, in1=b, op=op), R, W)

        def TS(eng, out, a, s1, s2, op0, op1, R, W):
            if s2 is None:
                return S.op(eng, lambda e: e.tensor_scalar(out=out, in0=a, scalar1=s1, scalar2=None, op0=op0), R, W)
            return S.op(eng, lambda e: e.tensor_scalar(out=out, in0=a, scalar1=s1, scalar2=s2, op0=op0, op1=op1), R, W)

        def STT(eng, out, a, s, b, op0, op1, R, W):
            eng = "dve"
            return S.op(eng, lambda e: e.scalar_tensor_tensor(out=out, in0=a, scalar=s, in1=b, op0=op0, op1=op1), R, W)

        def CP(eng, out, in_, R, W):
            if eng == "act":
                return S.op("act", lambda e: e.activation(out=out, in_=in_, func=AF.Copy), R, W)
            return S.op(eng, lambda e: e.tensor_copy(out=out, in_=in_), R, W)

        def MSET(eng, out, val, W):
            return S.op(eng, lambda e: e.memset(out, val), (), W)

        def SCAN(out, d0, d1, init, R, W):
            return S.op("dve", lambda e: e.tensor_tensor_scan(out=out, data0=d0, data1=d1, initial=init, op0=ALU.mult, op1=ALU.add), R, W)

        def RECIP(out, in_, R, W):
            return S.op("dve", lambda e: e.reciprocal(out=out, in_=in_), R, W)

        def DMA(q, out, in_, R, W):
            return S.dma(q, lambda e: e.dma_start(out=out, in_=in_), R, W)

        def DBG(name, src_ap, R):
            if name in dbg_d:
                DMA("pool", dbg_d[name].rearrange("p (c t) -> p c t", c=src_ap.shape[1]) if len(src_ap.shape) == 3 else dbg_d[name], src_ap, R, [("dbgout", name)])

        xT = A.f32(8 * L).rearrange("p (c t) -> p c t", c=8)
        ident = A.f32(128)
        onesb = A.bf16(128)
        pcol = [A.f32(PVROWS) for _ in range(DEPTH)]
        modt = [A.f32(48) for _ in range(DEPTH)]
        modA = [A.f32(16) for _ in range(DEPTH)]
        cond = A.f32(8)
        epsc = A.f32(1)
        halfpi = A.f32(1)
        onec = A.f32(1)
        magicc = A.f32(1)

        MSET("pool", ident, 1.0, ["ident"])
        S.op("pool", lambda e: e.affine_select(out=ident, in_=ident, pattern=[[-1, 128]], compare_op=ALU.is_equal,
                                                fill=0.0, base=0, channel_multiplier=1), ["ident"], ["ident"])
        MSET("pool", onesb, 1.0, ["onesb"])
        MSET("pool", epsc, EPS, ["epsc"])
        MSET("pool", halfpi, float(np.pi / 2), ["halfpi"])
        MSET("pool", onec, 1.0, ["onec"])
        MSET("pool", magicc, MAGIC, ["magicc"])

        A.push()
        for l in range(DEPTH):
            pvt = A.f32(2 * 128).rearrange("p (a b) -> p a b", a=2)
            DMA("sp", pvt[:, 0, :], Wd[l]["pv"][0:128, :], [], [("pvt", l)])
            DMA("sp", pvt[0:32, 1, :], Wd[l]["pv"][128:160, :], [], [("pvt", l)])
            ps, pk = PS()
            TR(ps[:, 0:128], pvt[:, 0, :], ident, [("pvt", l), "ident"], [pk])
            TR(ps[:, 128:160], pvt[0:32, 1, :], ident[0:32, 0:32], [("pvt", l), "ident"], [pk])
            CP("dve", pcol[l], ps[:, 0:PVROWS], [pk], [("pcol", l)])
        cbt = A.f32(128)
        DMA("sp", cbt[0:8, :], cb_d, [], ["cbt"])
        ps, pk = PS()
        TR(ps[:, 0:8], cbt[0:8, :], ident[0:8, 0:8], ["cbt", "ident"], [pk])
        ACT(cond, ps[:, 0:8], AF.Silu, [pk], ["cond"])
        def mod_piece_dma(l, kc, part, nparts, buf, bkey):
            Wc = 6144 // nparts
            c0 = part * Wc
            DMA("sp", buf[:, 0:Wc], Wd[l]["adaw"][kc * 128:(kc + 1) * 128, c0:c0 + Wc], [], [bkey])

        def mod_piece(l, kc, part, nparts, buf, bkey, dma=True, defer_add=False):
            Wc = 6144 // nparts
            nj = Wc // 128
            j0 = part * nj
            if dma:
                mod_piece_dma(l, kc, part, nparts, buf, bkey)
            ps, pk = PS()
            for j in range(nj):
                MM(ps[:, j:j + 1], buf[:, j * 128:(j + 1) * 128], cond[:, kc:kc + 1], True, True, [bkey, "cond"], [pk], j == nj - 1)
            dst = modt[l][:, j0:j0 + nj]
            mk = ("modp", l, part)

            def do_add():
                if kc == 0:
                    TT("dve", dst, ps[:, 0:nj], pcol[l][:, R_ADAB + j0:R_ADAB + j0 + nj], ALU.add, [pk, ("pcol", l)], [mk])
                else:
                    TT("dve", dst, ps[:, 0:nj], dst, ALU.add, [pk, mk], [mk])
            if defer_add:
                return do_add
            do_add()

        def mod_finish(l, nparts):
            mks = [("modp", l, p) for p in range(nparts)]
            STT("dve", modA[l][:, 0:8], modt[l][:, 8:16], 1.0, pcol[l][:, R_N1G:R_N1G + 8], ALU.add, ALU.mult,
                mks + [("pcol", l)], [("modA", l), ("mod", l)])
            STT("dve", modA[l][:, 8:16], modt[l][:, 32:40], 1.0, pcol[l][:, R_N2G:R_N2G + 8], ALU.add, ALU.mult,
                mks + [("pcol", l)], [("modA", l), ("mod", l)])

        awb = [A.f32(3072), A.f32(3072)]
        for kc in range(8):
            for part in range(2):
                bi = (kc * 2 + part) % 2
                mod_piece(0, kc, part, 2, awb[bi], ("aw", bi))
        mod_finish(0, 2)
        xtok = [A.f32(D), A.f32(D)]
        for n in range(16):
            xt = xtok[n % 2]
            DMA("sp", xt, x_d[n * 128:(n + 1) * 128, :], [], [("xtok", n % 2)])
            for cg in range(2):
                ps, pk = PS()
                for c4 in range(4):
                    c = cg * 4 + c4
                    TR(ps[:, c4 * 128:(c4 + 1) * 128], xt[:, c * 128:(c + 1) * 128], ident, [("xtok", n % 2), "ident"], [pk])
                eng = "dve" if cg == 0 else "act"
                CP(eng, xT[:, cg * 4:(cg + 1) * 4, n * 128:(n + 1) * 128], ps.rearrange("p (c t) -> p c t", c=4), [pk],
                   [("x", c, n // 4) for c in range(cg * 4, cg * 4 + 4)])
        S.barrier()
        A.pop()

        def rms_mod(l, which, tts, hT, hkey):
            A.push()
            sq = [A.bf16(8 * 512).rearrange("p (c t) -> p c t", c=8) for _ in range(2)]
            rt = [A.f32(512) for _ in range(2)]
            tmps = [A.f32(512) for _ in range(3)] + [tmp2]
            shc = 0 if which == 0 else 24
            tts = list(tts)
            pend = {}

            def stats_a(i):
                tt = tts[i]
                b = i % 2
                sl = slice(tt * 512, (tt + 1) * 512)
                ACT(sq[b], xT[:, :, sl], AF.Square, [("x", c, tt) for c in range(8)], [("sq", b)])
                ps, pk = PS()
                for c in range(8):
                    MM(ps, onesb, sq[b][:, c, :], c == 0, c == 7, [("sq", b), "onesb"], [pk], c == 7)
                pend[i] = (ps, pk)

            def stats_b(i):
                b = i % 2
                ps, pk = pend.pop(i)
                ACT(rt[b], ps, AF.Sqrt, [pk, "epsc"], [("rt", b)], bias=epsc[:, 0:1], scale=1.0 / D)
                RECIP(rt[b], rt[b], [("rt", b)], [("rt", b)])

            def apply(i):
                tt = tts[i]
                b = i % 2
                sl = slice(tt * 512, (tt + 1) * 512)
                for c in range(8):
                    tm = tmps[c % 4]
                    tk = ("tmp", c % 4)
                    STT("dve", tm, xT[:, c, sl], modA[l][:, which * 8 + c:which * 8 + c + 1], rt[b], ALU.mult, ALU.mult,
                        [("x", c, tt), ("rt", b), ("modA", l)], [tk])
                    dst = hT[:, c, sl] if hT.shape[2] == L else hT[:, c, (tt % 2) * 512:(tt % 2 + 1) * 512]
                    shcol = modt[l][:, shc + c:shc + c + 1]
                    if c % 2 == 0:
                        ACT(dst, tm, AF.Identity, [tk, ("mod", l)], [(hkey, c, tt)], bias=shcol)
                    else:
                        TS("dve", dst, tm, shcol, None, ALU.add, None, [tk, ("mod", l)], [(hkey, c, tt)])

            stats_a(0)
            stats_b(0)
            for i in range(len(tts)):
                if i + 1 < len(tts):
                    stats_a(i + 1)
                apply(i)
                if i + 1 < len(tts):
                    stats_b(i + 1)
            A.pop()

        tmp2 = A.f32(512)

        def load_w(q, dst, src, key):
            return DMA(q, dst, src, [], [key])

        def proj_fm(wt, wkey, ncol_chunks, hT, hkey, tts, sink):
            for j in range(ncol_chunks):
                for tt in tts:
                    ps, pk = PS()
                    for kc in range(8):
                        MM(ps, wt[:, kc, j * 128:(j + 1) * 128], hT[:, kc, tt * 512:(tt + 1) * 512], kc == 0, kc == 7,
                           [wkey, (hkey, kc, tt)], [pk], kc == 7)
                    sink(j, tt, ps, pk)

        persist_top = A.top

        def checkpoint(k):
            if stage == k:
                raise _Stop()
        try:
          checkpoint(0)
          for l in range(DEPTH):
                W = Wd[l]
                pc = pcol[l]
                pck = ("pcol", l)
                A.push()
                A.push()
                zsT = A.bf16(4 * L).rearrange("p (c t) -> p c t", c=4)
                A.push()
                hT = A.bf16(8 * L).rearrange("p (c t) -> p c t", c=8)
                wss = A.bf16(8 * 512).rearrange("p (k n) -> p k n", k=8)
                DMA("pool", wss, W["win"][:, 512:1024].rearrange("(k p) n -> p k n", p=128), [], ["wss"])
                rms_mod(l, 0, range(4), hT, "h")
                if l == DBGL:
                    DBG("h", hT, [("h", c, tt) for c in range(8) for tt in range(4)])

                def sink_zs(j, tt, ps, pk):
                    CP("act" if (j + tt) % 2 == 0 else "dve", zsT[:, j, tt * 512:(tt + 1) * 512], ps, [pk], [("zs", j, tt)])
                proj_fm(wss, "wss", 4, hT, "h", range(4), sink_zs)
                if l == DBGL:
                    DBG("zs", zsT, [("zs", c, tt) for c in range(4) for tt in range(4)])
                S.barrier()
                A.pop()
                checkpoint(1)
                ssc = A.f32(96)
                DMA("sp", ssc, W["ssc"], [], ["ssc"])
                ssb = A.f32(512).rearrange("p (r q h) -> p r q h", r=2, q=16)
                DMA("sp", ssb, W["ssb"].rearrange("p (r q h) -> p r q h", r=2, q=16), [], ["ssb"])
                scc = A.f32(1024).rearrange("p (r d q j) -> p r d q j", r=2, d=2, q=16)
                DMA("sp", scc, W["sscc"].rearrange("p (r d q j) -> p r d q j", r=2, d=2, q=16), [], ["scc"])
                sp_ = A.f32(32 * 12)
                col = lambda i: sp_[:, i * 32:(i + 1) * 32]
                dt, mag, th, t1, t2, cs, sn, abr, abi, fre, fim, phi = [col(i) for i in range(12)]
                lr, li, ldt = ssc[:, 0:32], ssc[:, 32:64], ssc[:, 64:96]
                K = ["sprep"]
                ACT(dt, ldt, AF.Exp, ["ssc"], K)
                TT("dve", t1, lr, dt, ALU.mult, ["ssc"] + K, K)
                ACT(mag, t1, AF.Exp, K, K)
                TT("dve", th, li, dt, ALU.mult, ["ssc"] + K, K)
                TS("dve", t1, th, 1.0 / TWO_PI, MAGIC, ALU.mult, ALU.add, K, K)
                TS("dve", t2, th, 1.0 / TWO_PI, None, ALU.mult, None, K, K)
                STT("dve", phi, t1, MAGIC, t2, ALU.subtract, ALU.subtract, K, K)
                TS("dve", phi, phi, -1.0, None, ALU.mult, None, K, K)
                ACT(sn, phi, AF.Sin, K, K, scale=TWO_PI)
                ACT(t1, phi, AF.Abs, K, K)
                ACT(cs, t1, AF.Sin, K + ["halfpi"], K, scale=-TWO_PI, bias=halfpi[:, 0:1])
                TT("dve", abr, mag, cs, ALU.mult, K, K)
                TT("dve", abi, mag, sn, ALU.mult, K, K)
                TT("dve", t1, lr, lr, ALU.mult, ["ssc"] + K, K)
                TT("dve", t2, li, li, ALU.mult, ["ssc"] + K, K)
                TT("dve", t1, t1, t2, ALU.add, K, K)
                RECIP(t1, t1, K, K)
                TS("dve", abr, abr, -1.0, None, ALU.add, None, K, K)
                TT("dve", fre, abr, lr, ALU.mult, K, K)
                TT("dve", t2, abi, li, ALU.mult, K, K)
                TT("dve", fre, fre, t2, ALU.add, K, K)
                TT("dve", fre, fre, t1, ALU.mult, K, K)
                TT("dve", fim, abi, lr, ALU.mult, K, K)
                TT("dve", t2, abr, li, ALU.mult, K, K)
                TT("dve", fim, fim, t2, ALU.subtract, K, K)
                TT("dve", fim, fim, t1, ALU.mult, K, K)
                bbar = A.f32(2 * 2 * 256).rearrange("p (d r q h) -> p d r q h", d=2, r=2, q=16)
                bt1 = A.f32(256).rearrange("p (q h) -> p q h", q=16)
                bt2 = A.f32(256).rearrange("p (q h) -> p q h", q=16)
                for d_ in range(2):
                    frb = fre[:, d_ * 16:(d_ + 1) * 16].unsqueeze(2).to_broadcast([128, 16, 16])
                    fib = fim[:, d_ * 16:(d_ + 1) * 16].unsqueeze(2).to_broadcast([128, 16, 16])
                    TT("dve", bt1, ssb[:, 0], frb, ALU.mult, ["ssb"] + K, ["bt1"])
                    TT("dve", bt2, ssb[:, 1], fib, ALU.mult, ["ssb"] + K, ["bt2"])
                    TT("dve", bbar[:, d_, 0], bt1, bt2, ALU.subtract, ["bt1", "bt2"], ["bbar"])
                    TT("dve", bt1, ssb[:, 1], frb, ALU.mult, ["ssb"] + K, ["bt1"])
                    TT("dve", bt2, ssb[:, 0], fib, ALU.mult, ["ssb"] + K, ["bt2"])
                    TT("dve", bbar[:, d_, 1], bt1, bt2, ALU.add, ["bt1", "bt2"], ["bbar"])
                Z = A.f32(4 * 128).rearrange("p (i c) -> p i c", i=4)
                BT = A.bf16(2 * 2 * 2 * 4 * 128).rearrange("p (s d r q c) -> p s d r q c", s=2, d=2, r=2, q=4)
                Cpad = A.bf16(2 * 2 * 3 * 4 * 128).rearrange("p (s d v q c) -> p s d v q c", s=2, d=2, v=3, q=4)
                MSET("pool", Z, 0.0, ["Z"])
                MSET("pool", Cpad, 0.0, [("Cpad", 0), ("Cpad", 1)])

                def bm_copy(blk, g):
                    d_, r_ = g // 2, g % 2
                    for qq in range(4):
                        q = blk * 4 + qq
                        for half in range(2):
                            c0 = 16 * (2 * qq + half)
                            CP("pool", Z[half * 64:(half + 1) * 64, qq, c0:c0 + 16], bbar[half * 64:(half + 1) * 64, d_, r_, q, :],
                               ["bbar"], ["Z"])

                def bm_tr(blk, g):
                    d_, r_ = g // 2, g % 2
                    st_ = blk % 2
                    ps, pk = PS()
                    for qq in range(4):
                        TR(ps[:, qq * 128:(qq + 1) * 128], Z[:, qq, :], ident, ["Z", "ident"], [pk])
                    CP("act", BT[:, st_, d_, r_, :, :], ps.rearrange("p (q c) -> p q c", q=4), [pk], [("BT", st_)])

                def bm_cpad(blk, d_):
                    st_ = blk % 2
                    for v in range(3):
                        r_ = 0 if v < 2 else 1
                        sgn = 1.0 if v == 0 else -1.0
                        for qq in range(4):
                            q = blk * 4 + qq
                            for half in range(2):
                                c0 = 16 * (2 * qq + half)
                                hs = slice(half * 64, (half + 1) * 64)
                                TS("pool", Cpad[hs, st_, d_, v, qq, c0:c0 + 16], scc[hs, r_, d_, q, :], sgn, None, ALU.mult, None,
                                   ["scc"], [("Cpad", st_)])

                def build_block_mats(blk):
                    for g in range(4):
                        bm_copy(blk, g)
                        bm_tr(blk, g)
                    for d_ in range(2):
                        bm_cpad(blk, d_)
                wk = A.f32(L)
                iot_i = wk.bitcast(I32)
                iot = A.f32(L)
                S.op("pool", lambda e: e.iota(iot_i, pattern=[[1, L]], base=0, channel_multiplier=0), [], ["ioti"])
                CP("dve", iot, iot_i, ["ioti"], ["iot"])
                carr = A.f32(2 * 4 * 2).rearrange("p (d q r) -> p d q r", d=2, q=4)
                tA = [wk[:, 0:512], wk[:, 1024:1536]]
                tB = [wk[:, 512:1024], wk[:, 1536:2048]]
                m2 = [A.bf16(512) for _ in range(2)]
                m4 = [A.bf16(512) for _ in range(2)]
                sinT = [A.bf16(512) for _ in range(4)]
                cosT = [A.bf16(512) for _ in range(4)]
                xre = [A.bf16(512) for _ in range(2)]
                xim = [A.bf16(512) for _ in range(2)]
                m1 = [A.bf16(512) for _ in range(2)]
                m3 = [A.bf16(512) for _ in range(2)]
                Gre = [A.bf16(512) for _ in range(2)]
                Gim = [A.bf16(512) for _ in range(2)]
                Pp = [[A.bf16(512) for _ in range(4)] for _ in range(2)]
                ygt = A.f32(512)
                ygT = zsT
                ybst = [A.bf16(512), A.bf16(512)]
                mod_next = l + 1 < DEPTH
                if mod_next:
                    awq = [A.f32(1536), A.f32(1536)]
                    mod_jobs = [(kc, part) for kc in range(8) for part in range(4)]
                mod_di = 0
                mod_ci = 0
                mod_adds = []
                it = 0
                for blk in range(4):
                    if blk == 0:
                        build_block_mats(0)
                    st_ = blk % 2
                    ybanks = [0, 1, 2, 3]
                    bank_state["free"] = [4, 5, 6, 7]
                    ykeys = [("ps", b) for b in ybanks]
                    iters = []
                    for d_ in range(2):
                        for step in range(4):
                            tt = step if d_ == 0 else 3 - step
                            for qq in range(4):
                                iters.append((d_, step, tt, qq, it % 2, it % 4))
                                it += 1

                    def stage_a(d_, step, tt, qq, b, b4):
                        k4 = lambda s_: (s_, b4)
                        sl = slice(tt * 512, (tt + 1) * 512)
                        q = blk * 4 + qq
                        kb = lambda s_: (s_, b)
                        phc = phi[:, d_ * 16 + q:d_ * 16 + q + 1]
                        ACT(tA[b], iot[:, sl], AF.Identity, ["iot", "magicc"] + K, [kb("tA")], scale=phc, bias=magicc[:, 0:1])
                        ACT(tB[b], iot[:, sl], AF.Copy, ["iot"] + K, [kb("tB")], scale=phc)
                        STT("dve", tA[b], tA[b], MAGIC, tB[b], ALU.subtract, ALU.subtract, [kb("tA"), kb("tB")], [kb("tA")])
                        ACT(sinT[b4], tA[b], AF.Sin, [kb("tA")], [k4("sinT")], scale=(-TWO_PI if d_ == 0 else TWO_PI))
                        ACT(tB[b], tA[b], AF.Abs, [kb("tA")], [kb("tB")])
                        ACT(cosT[b4], tB[b], AF.Sin, [kb("tB"), "halfpi"], [k4("cosT")], scale=-TWO_PI, bias=halfpi[:, 0:1])
                        psr, pkr = PS()
                        MM(psr, BT[:, st_, d_, 0, qq, :], zsT[:, blk, sl], True, True, [("BT", st_), ("zs", blk, tt)], [pkr], True)
                        psi, pki = PS()
                        MM(psi, BT[:, st_, d_, 1, qq, :], zsT[:, blk, sl], True, True, [("BT", st_), ("zs", blk, tt)], [pki], True)
                        CP("act", xre[b], psr, [pkr], [kb("xre")])
                        CP("act", xim[b], psi, [pki], [kb("xim")])

                    def stage_b1(d_, step, tt, qq, b, b4):
                        kb = lambda s_: (s_, b)
                        k4 = lambda s_: (s_, b4)
                        TT("dve", m1[b], cosT[b4], xre[b], ALU.mult, [k4("cosT"), kb("xre")], [kb("m1")])
                        TT("dve", m2[b], sinT[b4], xim[b], ALU.mult, [k4("sinT"), kb("xim")], [kb("m2")])
                        TT("dve", m3[b], cosT[b4], xim[b], ALU.mult, [k4("cosT"), kb("xim")], [kb("m3")])
                        TT("dve", m4[b], sinT[b4], xre[b], ALU.mult, [k4("sinT"), kb("xre")], [kb("m4")])
                        TT("dve", m3[b], m3[b], m4[b], ALU.subtract, [kb("m3"), kb("m4")], [kb("m3")])
                        TT("dve", m1[b], m1[b], m2[b], ALU.add, [kb("m1"), kb("m2")], [kb("m1")])

                    def stage_b2(d_, step, tt, qq, b, b4):
                        q = blk * 4 + qq
                        kb = lambda s_: (s_, b)
                        rho = mag[:, d_ * 16 + q:d_ * 16 + q + 1].to_broadcast([128, 512])
                        ck = ("carr", d_, qq)
                        if d_ == 0:
                            o_re, i_re, o_im, i_im = Gre[b], m1[b], Gim[b], m3[b]
                            last = slice(511, 512)
                        else:
                            o_re, i_re, o_im, i_im = Gre[b][:, ::-1], m1[b][:, ::-1], Gim[b][:, ::-1], m3[b][:, ::-1]
                            last = slice(0, 1)
                        init_re = 0.0 if step == 0 else carr[:, d_, qq, 0:1]
                        init_im = 0.0 if step == 0 else carr[:, d_, qq, 1:2]
                        SCAN(o_im, rho, i_im, init_im, [kb("m3"), ck] + K, [kb("Gim")])
                        SCAN(o_re, rho, i_re, init_re, [kb("m1"), ck] + K, [kb("Gre")])
                        if step < 3:
                            CP("act", carr[:, d_, qq, 0:1], Gre[b][:, last], [kb("Gre")], [ck])
                            CP("act", carr[:, d_, qq, 1:2], Gim[b][:, last], [kb("Gim")], [ck])

                    def stage_b3(d_, step, tt, qq, b, b4):
                        kb = lambda s_: (s_, b)
                        k4 = lambda s_: (s_, b4)
                        TT("dve", Pp[b][0], cosT[b4], Gre[b], ALU.mult, [k4("cosT"), kb("Gre")], [kb("P0")])
                        TT("dve", Pp[b][1], sinT[b4], Gim[b], ALU.mult, [k4("sinT"), kb("Gim")], [kb("P1")])
                        TT("dve", Pp[b][2], sinT[b4], Gre[b], ALU.mult, [k4("sinT"), kb("Gre")], [kb("P2")])
                        TT("dve", Pp[b][3], cosT[b4], Gim[b], ALU.mult, [k4("cosT"), kb("Gim")], [kb("P3")])
                        yps = banks[ybanks[tt]][:]
                        first = (d_ == 0 and qq == 0)
                        lastm = (d_ == 1 and qq == 3)
                        vs = [0, 1, 2, 2]
                        for m in range(4):
                            MM(yps, Cpad[:, st_, d_, vs[m], qq, :], Pp[b][m], first and m == 0, lastm and m == 3,
                               [("Cpad", st_), kb("P%d" % m)], [ykeys[tt]], m == 3)

                    n_it = len(iters)
                    for t_ in range(-2, n_it + 1):
                        if blk < 3:
                            if t_ in (2, 6, 10, 14):
                                bm_copy(blk + 1, (t_ - 2) // 4)
                            if t_ in (4, 8, 12, 16):
                                bm_tr(blk + 1, (t_ - 4) // 4)
                            if t_ in (18, 22):
                                bm_cpad(blk + 1, (t_ - 18) // 4)
                        if mod_next and t_ >= 0 and t_ % 4 == 1 and mod_di < len(mod_jobs):
                            kc_, part_ = mod_jobs[mod_di]
                            bi_ = mod_di % 2
                            mod_piece_dma(l + 1, kc_, part_, 4, awq[bi_], ("awq", bi_))
                            mod_di += 1
                        if mod_next and t_ >= 0 and t_ % 4 == 0 and mod_adds:
                            mod_adds.pop(0)()
                            if mod_ci == len(mod_jobs) and not mod_adds:
                                mod_finish(l + 1, 4)
                        if mod_next and t_ >= 0 and t_ % 4 == 3 and mod_ci < mod_di:
                            kc_, part_ = mod_jobs[mod_ci]
                            bi_ = mod_ci % 2
                            mod_adds.append(mod_piece(l + 1, kc_, part_, 4, awq[bi_], ("awq", bi_), dma=False, defer_add=True))
                            mod_ci += 1
                        if 0 <= t_ + 2 < n_it:
                            stage_a(*iters[t_ + 2])
                        if 0 <= t_ + 1 < n_it:
                            stage_b1(*iters[t_ + 1])
                        if 0 <= t_ < n_it:
                            stage_b2(*iters[t_])
                        if 0 <= t_ - 1 < n_it:
                            stage_b3(*iters[t_ - 1])
                    for tt in range(4):
                        sl = slice(tt * 512, (tt + 1) * 512)
                        STT("dve", ygt, zsT[:, blk, sl], pc[:, R_SSD + blk:R_SSD + blk + 1], banks[ybanks[tt]][:], ALU.mult, ALU.add,
                            [("zs", blk, tt), ykeys[tt], pck], ["ygt"])
                        ACT(ygT[:, blk, sl], ygt, AF.Gelu, ["ygt"], [("yg", blk, tt)])
                    bank_state["free"] = list(range(8))
                while mod_next and mod_adds:
                    mod_adds.pop(0)()
                    if mod_ci == len(mod_jobs) and not mod_adds:
                        mod_finish(l + 1, 4)
                wgl = A.bf16(4 * 512).rearrange("p (k n) -> p k n", k=4)
                DMA("pool", wgl, W["gluw"].rearrange("(k p) n -> p k n", p=128), [], ["wgl"])
                sg = ygt
                for ob in range(4):
                    for tt in range(4):
                        sl = slice(tt * 512, (tt + 1) * 512)
                        ps, pk = PS()
                        for kc in range(4):
                            MM(ps, wgl[:, kc, ob * 128:(ob + 1) * 128], ygT[:, kc, sl], kc == 0, kc == 3, ["wgl", ("yg", kc, tt)], [pk], kc == 3)
                        ACT(sg, ps, AF.Sigmoid, [pk, pck], ["ygt"], bias=pc[:, R_GLUB + ob:R_GLUB + ob + 1])
                        bb = (ob * 4 + tt) % 2
                        TT("dve", ybst[bb], sg, ygT[:, ob, sl], ALU.mult, ["ygt", ("yg", ob, tt)], [("ybst", bb)])
                        DMA("sp", y_d[1, :, ob, sl], ybst[bb], [("ybst", bb)], [("yD", 1, tt)])
                        if l == DBGL and "yb" in dbg_d:
                            DMA("pool", dbg_d["yb"][:, ob * L + tt * 512:ob * L + (tt + 1) * 512], ybst[bb], [("ybst", bb)], [("dbgo", "yb", ob, tt)])
                S.barrier()
                A.pop()
                checkpoint(2)
                hT = A.bf16(8 * L).rearrange("p (c t) -> p c t", c=8)
                A.push()
                wlr = A.bf16(8 * 1024).rearrange("p (k n) -> p k n", k=8)
                DMA("pool", wlr, W["win"][:, 1024:2048].rearrange("(k p) n -> p k n", p=128), [], ["wlr"])
                lbd = A.bf16(16 * 128).rearrange("p (i c) -> p i c", i=16)
                DMA("pool", lbd, W["lrubd"].rearrange("p (i c) -> p i c", i=16), [], ["lbd"])
                rms_mod(l, 0, range(4), hT, "h")
                S.barrier()
                cl = A.f32(8)
                lt1 = A.f32(8)
                lt2 = A.f32(8)
                lam = pc[:, R_LAM:R_LAM + 8]
                ACT(lt1, lam, AF.Abs, [pck], ["lt1"])
                ACT(lt1, lt1, AF.Exp, ["lt1"], ["lt1"], scale=-1.0)
                ACT(lt1, lt1, AF.Ln, ["lt1", "onec"], ["lt1"], bias=onec[:, 0:1])
                TS("dve", lt2, lam, -1.0, 0.0, ALU.mult, ALU.max, [pck], ["lt2"])
                TT("dve", cl, lt1, lt2, ALU.add, ["lt1", "lt2"], ["cl"])
                TS("dve", cl, cl, -8.0, None, ALU.mult, None, ["cl"], ["cl"])
                zrp = A.f32(L + 4)
                MSET("pool", zrp, 0.0, [("zrp", tt) for tt in range(4)])
                xc = A.f32(L)
                xcb = A.bf16(L)
                rr = A.f32(L)
                ii = A.f32(L)
                aa = A.f32(L)
                a2 = A.f32(L)
                hsum = A.f32(L)
                hh = rr
                gg = aa
                ycst = A.bf16(L)
                for j in range(4):
                    for tt in range(4):
                        ps, pk = PS()
                        for kc in range(8):
                            MM(ps, wlr[:, kc, j * 128:(j + 1) * 128], hT[:, kc, tt * 512:(tt + 1) * 512], kc == 0, kc == 7,
                               ["wlr", ("h", kc, tt)], [pk], kc == 7)
                        CP("act", zrp[:, 2 + tt * 512:2 + (tt + 1) * 512], ps, [pk], [("zrp", tt)])
                    zk = [("zrp", tt) for tt in range(4)]
                    cw = lambda k: pc[:, R_CW + k * 4 + j:R_CW + k * 4 + j + 1]
                    ACT(xc, zrp[:, 0:L], AF.Identity, zk + [pck], ["xc"], bias=pc[:, R_CB + j:R_CB + j + 1], scale=cw(0))
                    for k in range(1, 4):
                        STT("dve" if k != 2 else "pool", xc, zrp[:, k:k + L], cw(k), xc, ALU.mult, ALU.add, zk + ["xc", pck], ["xc"])
                    CP("act", xcb, xc, ["xc"], ["xcb"])
                    if l == DBGL and j == 0:
                        DBG("xc0", xc, ["xc"])
                        DBG("zrp0", zrp[:, 0:L], zk)
                    for d_ in range(2):
                        for (which, dst, dk, boff) in ((0, rr, "rr", R_BA), (1, ii, "ii", R_BX)):
                            for tt in range(4):
                                sl = slice(tt * 512, (tt + 1) * 512)
                                ps, pk = PS()
                                MM(ps, lbd[:, which * 8 + d_ * 4 + j, :], xcb[:, sl], True, True, ["lbd", "xcb"], [pk], True)
                                ACT(dst[:, sl], ps, AF.Sigmoid, [pk, pck], [dk], bias=pc[:, boff + d_ * 4 + j:boff + d_ * 4 + j + 1])
                        if l == DBGL and j == 0 and d_ == 0:
                            DBG("rr0", rr, ["rr"])
                            DBG("ii0", ii, ["ii"])
                            DBG("cl", cl, ["cl"])
                        ACT(aa, rr, AF.Exp, ["rr", "cl"], ["aa"], scale=cl[:, d_ * 4 + j:d_ * 4 + j + 1])
                        if l == DBGL and j == 0 and d_ == 0:
                            DBG("aa0", aa, ["aa"])
                        TT("dve", a2, aa, aa, ALU.mult, ["aa"], ["a2"])
                        ACT(a2, a2, AF.Sqrt, ["a2", "onec"], ["a2"], scale=-1.0, bias=onec[:, 0:1])
                        TT("dve", ii, ii, a2, ALU.mult, ["ii", "a2"], ["ii"])
                        TT("dve", ii, ii, xc, ALU.mult, ["ii", "xc"], ["ii"])
                        if d_ == 0:
                            SCAN(hsum, aa, ii, 0.0, ["aa", "ii"], ["hsum"])
                        else:
                            SCAN(hh[:, ::-1], aa[:, ::-1], ii[:, ::-1], 0.0, ["aa", "ii"], ["rr"])
                            TT("dve", hsum, hsum, hh, ALU.add, ["hsum", "rr"], ["hsum"])
                    for tt in range(4):
                        sl = slice(tt * 512, (tt + 1) * 512)
                        ps, pk = PS()
                        for kc in range(8):
                            MM(ps, wlr[:, kc, 512 + j * 128:512 + (j + 1) * 128], hT[:, kc, sl], kc == 0, kc == 7,
                               ["wlr", ("h", kc, tt)], [pk], kc == 7)
                        ACT(gg[:, sl], ps, AF.Gelu, [pk], ["aa"])
                    TT("dve", ycst, hsum, gg, ALU.mult, ["hsum", "aa"], ["ycst"])
                    DMA("sp", y_d[2, :, j, :], ycst, ["ycst"], [("yD", 2, tt) for tt in range(4)])
                    if l == DBGL and "yc" in dbg_d:
                        DMA("pool", dbg_d["yc"][:, j * L:(j + 1) * L], ycst, ["ycst"], [("dbgo", "yc", j)])
                S.barrier()
                A.pop()

                checkpoint(3)
                A.push()
                yast = [A.bf16(512), A.bf16(512)]
                ydst = [A.bf16(512).rearrange("p (g q) -> p g q", g=4), A.bf16(512).rearrange("p (g q) -> p g q", g=4)]
                wgm = A.bf16(8 * 1024).rearrange("p (k n) -> p k n", k=8)
                DMA("pool", wgm, W["win"][:, 2048:3072].rearrange("(k p) n -> p k n", p=128), [], ["wgm"])
                gng = A.f32(512)
                DMA("sp", gng, W["gng"], [], ["gng"])
                gws = A.bf16(4 * 128).rearrange("p (g q) -> p g q", g=4)
                DMA("pool", gws, W["gws"].rearrange("p (g q) -> p g q", g=4), [], ["gws"])
                gbs = A.bf16(512)
                DMA("pool", gbs[0:1, :], W["gbs"], [], ["gbs"])
                A.push()
                wpo = A.bf16(8 * 512).rearrange("p (k n) -> p k n", k=8)
                DMA("pool", wpo, W["win"][:, 0:512].rearrange("(k p) n -> p k n", p=128), [], ["wpo"])
                pmb = A.bf16(20 * 128).rearrange("p (g v t) -> p g v t", g=4, v=5)
                DMA("pool", pmb, pm_d.rearrange("p (g v t) -> p g v t", g=4, v=5), [], ["pmb"])
                pwb = A.bf16(4 * 128).rearrange("p (g d) -> p g d", g=4)
                DMA("pool", pwb, W["poolw"].rearrange("g c d -> c g d"), [], ["pwb"])
                utok = A.bf16(16 * 512).rearrange("p (n c) -> p n c", n=16)
                for n in range(16):
                    ps, pk = PS()
                    for kc in range(8):
                        MM(ps, hT[:, kc, n * 128:(n + 1) * 128], wpo[:, kc, :], kc == 0, kc == 7, ["wpo", ("h", kc, n // 4)], [pk], kc == 7)
                    CP("act" if n % 2 == 0 else "dve", utok[:, n, :], ps, [pk], [("utok", n)])
                    if l == DBGL and n == 1:
                        DBG("utok1", utok[:, n, :], [("utok", n)])
                pooled = A.bf16(L)
                for g in range(4):
                    for tt in range(4):
                        ps, pk = PS()
                        for n4 in range(4):
                            nt = tt * 4 + n4
                            srcs = [ns for ns in (nt - 1, nt, nt + 1) if 0 <= ns < 16]
                            for i, ns in enumerate(srcs):
                                if ns == nt - 1:
                                    v = 0
                                elif ns == nt + 1:
                                    v = 2
                                else:
                                    v = 3 if nt == 0 else (4 if nt == 15 else 1)
                                MM(ps[:, n4 * 128:(n4 + 1) * 128], utok[:, ns, g * 128:(g + 1) * 128], pmb[:, g, v, :], i == 0, i == len(srcs) - 1,
                                   [("utok", ns), "pmb"], [pk], (n4 == 3 and i == len(srcs) - 1))
                        CP("act", pooled[:, tt * 512:(tt + 1) * 512], ps, [pk], [("pooled", tt)])
                    if l == DBGL and g == 3:
                        DBG("pooled3", pooled, [("pooled", tt) for tt in range(4)])
                    for tt in range(4):
                        sl = slice(tt * 512, (tt + 1) * 512)
                        ps, pk = PS()
                        MM(ps, pwb[:, g, :], pooled[:, sl], True, True, ["pwb", ("pooled", tt)], [pk], True)
                        bb = (g * 4 + tt) % 2
                        ACT(yast[bb], ps, AF.Copy, [pk, pck], [("yast", bb)], scale=pc[:, R_PSC + g:R_PSC + g + 1])
                        DMA("sp", y_d[0, :, g, sl], yast[bb], [("yast", bb)], [("yD", 0, tt)])
                        if l == DBGL and "ya" in dbg_d:
                            DMA("pool", dbg_d["ya"][:, g * L + tt * 512:g * L + (tt + 1) * 512], yast[bb], [("yast", bb)], [("dbgo", "ya", g, tt)])
                S.barrier()
                A.pop()
                A.push()
                guT = A.bf16(4 * L).rearrange("p (c t) -> p c t", c=4)

                def sink_gu(j, tt, ps, pk):
                    ACT(guT[:, j, tt * 512:(tt + 1) * 512], ps, AF.Gelu, [pk], [("gu", j, tt)])
                proj_fm(wgm, "wgm", 4, hT, "h", range(4), sink_gu)
                gv = [A.f32(512), A.f32(512)]
                gsq = A.f32(512)
                ssq = [A.f32(1), A.f32(1)]
                vn = [A.bf16(512), A.bf16(512)]
                def gm_stage1(n):
                    b = n % 2
                    ps, pk = PS()
                    for kc in range(8):
                        MM(ps, hT[:, kc, n * 128:(n + 1) * 128], wgm[:, kc, 512:1024], kc == 0, kc == 7, ["wgm", ("h", kc, n // 4)], [pk], kc == 7)
                    ACT(gv[b], ps, AF.Gelu, [pk], [("gv", b)])
                    ACT(gsq, gv[b], AF.Square, [("gv", b)], ["gsq", ("ssq", b)], accum=ssq[b][:, 0:1])
                    ACT(ssq[b], ssq[b], AF.Sqrt, [("ssq", b), "epsc"], [("ssq", b)], bias=epsc[:, 0:1], scale=1.0 / 512)
                    RECIP(ssq[b], ssq[b], [("ssq", b)], [("ssq", b)])
                    STT("dve", vn[b], gv[b], ssq[b][:, 0:1], gng, ALU.mult, ALU.mult, [("gv", b), ("ssq", b), "gng"], [("vn", b)])
                    if l == DBGL and n == 1:
                        DBG("gv1", gv[b], [("gv", b)])
                        DBG("vn1", vn[b], [("vn", b)])
                        DBG("ssq1", ssq[b], [("ssq", b)])

                def gm_stage2(n):
                    b = n % 2
                    ps, pk = PS()
                    for g in range(4):
                        MM(ps[:, g * 128:(g + 1) * 128], vn[b][:, g * 128:(g + 1) * 128], gws[:, g, :], True, False, [("vn", b), "gws"], [pk], False)
                        MM(ps[:, g * 128:(g + 1) * 128], onesb[0:1, :], gbs[0:1, g * 128:(g + 1) * 128], False, True, ["onesb", "gbs"], [pk], g == 3)
                    TT("dve", ydst[b], ps.rearrange("p (g q) -> p g q", g=4), guT[:, :, n * 128:(n + 1) * 128], ALU.mult,
                       [pk] + [("gu", j, n // 4) for j in range(4)], [("ydst", b)])
                    DMA("sp", y_d[3, :, :, n * 128:(n + 1) * 128], ydst[b], [("ydst", b)], [("yD", 3, n // 4)])
                    if l == DBGL and "yd" in dbg_d:
                        DMA("pool", dbg_d["yd"].rearrange("p (g t) -> p g t", g=4)[:, :, n * 128:(n + 1) * 128], ydst[b], [("ydst", b)], [("dbgo", "yd", n)])

                gm_stage1(0)
                for n in range(16):
                    if n + 1 < 16:
                        gm_stage1(n + 1)
                    gm_stage2(n)
                A.pop()
                S.barrier()
                checkpoint(35)
                yh = [A.bf16(4 * 1024).rearrange("p (c t) -> p c t", c=4) for _ in range(4)]
                mrg = wgm
                wgb = [A.bf16(8 * 128).rearrange("p (k n) -> p k n", k=8) for _ in range(3)]
                wbb = [A.bf16(4 * 128).rearrange("p (k n) -> p k n", k=4) for _ in range(3)]
                wob = [A.bf16(8 * 128).rearrange("p (k n) -> p k n", k=8) for _ in range(2)]
                sgt = [gng, A.f32(512)]
                macc = [A.f32(512), A.f32(512)]
                mtmp = [A.f32(512), A.f32(512)]
                wi = 0
                yi = 0
                for hf in range(2):
                    for k in range(4):
                        DMA("sp", yh[k], y_d[k, :, :, hf * 1024:(hf + 1) * 1024], [("yD", k, hf * 2), ("yD", k, hf * 2 + 1)], [("yh", k)])
                    for dch in range(8):
                        for k in range(4):
                            b3 = wi % 3
                            wi += 1
                            DMA("pool", wgb[b3], W["wgate"][:, k * 1024 + dch * 128:k * 1024 + (dch + 1) * 128].rearrange("(kc p) n -> p kc n", p=128),
                                [], [("wgb", b3)])
                            DMA("pool", wbb[b3], W["wbr"][k, :, dch * 128:(dch + 1) * 128].rearrange("(kc p) n -> p kc n", p=128), [], [("wbb", b3)])
                            for t2 in range(2):
                                tt = hf * 2 + t2
                                sl = slice(tt * 512, (tt + 1) * 512)
                                ps, pk = PS()
                                for kc in range(8):
                                    MM(ps, wgb[b3][:, kc, :], hT[:, kc, sl], kc == 0, kc == 7, [("wgb", b3), ("h", kc, tt)], [pk], kc == 7)
                                ACT(sgt[t2], ps, AF.Sigmoid, [pk, pck], [("sgt", t2)], bias=pc[:, R_BG + k * 8 + dch:R_BG + k * 8 + dch + 1])
                                ps2, pk2 = PS()
                                for kc in range(4):
                                    MM(ps2, wbb[b3][:, kc, :], yh[k][:, kc, t2 * 512:(t2 + 1) * 512], kc == 0, kc == 3, [("wbb", b3), ("yh", k)], [pk2], kc == 3)
                                if k == 0:
                                    TT("dve", macc[t2], sgt[t2], ps2, ALU.mult, [("sgt", t2), pk2], [("macc", t2)])
                                elif k < 3:
                                    TT("dve", mtmp[t2], sgt[t2], ps2, ALU.mult, [("sgt", t2), pk2], [("mtmp", t2)])
                                    TT("dve", macc[t2], macc[t2], mtmp[t2], ALU.add, [("macc", t2), ("mtmp", t2)], [("macc", t2)])
                                else:
                                    TT("dve", mtmp[t2], sgt[t2], ps2, ALU.mult, [("sgt", t2), pk2], [("mtmp", t2)])
                                    TT("dve", mrg[:, dch, t2 * 512:(t2 + 1) * 512], macc[t2], mtmp[t2], ALU.add, [("macc", t2), ("mtmp", t2)],
                                       [("mrg", dch, t2)])
                    for do in range(8):
                        b2 = do % 2
                        DMA("pool", wob[b2], W["wout"][:, do * 128:(do + 1) * 128].rearrange("(kc p) n -> p kc n", p=128), [], [("wob", b2)])
                        for t2 in range(2):
                            tt = hf * 2 + t2
                            sl = slice(tt * 512, (tt + 1) * 512)
                            ps, pk = PS()
                            for kc in range(8):
                                MM(ps, wob[b2][:, kc, :], mrg[:, kc, t2 * 512:(t2 + 1) * 512], kc == 0, kc == 7, [("wob", b2), ("mrg", kc, t2)], [pk], kc == 7)
                            STT("dve", xT[:, do, sl], ps, modt[l][:, 16 + do:17 + do], xT[:, do, sl], ALU.mult, ALU.add,
                                [pk, ("mod", l), ("x", do, tt)], [("x", do, tt)])
                S.barrier()
                A.pop()
                A.pop()

                if l == DBGL:
                    DBG("x1", xT, [("x", c, tt) for c in range(8) for tt in range(4)])
                checkpoint(4)
                A.push()
                h2 = A.bf16(8 * 1024).rearrange("p (c t) -> p c t", c=8)
                w1b = [A.bf16(8 * 512).rearrange("p (k n) -> p k n", k=8) for _ in range(2)]
                w2b = [A.bf16(4 * 1024).rearrange("p (k n) -> p k n", k=4) for _ in range(2)]
                f1 = [A.bf16(4 * 1024).rearrange("p (c t) -> p c t", c=4) for _ in range(2)]
                fsq = [A.f32(512), A.f32(512)]
                fi = 0
                for hf in range(2):
                    rms_mod(l, 1, [hf * 2, hf * 2 + 1], h2, "h2")
                    for fg in range(8):
                        b = fi % 2
                        fi += 1
                        DMA("pool", w1b[b], W["wff1"][:, fg * 512:(fg + 1) * 512].rearrange("(kc p) n -> p kc n", p=128), [], [("w1b", b)])
                        DMA("pool", w2b[b], W["wff2"][fg * 512:(fg + 1) * 512, :].rearrange("(kc p) n -> p kc n", p=128), [], [("w2b", b)])
                        for fc in range(4):
                            for t2 in range(2):
                                tt = hf * 2 + t2
                                ps, pk = PS()
                                for kc in range(8):
                                    MM(ps, w1b[b][:, kc, fc * 128:(fc + 1) * 128], h2[:, kc, t2 * 512:(t2 + 1) * 512], kc == 0, kc == 7,
                                       [("w1b", b), ("h2", kc, tt)], [pk], kc == 7)
                                ACT(fsq[t2], ps, AF.Square, [pk], [("fsq", t2)])
                                STT("dve", f1[b][:, fc, t2 * 512:(t2 + 1) * 512], ps, 0.0, fsq[t2], ALU.is_gt, ALU.mult, [pk, ("fsq", t2)],
                                    [("f1", b, fc, t2)])
                        for do in range(8):
                            for t2 in range(2):
                                tt = hf * 2 + t2
                                sl = slice(tt * 512, (tt + 1) * 512)
                                ps, pk = PS()
                                for fc in range(4):
                                    MM(ps, w2b[b][:, fc, do * 128:(do + 1) * 128], f1[b][:, fc, t2 * 512:(t2 + 1) * 512], fc == 0, fc == 3,
                                       [("w2b", b), ("f1", b, fc, t2)], [pk], fc == 3)
                                STT("dve", xT[:, do, sl], ps, modt[l][:, 40 + do:41 + do], xT[:, do, sl], ALU.mult, ALU.add,
                                    [pk, ("mod", l), ("x", do, tt)], [("x", do, tt)])
                if l == DBGL:
                    DBG("x2", xT, [("x", c, tt) for c in range(8) for tt in range(4)])
                S.barrier()
                A.pop()
                checkpoint(5 + l)

        except _Stop:
            A.top = persist_top
            A.marks = []
            bank_state["free"] = list(range(8))
            S.barrier()
        A.push()
        sq = A.bf16(8 * 512).rearrange("p (c t) -> p c t", c=8)
        rt = A.f32(512)
        yn = A.f32(8 * 512).rearrange("p (c t) -> p c t", c=8)
        otok = [A.f32(D), A.f32(D)]
        pl = pcol[DEPTH - 1]
        outs = []
        for tt in range(4):
            sl = slice(tt * 512, (tt + 1) * 512)
            ACT(sq, xT[:, :, sl], AF.Square, [("x", c, tt) for c in range(8)], ["sq"])
            ps, pk = PS()
            for c in range(8):
                MM(ps, onesb, sq[:, c, :], c == 0, c == 7, ["sq", "onesb"], [pk], c == 7)
            ACT(rt, ps, AF.Sqrt, [pk, "epsc"], ["rt"], bias=epsc[:, 0:1], scale=1.0 / D)
            RECIP(rt, rt, ["rt"], ["rt"])
            for c in range(8):
                STT("dve" if c % 2 == 0 else "pool", yn[:, c, :], xT[:, c, sl], pl[:, R_FG + c:R_FG + c + 1], rt, ALU.mult, ALU.mult,
                    [("x", c, tt), "rt", ("pcol", DEPTH - 1)], [("yn", c)])
            for n4 in range(4):
                n = tt * 4 + n4
                ot = otok[n % 2]
                for cg in range(2):
                    ps, pk = PS()
                    for c4 in range(4):
                        c = cg * 4 + c4
                        TR(ps[:, c4 * 128:(c4 + 1) * 128], yn[:, c, n4 * 128:(n4 + 1) * 128], ident, [("yn", c), "ident"], [pk])
                    CP("act" if cg == 0 else "dve", ot[:, cg * 512:(cg + 1) * 512], ps, [pk], [("otok", n % 2, cg)])
                outs.append(DMA("sp", out_d[n * 128:(n + 1) * 128, :], ot, [("otok", n % 2, 0), ("otok", n % 2, 1)], [("out", n)]))
        A.pop()
        S.barrier()
        S.emit(block)
        print("instructions:", S.ninstr, "arena peak words:", A.peak)
    return nc


def _pool_mats():
    wins = (2, 4, 8, 16)
    Lm = 3 * 128
    out = np.zeros((128, 4, 5, 128), np.float32)
    for g, w in enumerate(wins):
        def blockfor(nt_kind):
            if nt_kind == "mid":
                Lv, t0 = 5 * 128, 2 * 128
            elif nt_kind == "first":
                Lv, t0 = 3 * 128, 0
            else:
                Lv, t0 = 3 * 128, 2 * 128
            Pm = np.zeros((Lv, Lv), np.float64)
            for t in range(Lv):
                lo = min(max(t - w // 2, 0), Lv)
                hi = min(max(t - w // 2 + w, 0), Lv)
                Pm[lo:hi, t] = 1.0 / (hi - lo)
                Pm[t, t] -= 1.0
            return Pm, t0
        Pm, t0 = blockfor("mid")
        out[:, g, 0, :] = Pm[t0 - 128:t0, t0:t0 + 128]
        out[:, g, 1, :] = Pm[t0:t0 + 128, t0:t0 + 128]
        out[:, g, 2, :] = Pm[t0 + 128:t0 + 256, t0:t0 + 128]
        Pm, t0 = blockfor("first")
        out[:, g, 3, :] = Pm[t0:t0 + 128, t0:t0 + 128]
        Pm, t0 = blockfor("last")
        out[:, g, 4, :] = Pm[t0:t0 + 128, t0:t0 + 128]
    return np.ascontiguousarray(out.reshape(128, 20 * 128))


def _prep_shared(inp):
    f = lambda a: np.ascontiguousarray(np.asarray(a, dtype=np.float32))
    sh = {"pm": _pool_mats()}
    for l in range(DEPTH):
        rows = [
            f(inp["ada_b"][l]).reshape(48, 128),
            f(inp["norm1_g"][l]).reshape(8, 128),
            f(inp["pool_scale"][l]).reshape(4, 128),
            f(inp["ssm_d"][l]).reshape(4, 128),
            f(inp["ssm_glu_b"][l]).reshape(4, 128),
            f(inp["lru_conv_w"][l]).reshape(16, 128),
            f(inp["lru_conv_b"][l]).reshape(4, 128),
            f(inp["lru_ba"][l]).reshape(8, 128),
            f(inp["lru_bx"][l]).reshape(8, 128),
            f(inp["lru_lam"][l]).reshape(8, 128),
            f(inp["b_gate"][l]).reshape(32, 128),
            f(inp["norm2_g"][l]).reshape(8, 128),
            f(inp["final_g"]).reshape(8, 128),
        ]
        sh[f"pv{l}"] = np.ascontiguousarray(np.concatenate(rows, axis=0))
        sh[f"adaw{l}"] = f(inp["ada_w"][l])
        sh[f"win{l}"] = f(inp["w_in"][l])
        sh[f"poolw{l}"] = f(inp["pool_w"][l])
        def sc(a):
            return a.reshape(2, 16, 2, 64).transpose(2, 3, 0, 1).reshape(128, 32)
        lre = f(inp["ssm_lam_re"][l])
        lim = f(inp["ssm_lam_im"][l])
        ldt = np.broadcast_to(f(inp["ssm_log_dt"][l])[:, :, None], (2, 32, 64))
        sh[f"ssc{l}"] = np.ascontiguousarray(np.concatenate([sc(lre), sc(lim), sc(ldt)], axis=1))
        def sb(a):
            return a.reshape(16, 2, 64, 16).transpose(1, 2, 0, 3).reshape(128, 16, 16)
        sh[f"ssb{l}"] = np.ascontiguousarray(np.stack([sb(f(inp["ssm_b_re"][l])), sb(f(inp["ssm_b_im"][l]))], axis=1).reshape(128, 512))
        def scm(a):
            return a.reshape(2, 16, 2, 16, 64).transpose(2, 4, 0, 1, 3).reshape(128, 2, 16, 16)
        sh[f"sscc{l}"] = np.ascontiguousarray(np.stack([scm(f(inp["ssm_c_re"][l])), scm(f(inp["ssm_c_im"][l]))], axis=1).reshape(128, 1024))
        sh[f"gluw{l}"] = f(inp["ssm_glu_w"][l])
        bd = np.zeros((128, 16, 128), np.float32)
        for which, nm in enumerate(("lru_wa", "lru_wx")):
            w = f(inp[nm][l])
            for d_ in range(2):
                for j in range(4):
                    i = which * 8 + d_ * 4 + j
                    bd[0:64, i, 0:64] = w[d_, 2 * j]
                    bd[64:128, i, 64:128] = w[d_, 2 * j + 1]
        sh[f"lrubd{l}"] = np.ascontiguousarray(bd.reshape(128, 16 * 128))
        sh[f"gng{l}"] = np.ascontiguousarray(np.broadcast_to(f(inp["gmlp_norm_g"][l])[None, :], (128, 512)))
        sh[f"gws{l}"] = np.ascontiguousarray(f(inp["gmlp_ws"][l]).transpose(2, 0, 1).reshape(128, 512))
        sh[f"gbs{l}"] = f(inp["gmlp_bs"][l]).reshape(1, 512)
        sh[f"wbr{l}"] = f(inp["w_branch"][l])
        sh[f"wgate{l}"] = f(inp["w_gate"][l])
        sh[f"wout{l}"] = f(inp["w_out"][l])
        sh[f"wff1{l}"] = f(inp["w_ff1"][l])
        sh[f"wff2{l}"] = f(inp["w_ff2"][l])
    return sh


_CACHE = {}


def kernel(**inputs):
    x = np.ascontiguousarray(np.asarray(inputs["x"], dtype=np.float32))
    c = np.ascontiguousarray(np.asarray(inputs["c"], dtype=np.float32))
    B = x.shape[0]
    sh = _prep_shared(inputs)
    if "nc" not in _CACHE:
        _CACHE["nc"] = build_program()
    nc = _CACHE["nc"]
    in_maps = []
    for b in range(B):
        m = dict(sh)
        m["x"] = x[b]
        m["cb"] = np.ascontiguousarray(c[b].reshape(8, 128))
        in_maps.append(m)
    res = run_bass_kernel_spmd(nc, in_maps, core_ids=list(range(B)))
    return np.stack([np.asarray(r["out"], dtype=np.float32) for r in res.results], axis=0)
```

```python
import numpy as np
from contextlib import ExitStack
import concourse.bass as bass
import concourse.mybir as mybir
from concourse.bass_utils import run_bass_kernel_spmd

F32 = mybir.dt.float32
BF16 = mybir.dt.bfloat16
I32 = mybir.dt.int32
AF = mybir.ActivationFunctionType
ALU = mybir.AluOpType

NDMASEM = 8
L = 2048
D = 1024
DEPTH = 2
EPS = 1e-6
import os as _os
DBGL = int(_os.environ.get("DBGL", "0"))
MAGIC = 12582912.0
TWO_PI = float(2 * np.pi)
PVROWS = 160
R_ADAB, R_N1G, R_PSC, R_SSD, R_GLUB, R_CW, R_CB, R_BA, R_BX, R_LAM, R_BG, R_N2G, R_FG = 0, 48, 56, 60, 64, 68, 84, 88, 96, 104, 112, 144, 152


class Sched:
    ENG = ("pe", "act", "dve", "pool", "sp")

    def __init__(self, nc, stack):
        self.nc = nc
        self.prog = {e: [] for e in self.ENG}
        self.count = {e: 0 for e in self.ENG}
        self.sem = {}
        for e in ("pe", "act", "dve", "pool"):
            self.sem[e] = stack.enter_context(nc.semaphore("s_" + e))
        self.dq = {}
        for q in ("sp", "pool"):
            for i in range(NDMASEM):
                self.sem[(q, i)] = stack.enter_context(nc.semaphore(f"d_{q}{i}"))
            self.dq[q] = 0
        self.seen = {e: {} for e in self.ENG}
        self.lastw = {}
        self.readers = {}
        self.ninstr = 0

    def _need(self, eng, tickets):
        waits = {}
        for t in tickets:
            if t is None:
                continue
            k, v = t
            if self.seen[eng].get(k, 0) >= v:
                continue
            if waits.get(k, 0) < v:
                waits[k] = v
        for k, v in waits.items():
            self.seen[eng][k] = v
        return list(waits.items())

    def _deps(self, reads, writes):
        ts = []
        for r in reads:
            ts.append(self.lastw.get(r))
        for w in writes:
            ts.append(self.lastw.get(w))
            ts.extend(self.readers.get(w, ()))
        return ts

    def _commit(self, ticket, reads, writes):
        for r in reads:
            self.readers.setdefault(r, []).append(ticket)
        for w in writes:
            self.lastw[w] = ticket
            self.readers[w] = []

    def op(self, eng, fn, reads=(), writes=(), inc=True):
        deps = self._deps(reads, writes)
        if eng == "pe":
            deps = [t for t in deps if t is not None and t[0] != "pe"]
        waits = self._need(eng, deps)
        if inc:
            self.count[eng] += 1
            ticket = (eng, self.count[eng])
        else:
            ticket = (eng, self.count[eng] + 1)
        self.prog[eng].append((fn, waits, self.sem[eng] if inc else None, 1))
        self._commit(ticket, reads, writes)
        self.ninstr += 1
        return ticket

    def dma(self, q, fn, reads=(), writes=()):
        i = self.dq[q]
        self.dq[q] += 1
        slot, rnd = i % NDMASEM, i // NDMASEM
        key = (q, slot)
        deps = self._deps(reads, writes)
        if rnd > 0:
            deps.append((key, 16 * rnd))
        waits = self._need(q, deps)
        ticket = (key, 16 * (rnd + 1))
        self.prog[q].append((fn, waits, self.sem[key], 16))
        self._commit(ticket, reads, writes)
        self.ninstr += 1
        return ticket

    def all_tickets(self):
        ts = [(e, self.count[e]) for e in ("pe", "act", "dve", "pool") if self.count[e] > 0]
        for q in ("sp", "pool"):
            n = self.dq[q]
            for slot in range(NDMASEM):
                cnt = (n - slot + NDMASEM - 1) // NDMASEM if n > slot else 0
                if cnt > 0:
                    ts.append(((q, slot), 16 * cnt))
        return ts

    def barrier(self):
        ts = self.all_tickets()
        for e in self.ENG:
            waits = self._need(e, ts)
            if waits:
                self.prog[e].append((None, waits, None, 0))
        self.lastw = {}
        self.readers = {}

    def emit(self, block):
        def run(name):
            def body(e):
                for fn, waits, sem, n in self.prog[name]:
                    for k, v in waits:
                        e.wait_ge(self.sem[k], v)
                    if fn is None:
                        continue
                    ins = fn(e)
                    if sem is not None:
                        ins.then_inc(sem, n)
            return body

        block.tensor(run("pe"))
        block.scalar(run("act"))
        block.vector(run("dve"))
        block.gpsimd(run("pool"))
        block.sync(run("sp"))


class Arena:
    def __init__(self, ap, words):
        self.ap = ap
        self.words = words
        self.top = 0
        self.marks = []
        self.peak = 0

    def _take(self, words):
        a = self.ap[:, self.top:self.top + words]
        self.top += words
        self.peak = max(self.peak, self.top)
        assert self.top <= self.words, f"arena overflow {self.top} > {self.words}"
        return a

    def f32(self, cols):
        return self._take(cols)

    def i32(self, cols):
        return self._take(cols).bitcast(I32)

    def bf16(self, cols):
        w = (cols + 1) // 2
        return self._take(w).bitcast(BF16)[:, :cols]

    def push(self):
        self.marks.append(self.top)

    def pop(self):
        self.top = self.marks.pop()


class _Stop(Exception):
    pass


def build_program(dbg=None, stage=99):
    nc = bass.Bass("TRN2", target_bir_lowering=False)
    dram_in = lambda name, shape: nc.dram_tensor(name, list(shape), F32, kind="ExternalInput").ap()
    x_d = dram_in("x", [L, D])
    cb_d = dram_in("cb", [8, 128])
    pm_d = dram_in("pm", [128, 20 * 128])
    Wd = []
    for l in range(DEPTH):
        Wd.append(dict(
            pv=dram_in(f"pv{l}", [PVROWS, 128]),
            adaw=dram_in(f"adaw{l}", [D, 6 * D]),
            win=dram_in(f"win{l}", [D, 3072]),
            poolw=dram_in(f"poolw{l}", [4, 128, 128]),
            ssc=dram_in(f"ssc{l}", [128, 96]),
            ssb=dram_in(f"ssb{l}", [128, 2 * 16 * 16]),
            sscc=dram_in(f"sscc{l}", [128, 2 * 2 * 16 * 16]),
            gluw=dram_in(f"gluw{l}", [512, 512]),
            lrubd=dram_in(f"lrubd{l}", [128, 16 * 128]),
            gng=dram_in(f"gng{l}", [128, 512]),
            gws=dram_in(f"gws{l}", [128, 4 * 128]),
            gbs=dram_in(f"gbs{l}", [1, 512]),
            wbr=dram_in(f"wbr{l}", [4, 512, D]),
            wgate=dram_in(f"wgate{l}", [D, 4 * D]),
            wout=dram_in(f"wout{l}", [D, D]),
            wff1=dram_in(f"wff1{l}", [D, 4 * D]),
            wff2=dram_in(f"wff2{l}", [4 * D, D]),
        ))
    out_d = nc.dram_tensor("out", [L, D], F32, kind="ExternalOutput").ap()
    y_d = nc.dram_tensor("yscr", [4, 128, 4, L], BF16, kind="Internal").ap()
    dbg_d = {}
    if dbg:
        for name, shape in dbg.items():
            dbg_d[name] = nc.dram_tensor("dbg_" + name, list(shape), F32, kind="ExternalOutput").ap()

    with ExitStack() as st:
        S = Sched(nc, st)
        AW = 48600
        arena_t = st.enter_context(nc.sbuf_tensor("arena", [128, AW], F32))
        A = Arena(arena_t[:], AW)
        banks = [st.enter_context(nc.psum_tensor(f"ps{i}", [128, 512], F32)) for i in range(8)]
        block = st.enter_context(nc.Block())
        bank_state = {"i": 0, "free": list(range(8))}

        def PS():
            fr = bank_state["free"]
            b = fr[bank_state["i"] % len(fr)]
            bank_state["i"] += 1
            return banks[b][:], ("ps", b)

        def MM(out, lhsT, rhs, start, stop, R, W, inc):
            return S.op("pe", lambda e: e.matmul(out, lhsT=lhsT, rhs=rhs, start=start, stop=stop), R, W, inc)

        def TR(out, in_, ident, R, W):
            return S.op("pe", lambda e: e.transpose(out=out, in_=in_, identity=ident), R, W, True)

        def ACT(out, in_, func, R, W, bias=None, scale=None, accum=None):
            kw = {}
            if bias is not None:
                kw["bias"] = bias
            if scale is not None:
                kw["scale"] = scale
            if accum is not None:
                kw["accum_out"] = accum
            return S.op("act", lambda e: e.activation(out=out, in_=in_, func=func, **kw), R, W)

        def TT(eng, out, a, b, op, R, W):
            return S.op(eng, lambda e: e.tensor_tensor(out=out, in0=a, in1=b, op=op), R, W)

        def TS(eng, out, a, s1, s2, op0, op1, R, W):
            if s2 is None:
                return S.op(eng, lambda e: e.tensor_scalar(out=out, in0=a, scalar1=s1, scalar2=None, op0=op0), R, W)
            return S.op(eng, lambda e: e.tensor_scalar(out=out, in0=a, scalar1=s1, scalar2=s2, op0=op0, op1=op1), R, W)

        def STT(eng, out, a, s, b, op0, op1, R, W):
            eng = "dve"
            return S.op(eng, lambda e: e.scalar_tensor_tensor(out=out, in0=a, scalar=s, in1=b, op0=op0, op1=op1), R, W)

        def CP(eng, out, in_, R, W):
            if eng == "act":
                return S.op("act", lambda e: e.activation(out=out, in_=in_, func=AF.Copy), R, W)
            return S.op(eng, lambda e: e.tensor_copy(out=out, in_=in_), R, W)

        def MSET(eng, out, val, W):
            return S.op(eng, lambda e: e.memset(out, val), (), W)

        def SCAN(out, d0, d1, init, R, W):
            return S.op("dve", lambda e: e.tensor_tensor_scan(out=out, data0=d0, data1=d1, initial=init, op0=ALU.mult, op1=ALU.add), R, W)

        def RECIP(out, in_, R, W):
            return S.op("dve", lambda e: e.reciprocal(out=out, in_=in_), R, W)

        def DMA(q, out, in_, R, W):
            return S.dma(q, lambda e: e.dma_start(out=out, in_=in_), R, W)

        def DBG(name, src_ap, R):
            if name in dbg_d:
                DMA("pool", dbg_d[name].rearrange("p (c t) -> p c t", c=src_ap.shape[1]) if len(src_ap.shape) == 3 else dbg_d[name], src_ap, R, [("dbgout", name)])

        xT = A.f32(8 * L).rearrange("p (c t) -> p c t", c=8)
        ident = A.f32(128)
        onesb = A.bf16(128)
        pcol = [A.f32(PVROWS) for _ in range(DEPTH)]
        modt = [A.f32(48) for _ in range(DEPTH)]
        modA = [A.f32(16) for _ in range(DEPTH)]
        cond = A.f32(8)
        epsc = A.f32(1)
        halfpi = A.f32(1)
        onec = A.f32(1)
        magicc = A.f32(1)

        MSET("pool", ident, 1.0, ["ident"])
        S.op("pool", lambda e: e.affine_select(out=ident, in_=ident, pattern=[[-1, 128]], compare_op=ALU.is_equal,
                                                fill=0.0, base=0, channel_multiplier=1), ["ident"], ["ident"])
        MSET("pool", onesb, 1.0, ["onesb"])
        MSET("pool", epsc, EPS, ["epsc"])
        MSET("pool", halfpi, float(np.pi / 2), ["halfpi"])
        MSET("pool", onec, 1.0, ["onec"])
        MSET("pool", magicc, MAGIC, ["magicc"])

        A.push()
        for l in range(DEPTH):
            pvt = A.f32(2 * 128).rearrange("p (a b) -> p a b", a=2)
            DMA("sp", pvt[:, 0, :], Wd[l]["pv"][0:128, :], [], [("pvt", l)])
            DMA("sp", pvt[0:32, 1, :], Wd[l]["pv"][128:160, :], [], [("pvt", l)])
            ps, pk = PS()
            TR(ps[:, 0:128], pvt[:, 0, :], ident, [("pvt", l), "ident"], [pk])
            TR(ps[:, 128:160], pvt[0:32, 1, :], ident[0:32, 0:32], [("pvt", l), "ident"], [pk])
            CP("dve", pcol[l], ps[:, 0:PVROWS], [pk], [("pcol", l)])
        cbt = A.f32(128)
        DMA("sp", cbt[0:8, :], cb_d, [], ["cbt"])
        ps, pk = PS()
        TR(ps[:, 0:8], cbt[0:8, :], ident[0:8, 0:8], ["cbt", "ident"], [pk])
        ACT(cond, ps[:, 0:8], AF.Silu, [pk], ["cond"])
        def mod_piece_dma(l, kc, part, nparts, buf, bkey):
            Wc = 6144 // nparts
            c0 = part * Wc
            DMA("sp", buf[:, 0:Wc], Wd[l]["adaw"][kc * 128:(kc + 1) * 128, c0:c0 + Wc], [], [bkey])

        def mod_piece(l, kc, part, nparts, buf, bkey, dma=True, defer_add=False):
            Wc = 6144 // nparts
            nj = Wc // 128
            j0 = part * nj
            if dma:
                mod_piece_dma(l, kc, part, nparts, buf, bkey)
            ps, pk = PS()
            for j in range(nj):
                MM(ps[:, j:j + 1], buf[:, j * 128:(j + 1) * 128], cond[:, kc:kc + 1], True, True, [bkey, "cond"], [pk], j == nj - 1)
            dst = modt[l][:, j0:j0 + nj]
            mk = ("modp", l, part)

            def do_add():
                if kc == 0:
                    TT("dve", dst, ps[:, 0:nj], pcol[l][:, R_ADAB + j0:R_ADAB + j0 + nj], ALU.add, [pk, ("pcol", l)], [mk])
                else:
                    TT("dve", dst, ps[:, 0:nj], dst, ALU.add, [pk, mk], [mk])
            if defer_add:
                return do_add
            do_add()

        def mod_finish(l, nparts):
            mks = [("modp", l, p) for p in range(nparts)]
            STT("dve", modA[l][:, 0:8], modt[l][:, 8:16], 1.0, pcol[l][:, R_N1G:R_N1G + 8], ALU.add, ALU.mult,
                mks + [("pcol", l)], [("modA", l), ("mod", l)])
            STT("dve", modA[l][:, 8:16], modt[l][:, 32:40], 1.0, pcol[l][:, R_N2G:R_N2G + 8], ALU.add, ALU.mult,
                mks + [("pcol", l)], [("modA", l), ("mod", l)])

        awb = [A.f32(3072), A.f32(3072)]
        for kc in range(8):
            for part in range(2):
                bi = (kc * 2 + part) % 2
                mod_piece(0, kc, part, 2, awb[bi], ("aw", bi))
        mod_finish(0, 2)
        xtok = [A.f32(D), A.f32(D)]
        for n in range(16):
            xt = xtok[n % 2]
            DMA("sp", xt, x_d[n * 128:(n + 1) * 128, :], [], [("xtok", n % 2)])
            for cg in range(2):
                ps, pk = PS()
                for c4 in range(4):
                    c = cg * 4 + c4
                    TR(ps[:, c4 * 128:(c4 + 1) * 128], xt[:, c * 128:(c + 1) * 128], ident, [("xtok", n % 2), "ident"], [pk])
                eng = "dve" if cg == 0 else "act"
                CP(eng, xT[:, cg * 4:(cg + 1) * 4, n * 128:(n + 1) * 128], ps.rearrange("p (c t) -> p c t", c=4), [pk],
                   [("x", c, n // 4) for c in range(cg * 4, cg * 4 + 4)])
        S.barrier()
        A.pop()

        def rms_mod(l, which, tts, hT, hkey):
            A.push()
            sq = [A.bf16(8 * 512).rearrange("p (c t) -> p c t", c=8) for _ in range(2)]
            rt = [A.f32(512) for _ in range(2)]
            tmps = [A.f32(512) for _ in range(3)] + [tmp2]
            shc = 0 if which == 0 else 24
            tts = list(tts)
            pend = {}

            def stats_a(i):
                tt = tts[i]
                b = i % 2
                sl = slice(tt * 512, (tt + 1) * 512)
                ACT(sq[b], xT[:, :, sl], AF.Square, [("x", c, tt) for c in range(8)], [("sq", b)])
                ps, pk = PS()
                for c in range(8):
                    MM(ps, onesb, sq[b][:, c, :], c == 0, c == 7, [("sq", b), "onesb"], [pk], c == 7)
                pend[i] = (ps, pk)

            def stats_b(i):
                b = i % 2
                ps, pk = pend.pop(i)
                ACT(rt[b], ps, AF.Sqrt, [pk, "epsc"], [("rt", b)], bias=epsc[:, 0:1], scale=1.0 / D)
                RECIP(rt[b], rt[b], [("rt", b)], [("rt", b)])

            def apply(i):
                tt = tts[i]
                b = i % 2
                sl = slice(tt * 512, (tt + 1) * 512)
                for c in range(8):
                    tm = tmps[c % 4]
                    tk = ("tmp", c % 4)
                    STT("dve", tm, xT[:, c, sl], modA[l][:, which * 8 + c:which * 8 + c + 1], rt[b], ALU.mult, ALU.mult,
                        [("x", c, tt), ("rt", b), ("modA", l)], [tk])
                    dst = hT[:, c, sl] if hT.shape[2] == L else hT[:, c, (tt % 2) * 512:(tt % 2 + 1) * 512]
                    shcol = modt[l][:, shc + c:shc + c + 1]
                    if c % 2 == 0:
                        ACT(dst, tm, AF.Identity, [tk, ("mod", l)], [(hkey, c, tt)], bias=shcol)
                    else:
                        TS("dve", dst, tm, shcol, None, ALU.add, None, [tk, ("mod", l)], [(hkey, c, tt)])

            stats_a(0)
            stats_b(0)
            for i in range(len(tts)):
                if i + 1 < len(tts):
                    stats_a(i + 1)
                apply(i)
                if i + 1 < len(tts):
                    stats_b(i + 1)
            A.pop()

        tmp2 = A.f32(512)

        def load_w(q, dst, src, key):
            return DMA(q, dst, src, [], [key])

        def proj_fm(wt, wkey, ncol_chunks, hT, hkey, tts, sink):
            for j in range(ncol_chunks):
                for tt in tts:
                    ps, pk = PS()
                    for kc in range(8):
                        MM(ps, wt[:, kc, j * 128:(j + 1) * 128], hT[:, kc, tt * 512:(tt + 1) * 512], kc == 0, kc == 7,
                           [wkey, (hkey, kc, tt)], [pk], kc == 7)
                    sink(j, tt, ps, pk)

        persist_top = A.top

        def checkpoint(k):
            if stage == k:
                raise _Stop()
        try:
          checkpoint(0)
          for l in range(DEPTH):
                W = Wd[l]
                pc = pcol[l]
                pck = ("pcol", l)
                A.push()
                A.push()
                zsT = A.bf16(4 * L).rearrange("p (c t) -> p c t", c=4)
                A.push()
                hT = A.bf16(8 * L).rearrange("p (c t) -> p c t", c=8)
                wss = A.bf16(8 * 512).rearrange("p (k n) -> p k n", k=8)
                DMA("pool", wss, W["win"][:, 512:1024].rearrange("(k p) n -> p k n", p=128), [], ["wss"])
                rms_mod(l, 0, range(4), hT, "h")
                if l == DBGL:
                    DBG("h", hT, [("h", c, tt) for c in range(8) for tt in range(4)])

                def sink_zs(j, tt, ps, pk):
                    CP("act" if (j + tt) % 2 == 0 else "dve", zsT[:, j, tt * 512:(tt + 1) * 512], ps, [pk], [("zs", j, tt)])
                proj_fm(wss, "wss", 4, hT, "h", range(4), sink_zs)
                if l == DBGL:
                    DBG("zs", zsT, [("zs", c, tt) for c in range(4) for tt in range(4)])
                S.barrier()
                A.pop()
                checkpoint(1)
                ssc = A.f32(96)
                DMA("sp", ssc, W["ssc"], [], ["ssc"])
                ssb = A.f32(512).rearrange("p (r q h) -> p r q h", r=2, q=16)
                DMA("sp", ssb, W["ssb"].rearrange("p (r q h) -> p r q h", r=2, q=16), [], ["ssb"])
                scc = A.f32(1024).rearrange("p (r d q j) -> p r d q j", r=2, d=2, q=16)
                DMA("sp", scc, W["sscc"].rearrange("p (r d q j) -> p r d q j", r=2, d=2, q=16), [], ["scc"])
                sp_ = A.f32(32 * 12)
                col = lambda i: sp_[:, i * 32:(i + 1) * 32]
                dt, mag, th, t1, t2, cs, sn, abr, abi, fre, fim, phi = [col(i) for i in range(12)]
                lr, li, ldt = ssc[:, 0:32], ssc[:, 32:64], ssc[:, 64:96]
                K = ["sprep"]
                ACT(dt, ldt, AF.Exp, ["ssc"], K)
                TT("dve", t1, lr, dt, ALU.mult, ["ssc"] + K, K)
                ACT(mag, t1, AF.Exp, K, K)
                TT("dve", th, li, dt, ALU.mult, ["ssc"] + K, K)
                TS("dve", t1, th, 1.0 / TWO_PI, MAGIC, ALU.mult, ALU.add, K, K)
                TS("dve", t2, th, 1.0 / TWO_PI, None, ALU.mult, None, K, K)
                STT("dve", phi, t1, MAGIC, t2, ALU.subtract, ALU.subtract, K, K)
                TS("dve", phi, phi, -1.0, None, ALU.mult, None, K, K)
                ACT(sn, phi, AF.Sin, K, K, scale=TWO_PI)
                ACT(t1, phi, AF.Abs, K, K)
                ACT(cs, t1, AF.Sin, K + ["halfpi"], K, scale=-TWO_PI, bias=halfpi[:, 0:1])
                TT("dve", abr, mag, cs, ALU.mult, K, K)
                TT("dve", abi, mag, sn, ALU.mult, K, K)
                TT("dve", t1, lr, lr, ALU.mult, ["ssc"] + K, K)
                TT("dve", t2, li, li, ALU.mult, ["ssc"] + K, K)
                TT("dve", t1, t1, t2, ALU.add, K, K)
                RECIP(t1, t1, K, K)
                TS("dve", abr, abr, -1.0, None, ALU.add, None, K, K)
                TT("dve", fre, abr, lr, ALU.mult, K, K)
                TT("dve", t2, abi, li, ALU.mult, K, K)
                TT("dve", fre, fre, t2, ALU.add, K, K)
                TT("dve", fre, fre, t1, ALU.mult, K, K)
                TT("dve", fim, abi, lr, ALU.mult, K, K)
                TT("dve", t2, abr, li, ALU.mult, K, K)
                TT("dve", fim, fim, t2, ALU.subtract, K, K)
                TT("dve", fim, fim, t1, ALU.mult, K, K)
                bbar = A.f32(2 * 2 * 256).rearrange("p (d r q h) -> p d r q h", d=2, r=2, q=16)
                bt1 = A.f32(256).rearrange("p (q h) -> p q h", q=16)
                bt2 = A.f32(256).rearrange("p (q h) -> p q h", q=16)
                for d_ in range(2):
                    frb = fre[:, d_ * 16:(d_ + 1) * 16].unsqueeze(2).to_broadcast([128, 16, 16])
                    fib = fim[:, d_ * 16:(d_ + 1) * 16].unsqueeze(2).to_broadcast([128, 16, 16])
                    TT("dve", bt1, ssb[:, 0], frb, ALU.mult, ["ssb"] + K, ["bt1"])
                    TT("dve", bt2, ssb[:, 1], fib, ALU.mult, ["ssb"] + K, ["bt2"])
                    TT("dve", bbar[:, d_, 0], bt1, bt2, ALU.subtract, ["bt1", "bt2"], ["bbar"])
                    TT("dve", bt1, ssb[:, 1], frb, ALU.mult, ["ssb"] + K, ["bt1"])
                    TT("dve", bt2, ssb[:, 0], fib, ALU.mult, ["ssb"] + K, ["bt2"])
                    TT("dve", bbar[:, d_, 1], bt1, bt2, ALU.add, ["bt1", "bt2"], ["bbar"])
                Z = A.f32(4 * 128).rearrange("p (i c) -> p i c", i=4)
                BT = A.bf16(2 * 2 * 2 * 4 * 128).rearrange("p (s d r q c) -> p s d r q c", s=2, d=2, r=2, q=4)
                Cpad = A.bf16(2 * 2 * 3 * 4 * 128).rearrange("p (s d v q c) -> p s d v q c", s=2, d=2, v=3, q=4)
                MSET("pool", Z, 0.0, ["Z"])
                MSET("pool", Cpad, 0.0, [("Cpad", 0), ("Cpad", 1)])

                def bm_copy(blk, g):
                    d_, r_ = g // 2, g % 2
                    for qq in range(4):
                        q = blk * 4 + qq
                        for half in range(2):
                            c0 = 16 * (2 * qq + half)
                            CP("pool", Z[half * 64:(half + 1) * 64, qq, c0:c0 + 16], bbar[half * 64:(half + 1) * 64, d_, r_, q, :],
                               ["bbar"], ["Z"])

                def bm_tr(blk, g):
                    d_, r_ = g // 2, g % 2
                    st_ = blk % 2
                    ps, pk = PS()
                    for qq in range(4):
                        TR(ps[:, qq * 128:(qq + 1) * 128], Z[:, qq, :], ident, ["Z", "ident"], [pk])
                    CP("act", BT[:, st_, d_, r_, :, :], ps.rearrange("p (q c) -> p q c", q=4), [pk], [("BT", st_)])

                def bm_cpad(blk, d_):
                    st_ = blk % 2
                    for v in range(3):
                        r_ = 0 if v < 2 else 1
                        sgn = 1.0 if v == 0 else -1.0
                        for qq in range(4):
                            q = blk * 4 + qq
                            for half in range(2):
                                c0 = 16 * (2 * qq + half)
                                hs = slice(half * 64, (half + 1) * 64)
                                TS("pool", Cpad[hs, st_, d_, v, qq, c0:c0 + 16], scc[hs, r_, d_, q, :], sgn, None, ALU.mult, None,
                                   ["scc"], [("Cpad", st_)])

                def build_block_mats(blk):
                    for g in range(4):
                        bm_copy(blk, g)
                        bm_tr(blk, g)
                    for d_ in range(2):
                        bm_cpad(blk, d_)
                wk = A.f32(L)
                iot_i = wk.bitcast(I32)
                iot = A.f32(L)
                S.op("pool", lambda e: e.iota(iot_i, pattern=[[1, L]], base=0, channel_multiplier=0), [], ["ioti"])
                CP("dve", iot, iot_i, ["ioti"], ["iot"])
                carr = A.f32(2 * 4 * 2).rearrange("p (d q r) -> p d q r", d=2, q=4)
                tA = [wk[:, 0:512], wk[:, 1024:1536]]
                tB = [wk[:, 512:1024], wk[:, 1536:2048]]
                m2 = [A.bf16(512) for _ in range(2)]
                m4 = [A.bf16(512) for _ in range(2)]
                sinT = [A.bf16(512) for _ in range(4)]
                cosT = [A.bf16(512) for _ in range(4)]
                xre = [A.bf16(512) for _ in range(2)]
                xim = [A.bf16(512) for _ in range(2)]
                m1 = [A.bf16(512) for _ in range(2)]
                m3 = [A.bf16(512) for _ in range(2)]
                Gre = [A.bf16(512) for _ in range(2)]
                Gim = [A.bf16(512) for _ in range(2)]
                Pp = [[A.bf16(512) for _ in range(4)] for _ in range(2)]
                ygt = A.f32(512)
                ygT = zsT
                ybst = [A.bf16(512), A.bf16(512)]
                mod_next = l + 1 < DEPTH
                if mod_next:
                    awq = [A.f32(1536), A.f32(1536)]
                    mod_jobs = [(kc, part) for kc in range(8) for part in range(4)]
                mod_di = 0
                mod_ci = 0
                mod_adds = []
                it = 0
                for blk in range(4):
                    if blk == 0:
                        build_block_mats(0)
                    st_ = blk % 2
                    ybanks = [0, 1, 2, 3]
                    bank_state["free"] = [4, 5, 6, 7]
                    ykeys = [("ps", b) for b in ybanks]
                    iters = []
                    for d_ in range(2):
                        for step in range(4):
                            tt = step if d_ == 0 else 3 - step
                            for qq in range(4):
                                iters.append((d_, step, tt, qq, it % 2, it % 4))
                                it += 1

                    def stage_a(d_, step, tt, qq, b, b4):
                        k4 = lambda s_: (s_, b4)
                        sl = slice(tt * 512, (tt + 1) * 512)
                        q = blk * 4 + qq
                        kb = lambda s_: (s_, b)
                        phc = phi[:, d_ * 16 + q:d_ * 16 + q + 1]
                        ACT(tA[b], iot[:, sl], AF.Identity, ["iot", "magicc"] + K, [kb("tA")], scale=phc, bias=magicc[:, 0:1])
                        ACT(tB[b], iot[:, sl], AF.Copy, ["iot"] + K, [kb("tB")], scale=phc)
                        STT("dve", tA[b], tA[b], MAGIC, tB[b], ALU.subtract, ALU.subtract, [kb("tA"), kb("tB")], [kb("tA")])
                        ACT(sinT[b4], tA[b], AF.Sin, [kb("tA")], [k4("sinT")], scale=(-TWO_PI if d_ == 0 else TWO_PI))
                        ACT(tB[b], tA[b], AF.Abs, [kb("tA")], [kb("tB")])
                        ACT(cosT[b4], tB[b], AF.Sin, [kb("tB"), "halfpi"], [k4("cosT")], scale=-TWO_PI, bias=halfpi[:, 0:1])
                        psr, pkr = PS()
                        MM(psr, BT[:, st_, d_, 0, qq, :], zsT[:, blk, sl], True, True, [("BT", st_), ("zs", blk, tt)], [pkr], True)
                        psi, pki = PS()
                        MM(psi, BT[:, st_, d_, 1, qq, :], zsT[:, blk, sl], True, True, [("BT", st_), ("zs", blk, tt)], [pki], True)
                        CP("act", xre[b], psr, [pkr], [kb("xre")])
                        CP("act", xim[b], psi, [pki], [kb("xim")])

                    def stage_b1(d_, step, tt, qq, b, b4):
                        kb = lambda s_: (s_, b)
                        k4 = lambda s_: (s_, b4)
                        TT("dve", m1[b], cosT[b4], xre[b], ALU.mult, [k4("cosT"), kb("xre")], [kb("m1")])
                        TT("dve", m2[b], sinT[b4], xim[b], ALU.mult, [k4("sinT"), kb("xim")], [kb("m2")])
                        TT("dve", m3[b], cosT[b4], xim[b], ALU.mult, [k4("cosT"), kb("xim")], [kb("m3")])
                        TT("dve", m4[b], sinT[b4], xre[b], ALU.mult, [k4("sinT"), kb("xre")], [kb("m4")])
                        TT("dve", m3[b], m3[b], m4[b], ALU.subtract, [kb("m3"), kb("m4")], [kb("m3")])
                        TT("dve", m1[b], m1[b], m2[b], ALU.add, [kb("m1"), kb("m2")], [kb("m1")])

                    def stage_b2(d_, step, tt, qq, b, b4):
                        q = blk * 4 + qq
                        kb = lambda s_: (s_, b)
                        rho = mag[:, d_ * 16 + q:d_ * 16 + q + 1].to_broadcast([128, 512])
                        ck = ("carr", d_, qq)
                        if d_ == 0:
                            o_re, i_re, o_im, i_im = Gre[b], m1[b], Gim[b], m3[b]
                            last = slice(511, 512)
                        else:
                            o_re, i_re, o_im, i_im = Gre[b][:, ::-1], m1[b][:, ::-1], Gim[b][:, ::-1], m3[b][:, ::-1]
                            last = slice(0, 1)
                        init_re = 0.0 if step == 0 else carr[:, d_, qq, 0:1]
                        init_im = 0.0 if step == 0 else carr[:, d_, qq, 1:2]
                        SCAN(o_im, rho, i_im, init_im, [kb("m3"), ck] + K, [kb("Gim")])
                        SCAN(o_re, rho, i_re, init_re, [kb("m1"), ck] + K, [kb("Gre")])
                        if step < 3:
                            CP("pool", carr[:, d_, qq, 0:1], Gre[b][:, last], [kb("Gre")], [ck])
                            CP("pool", carr[:, d_, qq, 1:2], Gim[b][:, last], [kb("Gim")], [ck])

                    def stage_b3(d_, step, tt, qq, b, b4):
                        kb = lambda s_: (s_, b)
                        k4 = lambda s_: (s_, b4)
                        TT("dve", Pp[b][0], cosT[b4], Gre[b], ALU.mult, [k4("cosT"), kb("Gre")], [kb("P0")])
                        TT("dve", Pp[b][1], sinT[b4], Gim[b], ALU.mult, [k4("sinT"), kb("Gim")], [kb("P1")])
                        TT("dve", Pp[b][2], sinT[b4], Gre[b], ALU.mult, [k4("sinT"), kb("Gre")], [kb("P2")])
                        TT("dve", Pp[b][3], cosT[b4], Gim[b], ALU.mult, [k4("cosT"), kb("Gim")], [kb("P3")])
                        yps = banks[ybanks[tt]][:]
                        first = (d_ == 0 and qq == 0)
                        lastm = (d_ == 1 and qq == 3)
                        vs = [0, 1, 2, 2]
                        for m in range(4):
                            MM(yps, Cpad[:, st_, d_, vs[m], qq, :], Pp[b][m], first and m == 0, lastm and m == 3,
                               [("Cpad", st_), kb("P%d" % m)], [ykeys[tt]], m == 3)

                    n_it = len(iters)
                    for t_ in range(-2, n_it + 1):
                        if blk < 3:
                            if t_ in (2, 6, 10, 14):
                                bm_copy(blk + 1, (t_ - 2) // 4)
                            if t_ in (4, 8, 12, 16):
                                bm_tr(blk + 1, (t_ - 4) // 4)
                            if t_ in (18, 22):
                                bm_cpad(blk + 1, (t_ - 18) // 4)
                        if mod_next and t_ >= 0 and t_ % 4 == 1 and mod_di < len(mod_jobs):
                            kc_, part_ = mod_jobs[mod_di]
                            bi_ = mod_di % 2
                            mod_piece_dma(l + 1, kc_, part_, 4, awq[bi_], ("awq", bi_))
                            mod_di += 1
                        if mod_next and t_ >= 0 and t_ % 4 == 0 and mod_adds:
                            mod_adds.pop(0)()
                            if mod_ci == len(mod_jobs) and not mod_adds:
                                mod_finish(l + 1, 4)
                        if mod_next and t_ >= 0 and t_ % 4 == 3 and mod_ci < mod_di:
                            kc_, part_ = mod_jobs[mod_ci]
                            bi_ = mod_ci % 2
                            mod_adds.append(mod_piece(l + 1, kc_, part_, 4, awq[bi_], ("awq", bi_), dma=False, defer_add=True))
                            mod_ci += 1
                        if 0 <= t_ + 2 < n_it:
                            stage_a(*iters[t_ + 2])
                        if 0 <= t_ + 1 < n_it:
                            stage_b1(*iters[t_ + 1])
                        if 0 <= t_ < n_it:
                            stage_b2(*iters[t_])
                        if 0 <= t_ - 1 < n_it:
                            stage_b3(*iters[t_ - 1])
                    for tt in range(4):
                        sl = slice(tt * 512, (tt + 1) * 512)
                        STT("dve", ygt, zsT[:, blk, sl], pc[:, R_SSD + blk:R_SSD + blk + 1], banks[ybanks[tt]][:], ALU.mult, ALU.add,
                            [("zs", blk, tt), ykeys[tt], pck], ["ygt"])
                        ACT(ygT[:, blk, sl], ygt, AF.Gelu, ["ygt"], [("yg", blk, tt)])
                    bank_state["free"] = list(range(8))
                while mod_next and mod_adds:
                    mod_adds.pop(0)()
                    if mod_ci == len(mod_jobs) and not mod_adds:
                        mod_finish(l + 1, 4)
                wgl = A.bf16(4 * 512).rearrange("p (k n) -> p k n", k=4)
                DMA("pool", wgl, W["gluw"].rearrange("(k p) n -> p k n", p=128), [], ["wgl"])
                sg = ygt
                for ob in range(4):
                    for tt in range(4):
                        sl = slice(tt * 512, (tt + 1) * 512)
                        ps, pk = PS()
                        for kc in range(4):
                            MM(ps, wgl[:, kc, ob * 128:(ob + 1) * 128], ygT[:, kc, sl], kc == 0, kc == 3, ["wgl", ("yg", kc, tt)], [pk], kc == 3)
                        ACT(sg, ps, AF.Sigmoid, [pk, pck], ["ygt"], bias=pc[:, R_GLUB + ob:R_GLUB + ob + 1])
                        bb = (ob * 4 + tt) % 2
                        TT("dve", ybst[bb], sg, ygT[:, ob, sl], ALU.mult, ["ygt", ("yg", ob, tt)], [("ybst", bb)])
                        DMA("sp", y_d[1, :, ob, sl], ybst[bb], [("ybst", bb)], [("yD", 1, tt)])
                        if l == DBGL and "yb" in dbg_d:
                            DMA("pool", dbg_d["yb"][:, ob * L + tt * 512:ob * L + (tt + 1) * 512], ybst[bb], [("ybst", bb)], [("dbgo", "yb", ob, tt)])
                S.barrier()
                A.pop()
                checkpoint(2)
                hT = A.bf16(8 * L).rearrange("p (c t) -> p c t", c=8)
                A.push()
                wlr = A.bf16(8 * 1024).rearrange("p (k n) -> p k n", k=8)
                DMA("pool", wlr, W["win"][:, 1024:2048].rearrange("(k p) n -> p k n", p=128), [], ["wlr"])
                lbd = A.bf16(16 * 128).rearrange("p (i c) -> p i c", i=16)
                DMA("pool", lbd, W["lrubd"].rearrange("p (i c) -> p i c", i=16), [], ["lbd"])
                rms_mod(l, 0, range(4), hT, "h")
                S.barrier()
                cl = A.f32(8)
                lt1 = A.f32(8)
                lt2 = A.f32(8)
                lam = pc[:, R_LAM:R_LAM + 8]
                ACT(lt1, lam, AF.Abs, [pck], ["lt1"])
                ACT(lt1, lt1, AF.Exp, ["lt1"], ["lt1"], scale=-1.0)
                ACT(lt1, lt1, AF.Ln, ["lt1", "onec"], ["lt1"], bias=onec[:, 0:1])
                TS("dve", lt2, lam, -1.0, 0.0, ALU.mult, ALU.max, [pck], ["lt2"])
                TT("dve", cl, lt1, lt2, ALU.add, ["lt1", "lt2"], ["cl"])
                TS("dve", cl, cl, -8.0, None, ALU.mult, None, ["cl"], ["cl"])
                zrp = A.f32(L + 4)
                MSET("pool", zrp, 0.0, [("zrp", tt) for tt in range(4)])
                xc = A.f32(L)
                xcb = A.bf16(L)
                rr = A.f32(L)
                ii = A.f32(L)
                aa = A.f32(L)
                a2 = A.f32(L)
                hsum = A.f32(L)
                hh = rr
                gg = aa
                ycst = A.bf16(L)
                for j in range(4):
                    for tt in range(4):
                        ps, pk = PS()
                        for kc in range(8):
                            MM(ps, wlr[:, kc, j * 128:(j + 1) * 128], hT[:, kc, tt * 512:(tt + 1) * 512], kc == 0, kc == 7,
                               ["wlr", ("h", kc, tt)], [pk], kc == 7)
                        CP("act", zrp[:, 2 + tt * 512:2 + (tt + 1) * 512], ps, [pk], [("zrp", tt)])
                    zk = [("zrp", tt) for tt in range(4)]
                    cw = lambda k: pc[:, R_CW + k * 4 + j:R_CW + k * 4 + j + 1]
                    ACT(xc, zrp[:, 0:L], AF.Identity, zk + [pck], ["xc"], bias=pc[:, R_CB + j:R_CB + j + 1], scale=cw(0))
                    for k in range(1, 4):
                        STT("dve" if k != 2 else "pool", xc, zrp[:, k:k + L], cw(k), xc, ALU.mult, ALU.add, zk + ["xc", pck], ["xc"])
                    CP("act", xcb, xc, ["xc"], ["xcb"])
                    if l == DBGL and j == 0:
                        DBG("xc0", xc, ["xc"])
                        DBG("zrp0", zrp[:, 0:L], zk)
                    for d_ in range(2):
                        for (which, dst, dk, boff) in ((0, rr, "rr", R_BA), (1, ii, "ii", R_BX)):
                            for tt in range(4):
                                sl = slice(tt * 512, (tt + 1) * 512)
                                ps, pk = PS()
                                MM(ps, lbd[:, which * 8 + d_ * 4 + j, :], xcb[:, sl], True, True, ["lbd", "xcb"], [pk], True)
                                ACT(dst[:, sl], ps, AF.Sigmoid, [pk, pck], [dk], bias=pc[:, boff + d_ * 4 + j:boff + d_ * 4 + j + 1])
                        if l == DBGL and j == 0 and d_ == 0:
                            DBG("rr0", rr, ["rr"])
                            DBG("ii0", ii, ["ii"])
                            DBG("cl", cl, ["cl"])
                        ACT(aa, rr, AF.Exp, ["rr", "cl"], ["aa"], scale=cl[:, d_ * 4 + j:d_ * 4 + j + 1])
                        if l == DBGL and j == 0 and d_ == 0:
                            DBG("aa0", aa, ["aa"])
                        TT("dve", a2, aa, aa, ALU.mult, ["aa"], ["a2"])
                        ACT(a2, a2, AF.Sqrt, ["a2", "onec"], ["a2"], scale=-1.0, bias=onec[:, 0:1])
                        TT("dve", ii, ii, a2, ALU.mult, ["ii", "a2"], ["ii"])
                        TT("dve", ii, ii, xc, ALU.mult, ["ii", "xc"], ["ii"])
                        if d_ == 0:
                            SCAN(hsum, aa, ii, 0.0, ["aa", "ii"], ["hsum"])
                        else:
                            SCAN(hh[:, ::-1], aa[:, ::-1], ii[:, ::-1], 0.0, ["aa", "ii"], ["rr"])
                            TT("dve", hsum, hsum, hh, ALU.add, ["hsum", "rr"], ["hsum"])
                    for tt in range(4):
                        sl = slice(tt * 512, (tt + 1) * 512)
                        ps, pk = PS()
                        for kc in range(8):
                            MM(ps, wlr[:, kc, 512 + j * 128:512 + (j + 1) * 128], hT[:, kc, sl], kc == 0, kc == 7,
                               ["wlr", ("h", kc, tt)], [pk], kc == 7)
                        ACT(gg[:, sl], ps, AF.Gelu, [pk], ["aa"])
                    TT("dve", ycst, hsum, gg, ALU.mult, ["hsum", "aa"], ["ycst"])
                    DMA("sp", y_d[2, :, j, :], ycst, ["ycst"], [("yD", 2, tt) for tt in range(4)])
                    if l == DBGL and "yc" in dbg_d:
                        DMA("pool", dbg_d["yc"][:, j * L:(j + 1) * L], ycst, ["ycst"], [("dbgo", "yc", j)])
                S.barrier()
                A.pop()

                checkpoint(3)
                A.push()
                yast = [A.bf16(512), A.bf16(512)]
                ydst = [A.bf16(512).rearrange("p (g q) -> p g q", g=4), A.bf16(512).rearrange("p (g q) -> p g q", g=4)]
                wgm = A.bf16(8 * 1024).rearrange("p (k n) -> p k n", k=8)
                DMA("pool", wgm, W["win"][:, 2048:3072].rearrange("(k p) n -> p k n", p=128), [], ["wgm"])
                gng = A.f32(512)
                DMA("sp", gng, W["gng"], [], ["gng"])
                gws = A.bf16(4 * 128).rearrange("p (g q) -> p g q", g=4)
                DMA("pool", gws, W["gws"].rearrange("p (g q) -> p g q", g=4), [], ["gws"])
                gbs = A.bf16(512)
                DMA("pool", gbs[0:1, :], W["gbs"], [], ["gbs"])
                A.push()
                wpo = A.bf16(8 * 512).rearrange("p (k n) -> p k n", k=8)
                DMA("pool", wpo, W["win"][:, 0:512].rearrange("(k p) n -> p k n", p=128), [], ["wpo"])
                pmb = A.bf16(20 * 128).rearrange("p (g v t) -> p g v t", g=4, v=5)
                DMA("pool", pmb, pm_d.rearrange("p (g v t) -> p g v t", g=4, v=5), [], ["pmb"])
                pwb = A.bf16(4 * 128).rearrange("p (g d) -> p g d", g=4)
                DMA("pool", pwb, W["poolw"].rearrange("g c d -> c g d"), [], ["pwb"])
                utok = A.bf16(16 * 512).rearrange("p (n c) -> p n c", n=16)
                for n in range(16):
                    ps, pk = PS()
                    for kc in range(8):
                        MM(ps, hT[:, kc, n * 128:(n + 1) * 128], wpo[:, kc, :], kc == 0, kc == 7, ["wpo", ("h", kc, n // 4)], [pk], kc == 7)
                    CP("act" if n % 2 == 0 else "dve", utok[:, n, :], ps, [pk], [("utok", n)])
                    if l == DBGL and n == 1:
                        DBG("utok1", utok[:, n, :], [("utok", n)])
                pooled = A.bf16(L)
                for g in range(4):
                    for tt in range(4):
                        ps, pk = PS()
                        for n4 in range(4):
                            nt = tt * 4 + n4
                            srcs = [ns for ns in (nt - 1, nt, nt + 1) if 0 <= ns < 16]
                            for i, ns in enumerate(srcs):
                                if ns == nt - 1:
                                    v = 0
                                elif ns == nt + 1:
                                    v = 2
                                else:
                                    v = 3 if nt == 0 else (4 if nt == 15 else 1)
                                MM(ps[:, n4 * 128:(n4 + 1) * 128], utok[:, ns, g * 128:(g + 1) * 128], pmb[:, g, v, :], i == 0, i == len(srcs) - 1,
                                   [("utok", ns), "pmb"], [pk], (n4 == 3 and i == len(srcs) - 1))
                        CP("act", pooled[:, tt * 512:(tt + 1) * 512], ps, [pk], [("pooled", tt)])
                    if l == DBGL and g == 3:
                        DBG("pooled3", pooled, [("pooled", tt) for tt in range(4)])
                    for tt in range(4):
                        sl = slice(tt * 512, (tt + 1) * 512)
                        ps, pk = PS()
                        MM(ps, pwb[:, g, :], pooled[:, sl], True, True, ["pwb", ("pooled", tt)], [pk], True)
                        bb = (g * 4 + tt) % 2
                        ACT(yast[bb], ps, AF.Copy, [pk, pck], [("yast", bb)], scale=pc[:, R_PSC + g:R_PSC + g + 1])
                        DMA("sp", y_d[0, :, g, sl], yast[bb], [("yast", bb)], [("yD", 0, tt)])
                        if l == DBGL and "ya" in dbg_d:
                            DMA("pool", dbg_d["ya"][:, g * L + tt * 512:g * L + (tt + 1) * 512], yast[bb], [("yast", bb)], [("dbgo", "ya", g, tt)])
                S.barrier()
                A.pop()
                A.push()
                guT = A.bf16(4 * L).rearrange("p (c t) -> p c t", c=4)

                def sink_gu(j, tt, ps, pk):
                    ACT(guT[:, j, tt * 512:(tt + 1) * 512], ps, AF.Gelu, [pk], [("gu", j, tt)])
                proj_fm(wgm, "wgm", 4, hT, "h", range(4), sink_gu)
                gv = [A.f32(512), A.f32(512)]
                gsq = A.f32(512)
                ssq = [A.f32(1), A.f32(1)]
                vn = [A.bf16(512), A.bf16(512)]
                def gm_stage1(n):
                    b = n % 2
                    ps, pk = PS()
                    for kc in range(8):
                        MM(ps, hT[:, kc, n * 128:(n + 1) * 128], wgm[:, kc, 512:1024], kc == 0, kc == 7, ["wgm", ("h", kc, n // 4)], [pk], kc == 7)
                    ACT(gv[b], ps, AF.Gelu, [pk], [("gv", b)])
                    ACT(gsq, gv[b], AF.Square, [("gv", b)], ["gsq", ("ssq", b)], accum=ssq[b][:, 0:1])
                    ACT(ssq[b], ssq[b], AF.Sqrt, [("ssq", b), "epsc"], [("ssq", b)], bias=epsc[:, 0:1], scale=1.0 / 512)
                    RECIP(ssq[b], ssq[b], [("ssq", b)], [("ssq", b)])
                    STT("dve", vn[b], gv[b], ssq[b][:, 0:1], gng, ALU.mult, ALU.mult, [("gv", b), ("ssq", b), "gng"], [("vn", b)])
                    if l == DBGL and n == 1:
                        DBG("gv1", gv[b], [("gv", b)])
                        DBG("vn1", vn[b], [("vn", b)])
                        DBG("ssq1", ssq[b], [("ssq", b)])

                def gm_stage2(n):
                    b = n % 2
                    ps, pk = PS()
                    for g in range(4):
                        MM(ps[:, g * 128:(g + 1) * 128], vn[b][:, g * 128:(g + 1) * 128], gws[:, g, :], True, False, [("vn", b), "gws"], [pk], False)
                        MM(ps[:, g * 128:(g + 1) * 128], onesb[0:1, :], gbs[0:1, g * 128:(g + 1) * 128], False, True, ["onesb", "gbs"], [pk], g == 3)
                    TT("dve", ydst[b], ps.rearrange("p (g q) -> p g q", g=4), guT[:, :, n * 128:(n + 1) * 128], ALU.mult,
                       [pk] + [("gu", j, n // 4) for j in range(4)], [("ydst", b)])
                    DMA("sp", y_d[3, :, :, n * 128:(n + 1) * 128], ydst[b], [("ydst", b)], [("yD", 3, n // 4)])
                    if l == DBGL and "yd" in dbg_d:
                        DMA("pool", dbg_d["yd"].rearrange("p (g t) -> p g t", g=4)[:, :, n * 128:(n + 1) * 128], ydst[b], [("ydst", b)], [("dbgo", "yd", n)])

                gm_stage1(0)
                for n in range(16):
                    if n + 1 < 16:
                        gm_stage1(n + 1)
                    gm_stage2(n)
                A.pop()
                S.barrier()
                checkpoint(35)
                yh = [A.bf16(4 * 1024).rearrange("p (c t) -> p c t", c=4) for _ in range(4)]
                mrg = wgm
                wgb = [A.bf16(8 * 128).rearrange("p (k n) -> p k n", k=8) for _ in range(3)]
                wbb = [A.bf16(4 * 128).rearrange("p (k n) -> p k n", k=4) for _ in range(3)]
                wob = [A.bf16(8 * 128).rearrange("p (k n) -> p k n", k=8) for _ in range(2)]
                sgt = [gng, A.f32(512)]
                macc = [A.f32(512), A.f32(512)]
                mtmp = [A.f32(512), A.f32(512)]
                wi = 0
                yi = 0
                for hf in range(2):
                    for k in range(4):
                        DMA("sp", yh[k], y_d[k, :, :, hf * 1024:(hf + 1) * 1024], [("yD", k, hf * 2), ("yD", k, hf * 2 + 1)], [("yh", k)])
                    for dch in range(8):
                        for k in range(4):
                            b3 = wi % 3
                            wi += 1
                            DMA("pool", wgb[b3], W["wgate"][:, k * 1024 + dch * 128:k * 1024 + (dch + 1) * 128].rearrange("(kc p) n -> p kc n", p=128),
                                [], [("wgb", b3)])
                            DMA("pool", wbb[b3], W["wbr"][k, :, dch * 128:(dch + 1) * 128].rearrange("(kc p) n -> p kc n", p=128), [], [("wbb", b3)])
                            for t2 in range(2):
                                tt = hf * 2 + t2
                                sl = slice(tt * 512, (tt + 1) * 512)
                                ps, pk = PS()
                                for kc in range(8):
                                    MM(ps, wgb[b3][:, kc, :], hT[:, kc, sl], kc == 0, kc == 7, [("wgb", b3), ("h", kc, tt)], [pk], kc == 7)
                                ACT(sgt[t2], ps, AF.Sigmoid, [pk, pck], [("sgt", t2)], bias=pc[:, R_BG + k * 8 + dch:R_BG + k * 8 + dch + 1])
                                ps2, pk2 = PS()
                                for kc in range(4):
                                    MM(ps2, wbb[b3][:, kc, :], yh[k][:, kc, t2 * 512:(t2 + 1) * 512], kc == 0, kc == 3, [("wbb", b3), ("yh", k)], [pk2], kc == 3)
                                if k == 0:
                                    TT("dve", macc[t2], sgt[t2], ps2, ALU.mult, [("sgt", t2), pk2], [("macc", t2)])
                                elif k < 3:
                                    TT("dve", mtmp[t2], sgt[t2], ps2, ALU.mult, [("sgt", t2), pk2], [("mtmp", t2)])
                                    TT("dve", macc[t2], macc[t2], mtmp[t2], ALU.add, [("macc", t2), ("mtmp", t2)], [("macc", t2)])
                                else:
                                    TT("dve", mtmp[t2], sgt[t2], ps2, ALU.mult, [("sgt", t2), pk2], [("mtmp", t2)])
                                    TT("dve", mrg[:, dch, t2 * 512:(t2 + 1) * 512], macc[t2], mtmp[t2], ALU.add, [("macc", t2), ("mtmp", t2)],
                                       [("mrg", dch, t2)])
                    for do in range(8):
                        b2 = do % 2
                        DMA("pool", wob[b2], W["wout"][:, do * 128:(do + 1) * 128].rearrange("(kc p) n -> p kc n", p=128), [], [("wob", b2)])
                        for t2 in range(2):
                            tt = hf * 2 + t2
                            sl = slice(tt * 512, (tt + 1) * 512)
                            ps, pk = PS()
                            for kc in range(8):
                                MM(ps, wob[b2][:, kc, :], mrg[:, kc, t2 * 512:(t2 + 1) * 512], kc == 0, kc == 7, [("wob", b2), ("mrg", kc, t2)], [pk], kc == 7)
                            STT("dve", xT[:, do, sl], ps, modt[l][:, 16 + do:17 + do], xT[:, do, sl], ALU.mult, ALU.add,
                                [pk, ("mod", l), ("x", do, tt)], [("x", do, tt)])
                S.barrier()
                A.pop()
                A.pop()

                if l == DBGL:
                    DBG("x1", xT, [("x", c, tt) for c in range(8) for tt in range(4)])
                checkpoint(4)
                A.push()
                h2 = A.bf16(8 * 1024).rearrange("p (c t) -> p c t", c=8)
                w1b = [A.bf16(8 * 512).rearrange("p (k n) -> p k n", k=8) for _ in range(2)]
                w2b = [A.bf16(4 * 1024).rearrange("p (k n) -> p k n", k=4) for _ in range(2)]
                f1 = [A.bf16(4 * 1024).rearrange("p (c t) -> p c t", c=4) for _ in range(2)]
                fsq = [A.f32(512), A.f32(512)]
                fi = 0
                for hf in range(2):
                    rms_mod(l, 1, [hf * 2, hf * 2 + 1], h2, "h2")
                    for fg in range(8):
                        b = fi % 2
                        fi += 1
                        DMA("pool", w1b[b], W["wff1"][:, fg * 512:(fg + 1) * 512].rearrange("(kc p) n -> p kc n", p=128), [], [("w1b", b)])
                        DMA("pool", w2b[b], W["wff2"][fg * 512:(fg + 1) * 512, :].rearrange("(kc p) n -> p kc n", p=128), [], [("w2b", b)])
                        for fc in range(4):
                            for t2 in range(2):
                                tt = hf * 2 + t2
                                ps, pk = PS()
                                for kc in range(8):
                                    MM(ps, w1b[b][:, kc, fc * 128:(fc + 1) * 128], h2[:, kc, t2 * 512:(t2 + 1) * 512], kc == 0, kc == 7,
                                       [("w1b", b), ("h2", kc, tt)], [pk], kc == 7)
                                ACT(fsq[t2], ps, AF.Square, [pk], [("fsq", t2)])
                                STT("dve", f1[b][:, fc, t2 * 512:(t2 + 1) * 512], ps, 0.0, fsq[t2], ALU.is_gt, ALU.mult, [pk, ("fsq", t2)],
                                    [("f1", b, fc, t2)])
                        for do in range(8):
                            for t2 in range(2):
                                tt = hf * 2 + t2
                                sl = slice(tt * 512, (tt + 1) * 512)
                                ps, pk = PS()
                                for fc in range(4):
                                    MM(ps, w2b[b][:, fc, do * 128:(do + 1) * 128], f1[b][:, fc, t2 * 512:(t2 + 1) * 512], fc == 0, fc == 3,
                                       [("w2b", b), ("f1", b, fc, t2)], [pk], fc == 3)
                                STT("dve", xT[:, do, sl], ps, modt[l][:, 40 + do:41 + do], xT[:, do, sl], ALU.mult, ALU.add,
                                    [pk, ("mod", l), ("x", do, tt)], [("x", do, tt)])
                if l == DBGL:
                    DBG("x2", xT, [("x", c, tt) for c in range(8) for tt in range(4)])
                S.barrier()
                A.pop()
                checkpoint(5 + l)

        except _Stop:
            A.top = persist_top
            A.marks = []
            bank_state["free"] = list(range(8))
            S.barrier()
        A.push()
        sq = A.bf16(8 * 512).rearrange("p (c t) -> p c t", c=8)
        rt = A.f32(512)
        yn = A.f32(8 * 512).rearrange("p (c t) -> p c t", c=8)
        otok = [A.f32(D), A.f32(D)]
        pl = pcol[DEPTH - 1]
        outs = []
        for tt in range(4):
            sl = slice(tt * 512, (tt + 1) * 512)
            ACT(sq, xT[:, :, sl], AF.Square, [("x", c, tt) for c in range(8)], ["sq"])
            ps, pk = PS()
            for c in range(8):
                MM(ps, onesb, sq[:, c, :], c == 0, c == 7, ["sq", "onesb"], [pk], c == 7)
            ACT(rt, ps, AF.Sqrt, [pk, "epsc"], ["rt"], bias=epsc[:, 0:1], scale=1.0 / D)
            RECIP(rt, rt, ["rt"], ["rt"])
            for c in range(8):
                STT("dve" if c % 2 == 0 else "pool", yn[:, c, :], xT[:, c, sl], pl[:, R_FG + c:R_FG + c + 1], rt, ALU.mult, ALU.mult,
                    [("x", c, tt), "rt", ("pcol", DEPTH - 1)], [("yn", c)])
            for n4 in range(4):
                n = tt * 4 + n4
                ot = otok[n % 2]
                for cg in range(2):
                    ps, pk = PS()
                    for c4 in range(4):
                        c = cg * 4 + c4
                        TR(ps[:, c4 * 128:(c4 + 1) * 128], yn[:, c, n4 * 128:(n4 + 1) * 128], ident, [("yn", c), "ident"], [pk])
                    CP("act" if cg == 0 else "dve", ot[:, cg * 512:(cg + 1) * 512], ps, [pk], [("otok", n % 2, cg)])
                outs.append(DMA("sp", out_d[n * 128:(n + 1) * 128, :], ot, [("otok", n % 2, 0), ("otok", n % 2, 1)], [("out", n)]))
        A.pop()
        S.barrier()
        S.emit(block)
        print("instructions:", S.ninstr, "arena peak words:", A.peak)
    return nc


def _pool_mats():
    wins = (2, 4, 8, 16)
    Lm = 3 * 128
    out = np.zeros((128, 4, 5, 128), np.float32)
    for g, w in enumerate(wins):
        def blockfor(nt_kind):
            if nt_kind == "mid":
                Lv, t0 = 5 * 128, 2 * 128
            elif nt_kind == "first":
                Lv, t0 = 3 * 128, 0
            else:
                Lv, t0 = 3 * 128, 2 * 128
            Pm = np.zeros((Lv, Lv), np.float64)
            for t in range(Lv):
                lo = min(max(t - w // 2, 0), Lv)
                hi = min(max(t - w // 2 + w, 0), Lv)
                Pm[lo:hi, t] = 1.0 / (hi - lo)
                Pm[t, t] -= 1.0
            return Pm, t0
        Pm, t0 = blockfor("mid")
        out[:, g, 0, :] = Pm[t0 - 128:t0, t0:t0 + 128]
        out[:, g, 1, :] = Pm[t0:t0 + 128, t0:t0 + 128]
        out[:, g, 2, :] = Pm[t0 + 128:t0 + 256, t0:t0 + 128]
        Pm, t0 = blockfor("first")
        out[:, g, 3, :] = Pm[t0:t0 + 128, t0:t0 + 128]
        Pm, t0 = blockfor("last")
        out[:, g, 4, :] = Pm[t0:t0 + 128, t0:t0 + 128]
    return np.ascontiguousarray(out.reshape(128, 20 * 128))


def _prep_shared(inp):
    f = lambda a: np.ascontiguousarray(np.asarray(a, dtype=np.float32))
    sh = {"pm": _pool_mats()}
    for l in range(DEPTH):
        rows = [
            f(inp["ada_b"][l]).reshape(48, 128),
            f(inp["norm1_g"][l]).reshape(8, 128),
            f(inp["pool_scale"][l]).reshape(4, 128),
            f(inp["ssm_d"][l]).reshape(4, 128),
            f(inp["ssm_glu_b"][l]).reshape(4, 128),
            f(inp["lru_conv_w"][l]).reshape(16, 128),
            f(inp["lru_conv_b"][l]).reshape(4, 128),
            f(inp["lru_ba"][l]).reshape(8, 128),
            f(inp["lru_bx"][l]).reshape(8, 128),
            f(inp["lru_lam"][l]).reshape(8, 128),
            f(inp["b_gate"][l]).reshape(32, 128),
            f(inp["norm2_g"][l]).reshape(8, 128),
            f(inp["final_g"]).reshape(8, 128),
        ]
        sh[f"pv{l}"] = np.ascontiguousarray(np.concatenate(rows, axis=0))
        sh[f"adaw{l}"] = f(inp["ada_w"][l])
        sh[f"win{l}"] = f(inp["w_in"][l])
        sh[f"poolw{l}"] = f(inp["pool_w"][l])
        def sc(a):
            return a.reshape(2, 16, 2, 64).transpose(2, 3, 0, 1).reshape(128, 32)
        lre = f(inp["ssm_lam_re"][l])
        lim = f(inp["ssm_lam_im"][l])
        ldt = np.broadcast_to(f(inp["ssm_log_dt"][l])[:, :, None], (2, 32, 64))
        sh[f"ssc{l}"] = np.ascontiguousarray(np.concatenate([sc(lre), sc(lim), sc(ldt)], axis=1))
        def sb(a):
            return a.reshape(16, 2, 64, 16).transpose(1, 2, 0, 3).reshape(128, 16, 16)
        sh[f"ssb{l}"] = np.ascontiguousarray(np.stack([sb(f(inp["ssm_b_re"][l])), sb(f(inp["ssm_b_im"][l]))], axis=1).reshape(128, 512))
        def scm(a):
            return a.reshape(2, 16, 2, 16, 64).transpose(2, 4, 0, 1, 3).reshape(128, 2, 16, 16)
        sh[f"sscc{l}"] = np.ascontiguousarray(np.stack([scm(f(inp["ssm_c_re"][l])), scm(f(inp["ssm_c_im"][l]))], axis=1).reshape(128, 1024))
        sh[f"gluw{l}"] = f(inp["ssm_glu_w"][l])
        bd = np.zeros((128, 16, 128), np.float32)
        for which, nm in enumerate(("lru_wa", "lru_wx")):
            w = f(inp[nm][l])
            for d_ in range(2):
                for j in range(4):
                    i = which * 8 + d_ * 4 + j
                    bd[0:64, i, 0:64] = w[d_, 2 * j]
                    bd[64:128, i, 64:128] = w[d_, 2 * j + 1]
        sh[f"lrubd{l}"] = np.ascontiguousarray(bd.reshape(128, 16 * 128))
        sh[f"gng{l}"] = np.ascontiguousarray(np.broadcast_to(f(inp["gmlp_norm_g"][l])[None, :], (128, 512)))
        sh[f"gws{l}"] = np.ascontiguousarray(f(inp["gmlp_ws"][l]).transpose(2, 0, 1).reshape(128, 512))
        sh[f"gbs{l}"] = f(inp["gmlp_bs"][l]).reshape(1, 512)
        sh[f"wbr{l}"] = f(inp["w_branch"][l])
        sh[f"wgate{l}"] = f(inp["w_gate"][l])
        sh[f"wout{l}"] = f(inp["w_out"][l])
        sh[f"wff1{l}"] = f(inp["w_ff1"][l])
        sh[f"wff2{l}"] = f(inp["w_ff2"][l])
    return sh


_CACHE = {}


def kernel(**inputs):
    x = np.ascontiguousarray(np.asarray(inputs["x"], dtype=np.float32))
    c = np.ascontiguousarray(np.asarray(inputs["c"], dtype=np.float32))
    B = x.shape[0]
    sh = _prep_shared(inputs)
    if "nc" not in _CACHE:
        _CACHE["nc"] = build_program()
    nc = _CACHE["nc"]
    in_maps = []
    for b in range(B):
        m = dict(sh)
        m["x"] = x[b]
        m["cb"] = np.ascontiguousarray(c[b].reshape(8, 128))
        in_maps.append(m)
    res = run_bass_kernel_spmd(nc, in_maps, core_ids=list(range(B)))
    return np.stack([np.asarray(r["out"], dtype=np.float32) for r in res.results], axis=0)
```

```python
import numpy as np
from contextlib import ExitStack
import concourse.bass as bass
import concourse.mybir as mybir
from concourse.bass_utils import run_bass_kernel_spmd

F32 = mybir.dt.float32
BF16 = mybir.dt.bfloat16
I32 = mybir.dt.int32
AF = mybir.ActivationFunctionType
ALU = mybir.AluOpType

NDMASEM = 8
L = 2048
D = 1024
DEPTH = 2
EPS = 1e-6
import os as _os
DBGL = int(_os.environ.get("DBGL", "0"))
MAGIC = 12582912.0
TWO_PI = float(2 * np.pi)
PVROWS = 160
R_ADAB, R_N1G, R_PSC, R_SSD, R_GLUB, R_CW, R_CB, R_BA, R_BX, R_LAM, R_BG, R_N2G, R_FG = 0, 48, 56, 60, 64, 68, 84, 88, 96, 104, 112, 144, 152


class Sched:
    ENG = ("pe", "act", "dve", "pool", "sp")

    def __init__(self, nc, stack):
        self.nc = nc
        self.prog = {e: [] for e in self.ENG}
        self.count = {e: 0 for e in self.ENG}
        self.sem = {}
        for e in ("pe", "act", "dve", "pool"):
            self.sem[e] = stack.enter_context(nc.semaphore("s_" + e))
        self.dq = {}
        for q in ("sp", "pool"):
            for i in range(NDMASEM):
                self.sem[(q, i)] = stack.enter_context(nc.semaphore(f"d_{q}{i}"))
            self.dq[q] = 0
        self.seen = {e: {} for e in self.ENG}
        self.lastw = {}
        self.readers = {}
        self.ninstr = 0

    def _need(self, eng, tickets):
        waits = {}
        for t in tickets:
            if t is None:
                continue
            k, v = t
            if self.seen[eng].get(k, 0) >= v:
                continue
            if waits.get(k, 0) < v:
                waits[k] = v
        for k, v in waits.items():
            self.seen[eng][k] = v
        return list(waits.items())

    def _deps(self, reads, writes):
        ts = []
        for r in reads:
            ts.append(self.lastw.get(r))
        for w in writes:
            ts.append(self.lastw.get(w))
            ts.extend(self.readers.get(w, ()))
        return ts

    def _commit(self, ticket, reads, writes):
        for r in reads:
            self.readers.setdefault(r, []).append(ticket)
        for w in writes:
            self.lastw[w] = ticket
            self.readers[w] = []

    def op(self, eng, fn, reads=(), writes=(), inc=True):
        deps = self._deps(reads, writes)
        if eng == "pe":
            deps = [t for t in deps if t is not None and t[0] != "pe"]
        waits = self._need(eng, deps)
        if inc:
            self.count[eng] += 1
            ticket = (eng, self.count[eng])
        else:
            ticket = (eng, self.count[eng] + 1)
        self.prog[eng].append((fn, waits, self.sem[eng] if inc else None, 1))
        self._commit(ticket, reads, writes)
        self.ninstr += 1
        return ticket

    def dma(self, q, fn, reads=(), writes=()):
        i = self.dq[q]
        self.dq[q] += 1
        slot, rnd = i % NDMASEM, i // NDMASEM
        key = (q, slot)
        deps = self._deps(reads, writes)
        if rnd > 0:
            deps.append((key, 16 * rnd))
        waits = self._need(q, deps)
        ticket = (key, 16 * (rnd + 1))
        self.prog[q].append((fn, waits, self.sem[key], 16))
        self._commit(ticket, reads, writes)
        self.ninstr += 1
        return ticket

    def all_tickets(self):
        ts = [(e, self.count[e]) for e in ("pe", "act", "dve", "pool") if self.count[e] > 0]
        for q in ("sp", "pool"):
            n = self.dq[q]
            for slot in range(NDMASEM):
                cnt = (n - slot + NDMASEM - 1) // NDMASEM if n > slot else 0
                if cnt > 0:
                    ts.append(((q, slot), 16 * cnt))
        return ts

    def barrier(self):
        ts = self.all_tickets()
        for e in self.ENG:
            waits = self._need(e, ts)
            if waits:
                self.prog[e].append((None, waits, None, 0))
        self.lastw = {}
        self.readers = {}

    def emit(self, block):
        def run(name):
            def body(e):
                for fn, waits, sem, n in self.prog[name]:
                    for k, v in waits:
                        e.wait_ge(self.sem[k], v)
                    if fn is None:
                        continue
                    ins = fn(e)
                    if sem is not None:
                        ins.then_inc(sem, n)
            return body

        block.tensor(run("pe"))
        block.scalar(run("act"))
        block.vector(run("dve"))
        block.gpsimd(run("pool"))
        block.sync(run("sp"))


class Arena:
    def __init__(self, ap, words):
        self.ap = ap
        self.words = words
        self.top = 0
        self.marks = []
        self.peak = 0

    def _take(self, words):
        a = self.ap[:, self.top:self.top + words]
        self.top += words
        self.peak = max(self.peak, self.top)
        assert self.top <= self.words, f"arena overflow {self.top} > {self.words}"
        return a

    def f32(self, cols):
        return self._take(cols)

    def i32(self, cols):
        return self._take(cols).bitcast(I32)

    def bf16(self, cols):
        w = (cols + 1) // 2
        return self._take(w).bitcast(BF16)[:, :cols]

    def push(self):
        self.marks.append(self.top)

    def pop(self):
        self.top = self.marks.pop()


class _Stop(Exception):
    pass


def build_program(dbg=None, stage=99):
    nc = bass.Bass("TRN2", target_bir_lowering=False)
    dram_in = lambda name, shape: nc.dram_tensor(name, list(shape), F32, kind="ExternalInput").ap()
    x_d = dram_in("x", [L, D])
    cb_d = dram_in("cb", [8, 128])
    pm_d = dram_in("pm", [128, 20 * 128])
    Wd = []
    for l in range(DEPTH):
        Wd.append(dict(
            pv=dram_in(f"pv{l}", [PVROWS, 128]),
            adaw=dram_in(f"adaw{l}", [D, 6 * D]),
            win=dram_in(f"win{l}", [D, 3072]),
            poolw=dram_in(f"poolw{l}", [4, 128, 128]),
            ssc=dram_in(f"ssc{l}", [128, 96]),
            ssb=dram_in(f"ssb{l}", [128, 2 * 16 * 16]),
            sscc=dram_in(f"sscc{l}", [128, 2 * 2 * 16 * 16]),
            gluw=dram_in(f"gluw{l}", [512, 512]),
            lrubd=dram_in(f"lrubd{l}", [128, 16 * 128]),
            gng=dram_in(f"gng{l}", [128, 512]),
            gws=dram_in(f"gws{l}", [128, 4 * 128]),
            gbs=dram_in(f"gbs{l}", [1, 512]),
            wbr=dram_in(f"wbr{l}", [4, 512, D]),
            wgate=dram_in(f"wgate{l}", [D, 4 * D]),
            wout=dram_in(f"wout{l}", [D, D]),
            wff1=dram_in(f"wff1{l}", [D, 4 * D]),
            wff2=dram_in(f"wff2{l}", [4 * D, D]),
        ))
    out_d = nc.dram_tensor("out", [L, D], F32, kind="ExternalOutput").ap()
    y_d = nc.dram_tensor("yscr", [4, 128, 4, L], BF16, kind="Internal").ap()
    dbg_d = {}
    if dbg:
        for name, shape in dbg.items():
            dbg_d[name] = nc.dram_tensor("dbg_" + name, list(shape), F32, kind="ExternalOutput").ap()

    with ExitStack() as st:
        S = Sched(nc, st)
        AW = 48600
        arena_t = st.enter_context(nc.sbuf_tensor("arena", [128, AW], F32))
        A = Arena(arena_t[:], AW)
        banks = [st.enter_context(nc.psum_tensor(f"ps{i}", [128, 512], F32)) for i in range(8)]
        block = st.enter_context(nc.Block())
        bank_state = {"i": 0, "free": list(range(8))}

        def PS():
            fr = bank_state["free"]
            b = fr[bank_state["i"] % len(fr)]
            bank_state["i"] += 1
            return banks[b][:], ("ps", b)

        def MM(out, lhsT, rhs, start, stop, R, W, inc):
            return S.op("pe", lambda e: e.matmul(out, lhsT=lhsT, rhs=rhs, start=start, stop=stop), R, W, inc)

        def TR(out, in_, ident, R, W):
            return S.op("pe", lambda e: e.transpose(out=out, in_=in_, identity=ident), R, W, True)

        def ACT(out, in_, func, R, W, bias=None, scale=None, accum=None):
            kw = {}
            if bias is not None:
                kw["bias"] = bias
            if scale is not None:
                kw["scale"] = scale
            if accum is not None:
                kw["accum_out"] = accum
            return S.op("act", lambda e: e.activation(out=out, in_=in_, func=func, **kw), R, W)

        def TT(eng, out, a, b, op, R, W):
            return S.op(eng, lambda e: e.tensor_tensor(out=out, in0=a, in1=b, op=op), R, W)

        def TS(eng, out, a, s1, s2, op0, op1, R, W):
            if s2 is None:
                return S.op(eng, lambda e: e.tensor_scalar(out=out, in0=a, scalar1=s1, scalar2=None, op0=op0), R, W)
            return S.op(eng, lambda e: e.tensor_scalar(out=out, in0=a, scalar1=s1, scalar2=s2, op0=op0, op1=op1), R, W)

        def STT(eng, out, a, s, b, op0, op1, R, W):
            eng = "dve"
            return S.op(eng, lambda e: e.scalar_tensor_tensor(out=out, in0=a, scalar=s, in1=b, op0=op0, op1=op1), R, W)

        def CP(eng, out, in_, R, W):
            if eng == "act":
                return S.op("act", lambda e: e.activation(out=out, in_=in_, func=AF.Copy), R, W)
            return S.op(eng, lambda e: e.tensor_copy(out=out, in_=in_), R, W)

        def MSET(eng, out, val, W):
            return S.op(eng, lambda e: e.memset(out, val), (), W)

        def SCAN(out, d0, d1, init, R, W):
            return S.op("dve", lambda e: e.tensor_tensor_scan(out=out, data0=d0, data1=d1, initial=init, op0=ALU.mult, op1=ALU.add), R, W)

        def RECIP(out, in_, R, W):
            return S.op("dve", lambda e: e.reciprocal(out=out, in_=in_), R, W)

        def DMA(q, out, in_, R, W):
            return S.dma(q, lambda e: e.dma_start(out=out, in_=in_), R, W)

        def DBG(name, src_ap, R):
            if name in dbg_d:
                DMA("pool", dbg_d[name].rearrange("p (c t) -> p c t", c=src_ap.shape[1]) if len(src_ap.shape) == 3 else dbg_d[name], src_ap, R, [("dbgout", name)])

        xT = A.f32(8 * L).rearrange("p (c t) -> p c t", c=8)
        ident = A.f32(128)
        onesb = A.bf16(128)
        pcol = [A.f32(PVROWS) for _ in range(DEPTH)]
        modt = [A.f32(48) for _ in range(DEPTH)]
        modA = [A.f32(16) for _ in range(DEPTH)]
        cond = A.f32(8)
        epsc = A.f32(1)
        halfpi = A.f32(1)
        onec = A.f32(1)
        magicc = A.f32(1)

        MSET("pool", ident, 1.0, ["ident"])
        S.op("pool", lambda e: e.affine_select(out=ident, in_=ident, pattern=[[-1, 128]], compare_op=ALU.is_equal,
                                                fill=0.0, base=0, channel_multiplier=1), ["ident"], ["ident"])
        MSET("pool", onesb, 1.0, ["onesb"])
        MSET("pool", epsc, EPS, ["epsc"])
        MSET("pool", halfpi, float(np.pi / 2), ["halfpi"])
        MSET("pool", onec, 1.0, ["onec"])
        MSET("pool", magicc, MAGIC, ["magicc"])

        A.push()
        for l in range(DEPTH):
            pvt = A.f32(2 * 128).rearrange("p (a b) -> p a b", a=2)
            DMA("sp", pvt[:, 0, :], Wd[l]["pv"][0:128, :], [], [("pvt", l)])
            DMA("sp", pvt[0:32, 1, :], Wd[l]["pv"][128:160, :], [], [("pvt", l)])
            ps, pk = PS()
            TR(ps[:, 0:128], pvt[:, 0, :], ident, [("pvt", l), "ident"], [pk])
            TR(ps[:, 128:160], pvt[0:32, 1, :], ident[0:32, 0:32], [("pvt", l), "ident"], [pk])
            CP("dve", pcol[l], ps[:, 0:PVROWS], [pk], [("pcol", l)])
        cbt = A.f32(128)
        DMA("sp", cbt[0:8, :], cb_d, [], ["cbt"])
        ps, pk = PS()
        TR(ps[:, 0:8], cbt[0:8, :], ident[0:8, 0:8], ["cbt", "ident"], [pk])
        ACT(cond, ps[:, 0:8], AF.Silu, [pk], ["cond"])
        def mod_piece_dma(l, kc, part, nparts, buf, bkey):
            Wc = 6144 // nparts
            c0 = part * Wc
            DMA("sp", buf[:, 0:Wc], Wd[l]["adaw"][kc * 128:(kc + 1) * 128, c0:c0 + Wc], [], [bkey])

        def mod_piece(l, kc, part, nparts, buf, bkey, dma=True, defer_add=False):
            Wc = 6144 // nparts
            nj = Wc // 128
            j0 = part * nj
            if dma:
                mod_piece_dma(l, kc, part, nparts, buf, bkey)
            ps, pk = PS()
            for j in range(nj):
                MM(ps[:, j:j + 1], buf[:, j * 128:(j + 1) * 128], cond[:, kc:kc + 1], True, True, [bkey, "cond"], [pk], j == nj - 1)
            dst = modt[l][:, j0:j0 + nj]
            mk = ("modp", l, part)

            def do_add():
                if kc == 0:
                    TT("dve", dst, ps[:, 0:nj], pcol[l][:, R_ADAB + j0:R_ADAB + j0 + nj], ALU.add, [pk, ("pcol", l)], [mk])
                else:
                    TT("dve", dst, ps[:, 0:nj], dst, ALU.add, [pk, mk], [mk])
            if defer_add:
                return do_add
            do_add()

        def mod_finish(l, nparts):
            mks = [("modp", l, p) for p in range(nparts)]
            STT("dve", modA[l][:, 0:8], modt[l][:, 8:16], 1.0, pcol[l][:, R_N1G:R_N1G + 8], ALU.add, ALU.mult,
                mks + [("pcol", l)], [("modA", l), ("mod", l)])
            STT("dve", modA[l][:, 8:16], modt[l][:, 32:40], 1.0, pcol[l][:, R_N2G:R_N2G + 8], ALU.add, ALU.mult,
                mks + [("pcol", l)], [("modA", l), ("mod", l)])

        awb = [A.f32(3072), A.f32(3072)]
        for kc in range(8):
            for part in range(2):
                bi = (kc * 2 + part) % 2
                mod_piece(0, kc, part, 2, awb[bi], ("aw", bi))
        mod_finish(0, 2)
        xtok = [A.f32(D), A.f32(D)]
        for n in range(16):
            xt = xtok[n % 2]
            DMA("sp", xt, x_d[n * 128:(n + 1) * 128, :], [], [("xtok", n % 2)])
            for cg in range(2):
                ps, pk = PS()
                for c4 in range(4):
                    c = cg * 4 + c4
                    TR(ps[:, c4 * 128:(c4 + 1) * 128], xt[:, c * 128:(c + 1) * 128], ident, [("xtok", n % 2), "ident"], [pk])
                eng = "dve" if cg == 0 else "act"
                CP(eng, xT[:, cg * 4:(cg + 1) * 4, n * 128:(n + 1) * 128], ps.rearrange("p (c t) -> p c t", c=4), [pk],
                   [("x", c, n // 4) for c in range(cg * 4, cg * 4 + 4)])
        S.barrier()
        A.pop()

        def rms_mod(l, which, tts, hT, hkey):
            A.push()
            sq = [A.bf16(8 * 512).rearrange("p (c t) -> p c t", c=8) for _ in range(2)]
            rt = [A.f32(512) for _ in range(2)]
            tmps = [A.f32(512) for _ in range(3)] + [tmp2]
            shc = 0 if which == 0 else 24
            tts = list(tts)
            pend = {}

            def stats_a(i):
                tt = tts[i]
                b = i % 2
                sl = slice(tt * 512, (tt + 1) * 512)
                ACT(sq[b], xT[:, :, sl], AF.Square, [("x", c, tt) for c in range(8)], [("sq", b)])
                ps, pk = PS()
                for c in range(8):
                    MM(ps, onesb, sq[b][:, c, :], c == 0, c == 7, [("sq", b), "onesb"], [pk], c == 7)
                pend[i] = (ps, pk)

            def stats_b(i):
                b = i % 2
                ps, pk = pend.pop(i)
                ACT(rt[b], ps, AF.Sqrt, [pk, "epsc"], [("rt", b)], bias=epsc[:, 0:1], scale=1.0 / D)
                RECIP(rt[b], rt[b], [("rt", b)], [("rt", b)])

            def apply(i):
                tt = tts[i]
                b = i % 2
                sl = slice(tt * 512, (tt + 1) * 512)
                for c in range(8):
                    tm = tmps[c % 4]
                    tk = ("tmp", c % 4)
                    STT("dve", tm, xT[:, c, sl], modA[l][:, which * 8 + c:which * 8 + c + 1], rt[b], ALU.mult, ALU.mult,
                        [("x", c, tt), ("rt", b), ("modA", l)], [tk])
                    dst = hT[:, c, sl] if hT.shape[2] == L else hT[:, c, (tt % 2) * 512:(tt % 2 + 1) * 512]
                    shcol = modt[l][:, shc + c:shc + c + 1]
                    if c % 2 == 0:
                        ACT(dst, tm, AF.Identity, [tk, ("mod", l)], [(hkey, c, tt)], bias=shcol)
                    else:
                        TS("dve", dst, tm, shcol, None, ALU.add, None, [tk, ("mod", l)], [(hkey, c, tt)])

            stats_a(0)
            stats_b(0)
            for i in range(len(tts)):
                if i + 1 < len(tts):
                    stats_a(i + 1)
                apply(i)
                if i + 1 < len(tts):
                    stats_b(i + 1)
            A.pop()

        tmp2 = A.f32(512)

        def load_w(q, dst, src, key):
            return DMA(q, dst, src, [], [key])

        def proj_fm(wt, wkey, ncol_chunks, hT, hkey, tts, sink):
            for j in range(ncol_chunks):
                for tt in tts:
                    ps, pk = PS()
                    for kc in range(8):
                        MM(ps, wt[:, kc, j * 128:(j + 1) * 128], hT[:, kc, tt * 512:(tt + 1) * 512], kc == 0, kc == 7,
                           [wkey, (hkey, kc, tt)], [pk], kc == 7)
                    sink(j, tt, ps, pk)

        persist_top = A.top

        def checkpoint(k):
            if stage == k:
                raise _Stop()
        try:
          checkpoint(0)
          for l in range(DEPTH):
                W = Wd[l]
                pc = pcol[l]
                pck = ("pcol", l)
                A.push()
                A.push()
                zsT = A.bf16(4 * L).rearrange("p (c t) -> p c t", c=4)
                A.push()
                hT = A.bf16(8 * L).rearrange("p (c t) -> p c t", c=8)
                wss = A.bf16(8 * 512).rearrange("p (k n) -> p k n", k=8)
                DMA("pool", wss, W["win"][:, 512:1024].rearrange("(k p) n -> p k n", p=128), [], ["wss"])
                rms_mod(l, 0, range(4), hT, "h")
                if l == DBGL:
                    DBG("h", hT, [("h", c, tt) for c in range(8) for tt in range(4)])

                def sink_zs(j, tt, ps, pk):
                    CP("act" if (j + tt) % 2 == 0 else "dve", zsT[:, j, tt * 512:(tt + 1) * 512], ps, [pk], [("zs", j, tt)])
                proj_fm(wss, "wss", 4, hT, "h", range(4), sink_zs)
                if l == DBGL:
                    DBG("zs", zsT, [("zs", c, tt) for c in range(4) for tt in range(4)])
                S.barrier()
                A.pop()
                checkpoint(1)
                ssc = A.f32(96)
                DMA("sp", ssc, W["ssc"], [], ["ssc"])
                ssb = A.f32(512).rearrange("p (r q h) -> p r q h", r=2, q=16)
                DMA("sp", ssb, W["ssb"].rearrange("p (r q h) -> p r q h", r=2, q=16), [], ["ssb"])
                scc = A.f32(1024).rearrange("p (r d q j) -> p r d q j", r=2, d=2, q=16)
                DMA("sp", scc, W["sscc"].rearrange("p (r d q j) -> p r d q j", r=2, d=2, q=16), [], ["scc"])
                sp_ = A.f32(32 * 12)
                col = lambda i: sp_[:, i * 32:(i + 1) * 32]
                dt, mag, th, t1, t2, cs, sn, abr, abi, fre, fim, phi = [col(i) for i in range(12)]
                lr, li, ldt = ssc[:, 0:32], ssc[:, 32:64], ssc[:, 64:96]
                K = ["sprep"]
                ACT(dt, ldt, AF.Exp, ["ssc"], K)
                TT("dve", t1, lr, dt, ALU.mult, ["ssc"] + K, K)
                ACT(mag, t1, AF.Exp, K, K)
                TT("dve", th, li, dt, ALU.mult, ["ssc"] + K, K)
                TS("dve", t1, th, 1.0 / TWO_PI, MAGIC, ALU.mult, ALU.add, K, K)
                TS("dve", t2, th, 1.0 / TWO_PI, None, ALU.mult, None, K, K)
                STT("dve", phi, t1, MAGIC, t2, ALU.subtract, ALU.subtract, K, K)
                TS("dve", phi, phi, -1.0, None, ALU.mult, None, K, K)
                ACT(sn, phi, AF.Sin, K, K, scale=TWO_PI)
                ACT(t1, phi, AF.Abs, K, K)
                ACT(cs, t1, AF.Sin, K + ["halfpi"], K, scale=-TWO_PI, bias=halfpi[:, 0:1])
                TT("dve", abr, mag, cs, ALU.mult, K, K)
                TT("dve", abi, mag, sn, ALU.mult, K, K)
                TT("dve", t1, lr, lr, ALU.mult, ["ssc"] + K, K)
                TT("dve", t2, li, li, ALU.mult, ["ssc"] + K, K)
                TT("dve", t1, t1, t2, ALU.add, K, K)
                RECIP(t1, t1, K, K)
                TS("dve", abr, abr, -1.0, None, ALU.add, None, K, K)
                TT("dve", fre, abr, lr, ALU.mult, K, K)
                TT("dve", t2, abi, li, ALU.mult, K, K)
                TT("dve", fre, fre, t2, ALU.add, K, K)
                TT("dve", fre, fre, t1, ALU.mult, K, K)
                TT("dve", fim, abi, lr, ALU.mult, K, K)
                TT("dve", t2, abr, li, ALU.mult, K, K)
                TT("dve", fim, fim, t2, ALU.subtract, K, K)
                TT("dve", fim, fim, t1, ALU.mult, K, K)
                bbar = A.f32(2 * 2 * 256).rearrange("p (d r q h) -> p d r q h", d=2, r=2, q=16)
                bt1 = A.f32(256).rearrange("p (q h) -> p q h", q=16)
                bt2 = A.f32(256).rearrange("p (q h) -> p q h", q=16)
                for d_ in range(2):
                    frb = fre[:, d_ * 16:(d_ + 1) * 16].unsqueeze(2).to_broadcast([128, 16, 16])
                    fib = fim[:, d_ * 16:(d_ + 1) * 16].unsqueeze(2).to_broadcast([128, 16, 16])
                    TT("dve", bt1, ssb[:, 0], frb, ALU.mult, ["ssb"] + K, ["bt1"])
                    TT("dve", bt2, ssb[:, 1], fib, ALU.mult, ["ssb"] + K, ["bt2"])
                    TT("dve", bbar[:, d_, 0], bt1, bt2, ALU.subtract, ["bt1", "bt2"], ["bbar"])
                    TT("dve", bt1, ssb[:, 1], frb, ALU.mult, ["ssb"] + K, ["bt1"])
                    TT("dve", bt2, ssb[:, 0], fib, ALU.mult, ["ssb"] + K, ["bt2"])
                    TT("dve", bbar[:, d_, 1], bt1, bt2, ALU.add, ["bt1", "bt2"], ["bbar"])
                Z = A.f32(4 * 128).rearrange("p (i c) -> p i c", i=4)
                BT = A.bf16(2 * 2 * 2 * 4 * 128).rearrange("p (s d r q c) -> p s d r q c", s=2, d=2, r=2, q=4)
                Cpad = A.bf16(2 * 2 * 3 * 4 * 128).rearrange("p (s d v q c) -> p s d v q c", s=2, d=2, v=3, q=4)
                MSET("pool", Z, 0.0, ["Z"])
                MSET("pool", Cpad, 0.0, [("Cpad", 0), ("Cpad", 1)])

                def bm_copy(blk, g):
                    d_, r_ = g // 2, g % 2
                    for qq in range(4):
                        q = blk * 4 + qq
                        for half in range(2):
                            c0 = 16 * (2 * qq + half)
                            CP("pool", Z[half * 64:(half + 1) * 64, qq, c0:c0 + 16], bbar[half * 64:(half + 1) * 64, d_, r_, q, :],
                               ["bbar"], ["Z"])

                def bm_tr(blk, g):
                    d_, r_ = g // 2, g % 2
                    st_ = blk % 2
                    ps, pk = PS()
                    for qq in range(4):
                        TR(ps[:, qq * 128:(qq + 1) * 128], Z[:, qq, :], ident, ["Z", "ident"], [pk])
                    CP("act", BT[:, st_, d_, r_, :, :], ps.rearrange("p (q c) -> p q c", q=4), [pk], [("BT", st_)])

                def bm_cpad(blk, d_):
                    st_ = blk % 2
                    for v in range(3):
                        r_ = 0 if v < 2 else 1
                        sgn = 1.0 if v == 0 else -1.0
                        for qq in range(4):
                            q = blk * 4 + qq
                            for half in range(2):
                                c0 = 16 * (2 * qq + half)
                                hs = slice(half * 64, (half + 1) * 64)
                                TS("pool", Cpad[hs, st_, d_, v, qq, c0:c0 + 16], scc[hs, r_, d_, q, :], sgn, None, ALU.mult, None,
                                   ["scc"], [("Cpad", st_)])

                def build_block_mats(blk):
                    for g in range(4):
                        bm_copy(blk, g)
                        bm_tr(blk, g)
                    for d_ in range(2):
                        bm_cpad(blk, d_)
                wk = A.f32(L)
                iot_i = wk.bitcast(I32)
                iot = A.f32(L)
                S.op("pool", lambda e: e.iota(iot_i, pattern=[[1, L]], base=0, channel_multiplier=0), [], ["ioti"])
                CP("dve", iot, iot_i, ["ioti"], ["iot"])
                carr = A.f32(2 * 4 * 2).rearrange("p (d q r) -> p d q r", d=2, q=4)
                tA = [wk[:, 0:512], wk[:, 1024:1536]]
                tB = [wk[:, 512:1024], wk[:, 1536:2048]]
                m2 = [A.bf16(512) for _ in range(2)]
                m4 = [A.bf16(512) for _ in range(2)]
                sinT = [A.bf16(512) for _ in range(4)]
                cosT = [A.bf16(512) for _ in range(4)]
                xre = [A.bf16(512) for _ in range(2)]
                xim = [A.bf16(512) for _ in range(2)]
                m1 = [A.bf16(512) for _ in range(2)]
                m3 = [A.bf16(512) for _ in range(2)]
                Gre = [A.bf16(512) for _ in range(2)]
                Gim = [A.bf16(512) for _ in range(2)]
                Pp = [[A.bf16(512) for _ in range(4)] for _ in range(2)]
                ygt = A.f32(512)
                ygT = zsT
                ybst = [A.bf16(512), A.bf16(512)]
                mod_next = l + 1 < DEPTH
                if mod_next:
                    awq = [A.f32(1536), A.f32(1536)]
                    mod_jobs = [(kc, part) for kc in range(8) for part in range(4)]
                mod_di = 0
                mod_ci = 0
                mod_adds = []
                it = 0
                for blk in range(4):
                    if blk == 0:
                        build_block_mats(0)
                    st_ = blk % 2
                    ybanks = [0, 1, 2, 3]
                    bank_state["free"] = [4, 5, 6, 7]
                    ykeys = [("ps", b) for b in ybanks]
                    iters = []
                    for d_ in range(2):
                        for step in range(4):
                            tt = step if d_ == 0 else 3 - step
                            for qq in range(4):
                                iters.append((d_, step, tt, qq, it % 2, it % 4))
                                it += 1

                    def stage_a(d_, step, tt, qq, b, b4):
                        k4 = lambda s_: (s_, b4)
                        sl = slice(tt * 512, (tt + 1) * 512)
                        q = blk * 4 + qq
                        kb = lambda s_: (s_, b)
                        phc = phi[:, d_ * 16 + q:d_ * 16 + q + 1]
                        ACT(tA[b], iot[:, sl], AF.Identity, ["iot", "magicc"] + K, [kb("tA")], scale=phc, bias=magicc[:, 0:1])
                        ACT(tB[b], iot[:, sl], AF.Copy, ["iot"] + K, [kb("tB")], scale=phc)
                        STT("dve", tA[b], tA[b], MAGIC, tB[b], ALU.subtract, ALU.subtract, [kb("tA"), kb("tB")], [kb("tA")])
                        ACT(sinT[b4], tA[b], AF.Sin, [kb("tA")], [k4("sinT")], scale=(-TWO_PI if d_ == 0 else TWO_PI))
                        ACT(tB[b], tA[b], AF.Abs, [kb("tA")], [kb("tB")])
                        ACT(cosT[b4], tB[b], AF.Sin, [kb("tB"), "halfpi"], [k4("cosT")], scale=-TWO_PI, bias=halfpi[:, 0:1])
                        psr, pkr = PS()
                        MM(psr, BT[:, st_, d_, 0, qq, :], zsT[:, blk, sl], True, True, [("BT", st_), ("zs", blk, tt)], [pkr], True)
                        psi, pki = PS()
                        MM(psi, BT[:, st_, d_, 1, qq, :], zsT[:, blk, sl], True, True, [("BT", st_), ("zs", blk, tt)], [pki], True)
                        CP("act", xre[b], psr, [pkr], [kb("xre")])
                        CP("act", xim[b], psi, [pki], [kb("xim")])

                    def stage_b1(d_, step, tt, qq, b, b4):
                        kb = lambda s_: (s_, b)
                        k4 = lambda s_: (s_, b4)
                        TT("dve", m1[b], cosT[b4], xre[b], ALU.mult, [k4("cosT"), kb("xre")], [kb("m1")])
                        TT("dve", m2[b], sinT[b4], xim[b], ALU.mult, [k4("sinT"), kb("xim")], [kb("m2")])
                        TT("dve", m3[b], cosT[b4], xim[b], ALU.mult, [k4("cosT"), kb("xim")], [kb("m3")])
                        TT("dve", m4[b], sinT[b4], xre[b], ALU.mult, [k4("sinT"), kb("xre")], [kb("m4")])
                        TT("dve", m3[b], m3[b], m4[b], ALU.subtract, [kb("m3"), kb("m4")], [kb("m3")])
                        TT("dve", m1[b], m1[b], m2[b], ALU.add, [kb("m1"), kb("m2")], [kb("m1")])

                    def stage_b2(d_, step, tt, qq, b, b4):
                        q = blk * 4 + qq
                        kb = lambda s_: (s_, b)
                        rho = mag[:, d_ * 16 + q:d_ * 16 + q + 1].to_broadcast([128, 512])
                        ck = ("carr", d_, qq)
                        if d_ == 0:
                            o_re, i_re, o_im, i_im = Gre[b], m1[b], Gim[b], m3[b]
                            last = slice(511, 512)
                        else:
                            o_re, i_re, o_im, i_im = Gre[b][:, ::-1], m1[b][:, ::-1], Gim[b][:, ::-1], m3[b][:, ::-1]
                            last = slice(0, 1)
                        init_re = 0.0 if step == 0 else carr[:, d_, qq, 0:1]
                        init_im = 0.0 if step == 0 else carr[:, d_, qq, 1:2]
                        SCAN(o_im, rho, i_im, init_im, [kb("m3"), ck] + K, [kb("Gim")])
                        SCAN(o_re, rho, i_re, init_re, [kb("m1"), ck] + K, [kb("Gre")])
                        if step < 3:
                            CP("pool", carr[:, d_, qq, 0:1], Gre[b][:, last], [kb("Gre")], [ck])
                            CP("pool", carr[:, d_, qq, 1:2], Gim[b][:, last], [kb("Gim")], [ck])

                    def stage_b3(d_, step, tt, qq, b, b4):
                        kb = lambda s_: (s_, b)
                        k4 = lambda s_: (s_, b4)
                        TT("dve", Pp[b][0], cosT[b4], Gre[b], ALU.mult, [k4("cosT"), kb("Gre")], [kb("P0")])
                        TT("dve", Pp[b][1], sinT[b4], Gim[b], ALU.mult, [k4("sinT"), kb("Gim")], [kb("P1")])
                        TT("dve", Pp[b][2], sinT[b4], Gre[b], ALU.mult, [k4("sinT"), kb("Gre")], [kb("P2")])
                        TT("dve", Pp[b][3], cosT[b4], Gim[b], ALU.mult, [k4("cosT"), kb("Gim")], [kb("P3")])
                        yps = banks[ybanks[tt]][:]
                        first = (d_ == 0 and qq == 0)
                        lastm = (d_ == 1 and qq == 3)
                        vs = [0, 1, 2, 2]
                        for m in range(4):
                            MM(yps, Cpad[:, st_, d_, vs[m], qq, :], Pp[b][m], first and m == 0, lastm and m == 3,
                               [("Cpad", st_), kb("P%d" % m)], [ykeys[tt]], m == 3)

                    n_it = len(iters)
                    for t_ in range(-2, n_it + 1):
                        if blk < 3:
                            if t_ in (2, 6, 10, 14):
                                bm_copy(blk + 1, (t_ - 2) // 4)
                            if t_ in (18, 22):
                                bm_cpad(blk + 1, (t_ - 18) // 4)
                        if mod_next and t_ >= 0 and t_ % 4 == 1 and mod_di < len(mod_jobs):
                            kc_, part_ = mod_jobs[mod_di]
                            bi_ = mod_di % 2
                            mod_piece_dma(l + 1, kc_, part_, 4, awq[bi_], ("awq", bi_))
                            mod_di += 1
                        if mod_next and t_ >= 0 and t_ % 4 == 0 and mod_adds:
                            mod_adds.pop(0)()
                            if mod_ci == len(mod_jobs) and not mod_adds:
                                mod_finish(l + 1, 4)
                        if 0 <= t_ + 2 < n_it:
                            stage_a(*iters[t_ + 2])
                        if blk < 3 and t_ in (4, 8, 12, 16):
                            bm_tr(blk + 1, (t_ - 4) // 4)
                        if mod_next and t_ >= 0 and t_ % 4 == 3 and mod_ci < mod_di:
                            kc_, part_ = mod_jobs[mod_ci]
                            bi_ = mod_ci % 2
                            mod_adds.append(mod_piece(l + 1, kc_, part_, 4, awq[bi_], ("awq", bi_), dma=False, defer_add=True))
                            mod_ci += 1
                        if 0 <= t_ + 1 < n_it:
                            stage_b1(*iters[t_ + 1])
                        if 0 <= t_ < n_it:
                            stage_b2(*iters[t_])
                        if 0 <= t_ - 1 < n_it:
                            stage_b3(*iters[t_ - 1])
                    for tt in range(4):
                        sl = slice(tt * 512, (tt + 1) * 512)
                        STT("dve", ygt, zsT[:, blk, sl], pc[:, R_SSD + blk:R_SSD + blk + 1], banks[ybanks[tt]][:], ALU.mult, ALU.add,
                            [("zs", blk, tt), ykeys[tt], pck], ["ygt"])
                        ACT(ygT[:, blk, sl], ygt, AF.Gelu, ["ygt"], [("yg", blk, tt)])
                    bank_state["free"] = list(range(8))
                while mod_next and mod_adds:
                    mod_adds.pop(0)()
                    if mod_ci == len(mod_jobs) and not mod_adds:
                        mod_finish(l + 1, 4)
                wgl = A.bf16(4 * 512).rearrange("p (k n) -> p k n", k=4)
                DMA("pool", wgl, W["gluw"].rearrange("(k p) n -> p k n", p=128), [], ["wgl"])
                sg = ygt
                for ob in range(4):
                    for tt in range(4):
                        sl = slice(tt * 512, (tt + 1) * 512)
                        ps, pk = PS()
                        for kc in range(4):
                            MM(ps, wgl[:, kc, ob * 128:(ob + 1) * 128], ygT[:, kc, sl], kc == 0, kc == 3, ["wgl", ("yg", kc, tt)], [pk], kc == 3)
                        ACT(sg, ps, AF.Sigmoid, [pk, pck], ["ygt"], bias=pc[:, R_GLUB + ob:R_GLUB + ob + 1])
                        bb = (ob * 4 + tt) % 2
                        TT("dve", ybst[bb], sg, ygT[:, ob, sl], ALU.mult, ["ygt", ("yg", ob, tt)], [("ybst", bb)])
                        DMA("sp", y_d[1, :, ob, sl], ybst[bb], [("ybst", bb)], [("yD", 1, tt)])
                        if l == DBGL and "yb" in dbg_d:
                            DMA("pool", dbg_d["yb"][:, ob * L + tt * 512:ob * L + (tt + 1) * 512], ybst[bb], [("ybst", bb)], [("dbgo", "yb", ob, tt)])
                S.barrier()
                A.pop()
                checkpoint(2)
                hT = A.bf16(8 * L).rearrange("p (c t) -> p c t", c=8)
                A.push()
                wlr = A.bf16(8 * 1024).rearrange("p (k n) -> p k n", k=8)
                DMA("pool", wlr, W["win"][:, 1024:2048].rearrange("(k p) n -> p k n", p=128), [], ["wlr"])
                lbd = A.bf16(16 * 128).rearrange("p (i c) -> p i c", i=16)
                DMA("pool", lbd, W["lrubd"].rearrange("p (i c) -> p i c", i=16), [], ["lbd"])
                rms_mod(l, 0, range(4), hT, "h")
                S.barrier()
                cl = A.f32(8)
                lt1 = A.f32(8)
                lt2 = A.f32(8)
                lam = pc[:, R_LAM:R_LAM + 8]
                ACT(lt1, lam, AF.Abs, [pck], ["lt1"])
                ACT(lt1, lt1, AF.Exp, ["lt1"], ["lt1"], scale=-1.0)
                ACT(lt1, lt1, AF.Ln, ["lt1", "onec"], ["lt1"], bias=onec[:, 0:1])
                TS("dve", lt2, lam, -1.0, 0.0, ALU.mult, ALU.max, [pck], ["lt2"])
                TT("dve", cl, lt1, lt2, ALU.add, ["lt1", "lt2"], ["cl"])
                TS("dve", cl, cl, -8.0, None, ALU.mult, None, ["cl"], ["cl"])
                zrp = A.f32(L + 4)
                MSET("pool", zrp, 0.0, [("zrp", tt) for tt in range(4)])
                xc = A.f32(L)
                xcb = A.bf16(L)
                rr = A.f32(L)
                ii = A.f32(L)
                aa = A.f32(L)
                a2 = A.f32(L)
                hsum = A.f32(L)
                hh = rr
                gg = aa
                ycst = A.bf16(L)
                for j in range(4):
                    for tt in range(4):
                        ps, pk = PS()
                        for kc in range(8):
                            MM(ps, wlr[:, kc, j * 128:(j + 1) * 128], hT[:, kc, tt * 512:(tt + 1) * 512], kc == 0, kc == 7,
                               ["wlr", ("h", kc, tt)], [pk], kc == 7)
                        CP("act", zrp[:, 2 + tt * 512:2 + (tt + 1) * 512], ps, [pk], [("zrp", tt)])
                    zk = [("zrp", tt) for tt in range(4)]
                    cw = lambda k: pc[:, R_CW + k * 4 + j:R_CW + k * 4 + j + 1]
                    ACT(xc, zrp[:, 0:L], AF.Identity, zk + [pck], ["xc"], bias=pc[:, R_CB + j:R_CB + j + 1], scale=cw(0))
                    for k in range(1, 4):
                        STT("dve" if k != 2 else "pool", xc, zrp[:, k:k + L], cw(k), xc, ALU.mult, ALU.add, zk + ["xc", pck], ["xc"])
                    CP("act", xcb, xc, ["xc"], ["xcb"])
                    if l == DBGL and j == 0:
                        DBG("xc0", xc, ["xc"])
                        DBG("zrp0", zrp[:, 0:L], zk)
                    for d_ in range(2):
                        for (which, dst, dk, boff) in ((0, rr, "rr", R_BA), (1, ii, "ii", R_BX)):
                            for tt in range(4):
                                sl = slice(tt * 512, (tt + 1) * 512)
                                ps, pk = PS()
                                MM(ps, lbd[:, which * 8 + d_ * 4 + j, :], xcb[:, sl], True, True, ["lbd", "xcb"], [pk], True)
                                ACT(dst[:, sl], ps, AF.Sigmoid, [pk, pck], [dk], bias=pc[:, boff + d_ * 4 + j:boff + d_ * 4 + j + 1])
                        if l == DBGL and j == 0 and d_ == 0:
                            DBG("rr0", rr, ["rr"])
                            DBG("ii0", ii, ["ii"])
                            DBG("cl", cl, ["cl"])
                        ACT(aa, rr, AF.Exp, ["rr", "cl"], ["aa"], scale=cl[:, d_ * 4 + j:d_ * 4 + j + 1])
                        if l == DBGL and j == 0 and d_ == 0:
                            DBG("aa0", aa, ["aa"])
                        TT("dve", a2, aa, aa, ALU.mult, ["aa"], ["a2"])
                        ACT(a2, a2, AF.Sqrt, ["a2", "onec"], ["a2"], scale=-1.0, bias=onec[:, 0:1])
                        TT("dve", ii, ii, a2, ALU.mult, ["ii", "a2"], ["ii"])
                        TT("dve", ii, ii, xc, ALU.mult, ["ii", "xc"], ["ii"])
                        if d_ == 0:
                            SCAN(hsum, aa, ii, 0.0, ["aa", "ii"], ["hsum"])
                        else:
                            SCAN(hh[:, ::-1], aa[:, ::-1], ii[:, ::-1], 0.0, ["aa", "ii"], ["rr"])
                            TT("dve", hsum, hsum, hh, ALU.add, ["hsum", "rr"], ["hsum"])
                    for tt in range(4):
                        sl = slice(tt * 512, (tt + 1) * 512)
                        ps, pk = PS()
                        for kc in range(8):
                            MM(ps, wlr[:, kc, 512 + j * 128:512 + (j + 1) * 128], hT[:, kc, sl], kc == 0, kc == 7,
                               ["wlr", ("h", kc, tt)], [pk], kc == 7)
                        ACT(gg[:, sl], ps, AF.Gelu, [pk], ["aa"])
                    TT("dve", ycst, hsum, gg, ALU.mult, ["hsum", "aa"], ["ycst"])
                    DMA("sp", y_d[2, :, j, :], ycst, ["ycst"], [("yD", 2, tt) for tt in range(4)])
                    if l == DBGL and "yc" in dbg_d:
                        DMA("pool", dbg_d["yc"][:, j * L:(j + 1) * L], ycst, ["ycst"], [("dbgo", "yc", j)])
                S.barrier()
                A.pop()

                checkpoint(3)
                A.push()
                yast = [A.bf16(512), A.bf16(512)]
                ydst = [A.bf16(512).rearrange("p (g q) -> p g q", g=4), A.bf16(512).rearrange("p (g q) -> p g q", g=4)]
                wgm = A.bf16(8 * 1024).rearrange("p (k n) -> p k n", k=8)
                DMA("pool", wgm, W["win"][:, 2048:3072].rearrange("(k p) n -> p k n", p=128), [], ["wgm"])
                gng = A.f32(512)
                DMA("sp", gng, W["gng"], [], ["gng"])
                gws = A.bf16(4 * 128).rearrange("p (g q) -> p g q", g=4)
                DMA("pool", gws, W["gws"].rearrange("p (g q) -> p g q", g=4), [], ["gws"])
                gbs = A.bf16(512)
                DMA("pool", gbs[0:1, :], W["gbs"], [], ["gbs"])
                A.push()
                wpo = A.bf16(8 * 512).rearrange("p (k n) -> p k n", k=8)
                DMA("pool", wpo, W["win"][:, 0:512].rearrange("(k p) n -> p k n", p=128), [], ["wpo"])
                pmb = A.bf16(20 * 128).rearrange("p (g v t) -> p g v t", g=4, v=5)
                DMA("pool", pmb, pm_d.rearrange("p (g v t) -> p g v t", g=4, v=5), [], ["pmb"])
                pwb = A.bf16(4 * 128).rearrange("p (g d) -> p g d", g=4)
                DMA("pool", pwb, W["poolw"].rearrange("g c d -> c g d"), [], ["pwb"])
                utok = A.bf16(16 * 512).rearrange("p (n c) -> p n c", n=16)
                for n in range(16):
                    ps, pk = PS()
                    for kc in range(8):
                        MM(ps, hT[:, kc, n * 128:(n + 1) * 128], wpo[:, kc, :], kc == 0, kc == 7, ["wpo", ("h", kc, n // 4)], [pk], kc == 7)
                    CP("act" if n % 2 == 0 else "dve", utok[:, n, :], ps, [pk], [("utok", n)])
                    if l == DBGL and n == 1:
                        DBG("utok1", utok[:, n, :], [("utok", n)])
                pooled = A.bf16(L)
                for g in range(4):
                    for tt in range(4):
                        ps, pk = PS()
                        for n4 in range(4):
                            nt = tt * 4 + n4
                            srcs = [ns for ns in (nt - 1, nt, nt + 1) if 0 <= ns < 16]
                            for i, ns in enumerate(srcs):
                                if ns == nt - 1:
                                    v = 0
                                elif ns == nt + 1:
                                    v = 2
                                else:
                                    v = 3 if nt == 0 else (4 if nt == 15 else 1)
                                MM(ps[:, n4 * 128:(n4 + 1) * 128], utok[:, ns, g * 128:(g + 1) * 128], pmb[:, g, v, :], i == 0, i == len(srcs) - 1,
                                   [("utok", ns), "pmb"], [pk], (n4 == 3 and i == len(srcs) - 1))
                        CP("act", pooled[:, tt * 512:(tt + 1) * 512], ps, [pk], [("pooled", tt)])
                    if l == DBGL and g == 3:
                        DBG("pooled3", pooled, [("pooled", tt) for tt in range(4)])
                    for tt in range(4):
                        sl = slice(tt * 512, (tt + 1) * 512)
                        ps, pk = PS()
                        MM(ps, pwb[:, g, :], pooled[:, sl], True, True, ["pwb", ("pooled", tt)], [pk], True)
                        bb = (g * 4 + tt) % 2
                        ACT(yast[bb], ps, AF.Copy, [pk, pck], [("yast", bb)], scale=pc[:, R_PSC + g:R_PSC + g + 1])
                        DMA("sp", y_d[0, :, g, sl], yast[bb], [("yast", bb)], [("yD", 0, tt)])
                        if l == DBGL and "ya" in dbg_d:
                            DMA("pool", dbg_d["ya"][:, g * L + tt * 512:g * L + (tt + 1) * 512], yast[bb], [("yast", bb)], [("dbgo", "ya", g, tt)])
                S.barrier()
                A.pop()
                A.push()
                guT = A.bf16(4 * L).rearrange("p (c t) -> p c t", c=4)

                def sink_gu(j, tt, ps, pk):
                    ACT(guT[:, j, tt * 512:(tt + 1) * 512], ps, AF.Gelu, [pk], [("gu", j, tt)])
                proj_fm(wgm, "wgm", 4, hT, "h", range(4), sink_gu)
                gv = [A.f32(512), A.f32(512)]
                gsq = A.f32(512)
                ssq = [A.f32(1), A.f32(1)]
                vn = [A.bf16(512), A.bf16(512)]
                def gm_stage1(n):
                    b = n % 2
                    ps, pk = PS()
                    for kc in range(8):
                        MM(ps, hT[:, kc, n * 128:(n + 1) * 128], wgm[:, kc, 512:1024], kc == 0, kc == 7, ["wgm", ("h", kc, n // 4)], [pk], kc == 7)
                    ACT(gv[b], ps, AF.Gelu, [pk], [("gv", b)])
                    ACT(gsq, gv[b], AF.Square, [("gv", b)], ["gsq", ("ssq", b)], accum=ssq[b][:, 0:1])
                    ACT(ssq[b], ssq[b], AF.Sqrt, [("ssq", b), "epsc"], [("ssq", b)], bias=epsc[:, 0:1], scale=1.0 / 512)
                    RECIP(ssq[b], ssq[b], [("ssq", b)], [("ssq", b)])
                    STT("dve", vn[b], gv[b], ssq[b][:, 0:1], gng, ALU.mult, ALU.mult, [("gv", b), ("ssq", b), "gng"], [("vn", b)])
                    if l == DBGL and n == 1:
                        DBG("gv1", gv[b], [("gv", b)])
                        DBG("vn1", vn[b], [("vn", b)])
                        DBG("ssq1", ssq[b], [("ssq", b)])

                def gm_stage2(n):
                    b = n % 2
                    ps, pk = PS()
                    for g in range(4):
                        MM(ps[:, g * 128:(g + 1) * 128], vn[b][:, g * 128:(g + 1) * 128], gws[:, g, :], True, False, [("vn", b), "gws"], [pk], False)
                        MM(ps[:, g * 128:(g + 1) * 128], onesb[0:1, :], gbs[0:1, g * 128:(g + 1) * 128], False, True, ["onesb", "gbs"], [pk], g == 3)
                    TT("dve", ydst[b], ps.rearrange("p (g q) -> p g q", g=4), guT[:, :, n * 128:(n + 1) * 128], ALU.mult,
                       [pk] + [("gu", j, n // 4) for j in range(4)], [("ydst", b)])
                    DMA("sp", y_d[3, :, :, n * 128:(n + 1) * 128], ydst[b], [("ydst", b)], [("yD", 3, n // 4)])
                    if l == DBGL and "yd" in dbg_d:
                        DMA("pool", dbg_d["yd"].rearrange("p (g t) -> p g t", g=4)[:, :, n * 128:(n + 1) * 128], ydst[b], [("ydst", b)], [("dbgo", "yd", n)])

                gm_stage1(0)
                for n in range(16):
                    if n + 1 < 16:
                        gm_stage1(n + 1)
                    gm_stage2(n)
                A.pop()
                S.barrier()
                checkpoint(35)
                yh = [A.bf16(4 * 1024).rearrange("p (c t) -> p c t", c=4) for _ in range(4)]
                mrg = wgm
                wgb = [A.bf16(8 * 128).rearrange("p (k n) -> p k n", k=8) for _ in range(3)]
                wbb = [A.bf16(4 * 128).rearrange("p (k n) -> p k n", k=4) for _ in range(3)]
                wob = [A.bf16(8 * 128).rearrange("p (k n) -> p k n", k=8) for _ in range(2)]
                sgt = [gng, A.f32(512)]
                macc = [A.f32(512), A.f32(512)]
                mtmp = [A.f32(512), A.f32(512)]
                wi = 0
                yi = 0
                for hf in range(2):
                    for k in range(4):
                        DMA("sp", yh[k], y_d[k, :, :, hf * 1024:(hf + 1) * 1024], [("yD", k, hf * 2), ("yD", k, hf * 2 + 1)], [("yh", k)])
                    for dch in range(8):
                        for k in range(4):
                            b3 = wi % 3
                            wi += 1
                            DMA("pool", wgb[b3], W["wgate"][:, k * 1024 + dch * 128:k * 1024 + (dch + 1) * 128].rearrange("(kc p) n -> p kc n", p=128),
                                [], [("wgb", b3)])
                            DMA("pool", wbb[b3], W["wbr"][k, :, dch * 128:(dch + 1) * 128].rearrange("(kc p) n -> p kc n", p=128), [], [("wbb", b3)])
                            for t2 in range(2):
                                tt = hf * 2 + t2
                                sl = slice(tt * 512, (tt + 1) * 512)
                                ps, pk = PS()
                                for kc in range(8):
                                    MM(ps, wgb[b3][:, kc, :], hT[:, kc, sl], kc == 0, kc == 7, [("wgb", b3), ("h", kc, tt)], [pk], kc == 7)
                                ACT(sgt[t2], ps, AF.Sigmoid, [pk, pck], [("sgt", t2)], bias=pc[:, R_BG + k * 8 + dch:R_BG + k * 8 + dch + 1])
                                ps2, pk2 = PS()
                                for kc in range(4):
                                    MM(ps2, wbb[b3][:, kc, :], yh[k][:, kc, t2 * 512:(t2 + 1) * 512], kc == 0, kc == 3, [("wbb", b3), ("yh", k)], [pk2], kc == 3)
                                if k == 0:
                                    TT("dve", macc[t2], sgt[t2], ps2, ALU.mult, [("sgt", t2), pk2], [("macc", t2)])
                                elif k < 3:
                                    TT("dve", mtmp[t2], sgt[t2], ps2, ALU.mult, [("sgt", t2), pk2], [("mtmp", t2)])
                                    TT("dve", macc[t2], macc[t2], mtmp[t2], ALU.add, [("macc", t2), ("mtmp", t2)], [("macc", t2)])
                                else:
                                    TT("dve", mtmp[t2], sgt[t2], ps2, ALU.mult, [("sgt", t2), pk2], [("mtmp", t2)])
                                    TT("dve", mrg[:, dch, t2 * 512:(t2 + 1) * 512], macc[t2], mtmp[t2], ALU.add, [("macc", t2), ("mtmp", t2)],
                                       [("mrg", dch, t2)])
                    for do in range(8):
                        b2 = do % 2
                        DMA("pool", wob[b2], W["wout"][:, do * 128:(do + 1) * 128].rearrange("(kc p) n -> p kc n", p=128), [], [("wob", b2)])
                        for t2 in range(2):
                            tt = hf * 2 + t2
                            sl = slice(tt * 512, (tt + 1) * 512)
                            ps, pk = PS()
                            for kc in range(8):
                                MM(ps, wob[b2][:, kc, :], mrg[:, kc, t2 * 512:(t2 + 1) * 512], kc == 0, kc == 7, [("wob", b2), ("mrg", kc, t2)], [pk], kc == 7)
                            STT("dve", xT[:, do, sl], ps, modt[l][:, 16 + do:17 + do], xT[:, do, sl], ALU.mult, ALU.add,
                                [pk, ("mod", l), ("x", do, tt)], [("x", do, tt)])
                S.barrier()
                A.pop()
                A.pop()

                if l == DBGL:
                    DBG("x1", xT, [("x", c, tt) for c in range(8) for tt in range(4)])
                checkpoint(4)
                A.push()
                h2 = A.bf16(8 * 1024).rearrange("p (c t) -> p c t", c=8)
                w1b = [A.bf16(8 * 512).rearrange("p (k n) -> p k n", k=8) for _ in range(2)]
                w2b = [A.bf16(4 * 1024).rearrange("p (k n) -> p k n", k=4) for _ in range(2)]
                f1 = [A.bf16(4 * 1024).rearrange("p (c t) -> p c t", c=4) for _ in range(2)]
                fsq = [A.f32(512), A.f32(512)]
                fi = 0
                for hf in range(2):
                    rms_mod(l, 1, [hf * 2, hf * 2 + 1], h2, "h2")
                    for fg in range(8):
                        b = fi % 2
                        fi += 1
                        DMA("pool", w1b[b], W["wff1"][:, fg * 512:(fg + 1) * 512].rearrange("(kc p) n -> p kc n", p=128), [], [("w1b", b)])
                        DMA("pool", w2b[b], W["wff2"][fg * 512:(fg + 1) * 512, :].rearrange("(kc p) n -> p kc n", p=128), [], [("w2b", b)])
                        for fc in range(4):
                            for t2 in range(2):
                                tt = hf * 2 + t2
                                ps, pk = PS()
                                for kc in range(8):
                                    MM(ps, w1b[b][:, kc, fc * 128:(fc + 1) * 128], h2[:, kc, t2 * 512:(t2 + 1) * 512], kc == 0, kc == 7,
                                       [("w1b", b), ("h2", kc, tt)], [pk], kc == 7)
                                ACT(fsq[t2], ps, AF.Square, [pk], [("fsq", t2)])
                                STT("dve", f1[b][:, fc, t2 * 512:(t2 + 1) * 512], ps, 0.0, fsq[t2], ALU.is_gt, ALU.mult, [pk, ("fsq", t2)],
                                    [("f1", b, fc, t2)])
                        for do in range(8):
                            for t2 in range(2):
                                tt = hf * 2 + t2
                                sl = slice(tt * 512, (tt + 1) * 512)
                                ps, pk = PS()
                                for fc in range(4):
                                    MM(ps, w2b[b][:, fc, do * 128:(do + 1) * 128], f1[b][:, fc, t2 * 512:(t2 + 1) * 512], fc == 0, fc == 3,
                                       [("w2b", b), ("f1", b, fc, t2)], [pk], fc == 3)
                                STT("dve", xT[:, do, sl], ps, modt[l][:, 40 + do:41 + do], xT[:, do, sl], ALU.mult, ALU.add,
                                    [pk, ("mod", l), ("x", do, tt)], [("x", do, tt)])
                if l == DBGL:
                    DBG("x2", xT, [("x", c, tt) for c in range(8) for tt in range(4)])
                S.barrier()
                A.pop()
                checkpoint(5 + l)

        except _Stop:
            A.top = persist_top
            A.marks = []
            bank_state["free"] = list(range(8))
            S.barrier()
        A.push()
        sq = A.bf16(8 * 512).rearrange("p (c t) -> p c t", c=8)
        rt = A.f32(512)
        yn = A.f32(8 * 512).rearrange("p (c t) -> p c t", c=8)
        otok = [A.f32(D), A.f32(D)]
        pl = pcol[DEPTH - 1]
        outs = []
        for tt in range(4):
            sl = slice(tt * 512, (tt + 1) * 512)
            ACT(sq, xT[:, :, sl], AF.Square, [("x", c, tt) for c in range(8)], ["sq"])
            ps, pk = PS()
            for c in range(8):
                MM(ps, onesb, sq[:, c, :], c == 0, c == 7, ["sq", "onesb"], [pk], c == 7)
            ACT(rt, ps, AF.Sqrt, [pk, "epsc"], ["rt"], bias=epsc[:, 0:1], scale=1.0 / D)
            RECIP(rt, rt, ["rt"], ["rt"])
            for c in range(8):
                STT("dve" if c % 2 == 0 else "pool", yn[:, c, :], xT[:, c, sl], pl[:, R_FG + c:R_FG + c + 1], rt, ALU.mult, ALU.mult,
                    [("x", c, tt), "rt", ("pcol", DEPTH - 1)], [("yn", c)])
            for n4 in range(4):
                n = tt * 4 + n4
                ot = otok[n % 2]
                for cg in range(2):
                    ps, pk = PS()
                    for c4 in range(4):
                        c = cg * 4 + c4
                        TR(ps[:, c4 * 128:(c4 + 1) * 128], yn[:, c, n4 * 128:(n4 + 1) * 128], ident, [("yn", c), "ident"], [pk])
                    CP("act" if cg == 0 else "dve", ot[:, cg * 512:(cg + 1) * 512], ps, [pk], [("otok", n % 2, cg)])
                outs.append(DMA("sp", out_d[n * 128:(n + 1) * 128, :], ot, [("otok", n % 2, 0), ("otok", n % 2, 1)], [("out", n)]))
        A.pop()
        S.barrier()
        S.emit(block)
        print("instructions:", S.ninstr, "arena peak words:", A.peak)
    return nc


def _pool_mats():
    wins = (2, 4, 8, 16)
    Lm = 3 * 128
    out = np.zeros((128, 4, 5, 128), np.float32)
    for g, w in enumerate(wins):
        def blockfor(nt_kind):
            if nt_kind == "mid":
                Lv, t0 = 5 * 128, 2 * 128
            elif nt_kind == "first":
                Lv, t0 = 3 * 128, 0
            else:
                Lv, t0 = 3 * 128, 2 * 128
            Pm = np.zeros((Lv, Lv), np.float64)
            for t in range(Lv):
                lo = min(max(t - w // 2, 0), Lv)
                hi = min(max(t - w // 2 + w, 0), Lv)
                Pm[lo:hi, t] = 1.0 / (hi - lo)
                Pm[t, t] -= 1.0
            return Pm, t0
        Pm, t0 = blockfor("mid")
        out[:, g, 0, :] = Pm[t0 - 128:t0, t0:t0 + 128]
        out[:, g, 1, :] = Pm[t0:t0 + 128, t0:t0 + 128]
        out[:, g, 2, :] = Pm[t0 + 128:t0 + 256, t0:t0 + 128]
        Pm, t0 = blockfor("first")
        out[:, g, 3, :] = Pm[t0:t0 + 128, t0:t0 + 128]
        Pm, t0 = blockfor("last")
        out[:, g, 4, :] = Pm[t0:t0 + 128, t0:t0 + 128]
    return np.ascontiguousarray(out.reshape(128, 20 * 128))


def _prep_shared(inp):
    f = lambda a: np.ascontiguousarray(np.asarray(a, dtype=np.float32))
    sh = {"pm": _pool_mats()}
    for l in range(DEPTH):
        rows = [
            f(inp["ada_b"][l]).reshape(48, 128),
            f(inp["norm1_g"][l]).reshape(8, 128),
            f(inp["pool_scale"][l]).reshape(4, 128),
            f(inp["ssm_d"][l]).reshape(4, 128),
            f(inp["ssm_glu_b"][l]).reshape(4, 128),
            f(inp["lru_conv_w"][l]).reshape(16, 128),
            f(inp["lru_conv_b"][l]).reshape(4, 128),
            f(inp["lru_ba"][l]).reshape(8, 128),
            f(inp["lru_bx"][l]).reshape(8, 128),
            f(inp["lru_lam"][l]).reshape(8, 128),
            f(inp["b_gate"][l]).reshape(32, 128),
            f(inp["norm2_g"][l]).reshape(8, 128),
            f(inp["final_g"]).reshape(8, 128),
        ]
        sh[f"pv{l}"] = np.ascontiguousarray(np.concatenate(rows, axis=0))
        sh[f"adaw{l}"] = f(inp["ada_w"][l])
        sh[f"win{l}"] = f(inp["w_in"][l])
        sh[f"poolw{l}"] = f(inp["pool_w"][l])
        def sc(a):
            return a.reshape(2, 16, 2, 64).transpose(2, 3, 0, 1).reshape(128, 32)
        lre = f(inp["ssm_lam_re"][l])
        lim = f(inp["ssm_lam_im"][l])
        ldt = np.broadcast_to(f(inp["ssm_log_dt"][l])[:, :, None], (2, 32, 64))
        sh[f"ssc{l}"] = np.ascontiguousarray(np.concatenate([sc(lre), sc(lim), sc(ldt)], axis=1))
        def sb(a):
            return a.reshape(16, 2, 64, 16).transpose(1, 2, 0, 3).reshape(128, 16, 16)
        sh[f"ssb{l}"] = np.ascontiguousarray(np.stack([sb(f(inp["ssm_b_re"][l])), sb(f(inp["ssm_b_im"][l]))], axis=1).reshape(128, 512))
        def scm(a):
            return a.reshape(2, 16, 2, 16, 64).transpose(2, 4, 0, 1, 3).reshape(128, 2, 16, 16)
        sh[f"sscc{l}"] = np.ascontiguousarray(np.stack([scm(f(inp["ssm_c_re"][l])), scm(f(inp["ssm_c_im"][l]))], axis=1).reshape(128, 1024))
        sh[f"gluw{l}"] = f(inp["ssm_glu_w"][l])
        bd = np.zeros((128, 16, 128), np.float32)
        for which, nm in enumerate(("lru_wa", "lru_wx")):
            w = f(inp[nm][l])
            for d_ in range(2):
                for j in range(4):
                    i = which * 8 + d_ * 4 + j
                    bd[0:64, i, 0:64] = w[d_, 2 * j]
                    bd[64:128, i, 64:128] = w[d_, 2 * j + 1]
        sh[f"lrubd{l}"] = np.ascontiguousarray(bd.reshape(128, 16 * 128))
        sh[f"gng{l}"] = np.ascontiguousarray(np.broadcast_to(f(inp["gmlp_norm_g"][l])[None, :], (128, 512)))
        sh[f"gws{l}"] = np.ascontiguousarray(f(inp["gmlp_ws"][l]).transpose(2, 0, 1).reshape(128, 512))
        sh[f"gbs{l}"] = f(inp["gmlp_bs"][l]).reshape(1, 512)
        sh[f"wbr{l}"] = f(inp["w_branch"][l])
        sh[f"wgate{l}"] = f(inp["w_gate"][l])
        sh[f"wout{l}"] = f(inp["w_out"][l])
        sh[f"wff1{l}"] = f(inp["w_ff1"][l])
        sh[f"wff2{l}"] = f(inp["w_ff2"][l])
    return sh


_CACHE = {}


def kernel(**inputs):
    x = np.ascontiguousarray(np.asarray(inputs["x"], dtype=np.float32))
    c = np.ascontiguousarray(np.asarray(inputs["c"], dtype=np.float32))
    B = x.shape[0]
    sh = _prep_shared(inputs)
    if "nc" not in _CACHE:
        _CACHE["nc"] = build_program()
    nc = _CACHE["nc"]
    in_maps = []
    for b in range(B):
        m = dict(sh)
        m["x"] = x[b]
        m["cb"] = np.ascontiguousarray(c[b].reshape(8, 128))
        in_maps.append(m)
    res = run_bass_kernel_spmd(nc, in_maps, core_ids=list(range(B)))
    return np.stack([np.asarray(r["out"], dtype=np.float32) for r in res.results], axis=0)
```

```python
import numpy as np
from contextlib import ExitStack
import concourse.bass as bass
import concourse.mybir as mybir
from concourse.bass_utils import run_bass_kernel_spmd

F32 = mybir.dt.float32
BF16 = mybir.dt.bfloat16
I32 = mybir.dt.int32
AF = mybir.ActivationFunctionType
ALU = mybir.AluOpType

NDMASEM = 8
L = 2048
D = 1024
DEPTH = 2
EPS = 1e-6
import os as _os
DBGL = int(_os.environ.get("DBGL", "0"))
MAGIC = 12582912.0
TWO_PI = float(2 * np.pi)
PVROWS = 160
R_ADAB, R_N1G, R_PSC, R_SSD, R_GLUB, R_CW, R_CB, R_BA, R_BX, R_LAM, R_BG, R_N2G, R_FG = 0, 48, 56, 60, 64, 68, 84, 88, 96, 104, 112, 144, 152


class Sched:
    ENG = ("pe", "act", "dve", "pool", "sp")

    def __init__(self, nc, stack):
        self.nc = nc
        self.prog = {e: [] for e in self.ENG}
        self.count = {e: 0 for e in self.ENG}
        self.sem = {}
        for e in ("pe", "act", "dve", "pool"):
            self.sem[e] = stack.enter_context(nc.semaphore("s_" + e))
        self.dq = {}
        for q in ("sp", "pool"):
            for i in range(NDMASEM):
                self.sem[(q, i)] = stack.enter_context(nc.semaphore(f"d_{q}{i}"))
            self.dq[q] = 0
        self.seen = {e: {} for e in self.ENG}
        self.lastw = {}
        self.readers = {}
        self.ninstr = 0

    def _need(self, eng, tickets):
        waits = {}
        for t in tickets:
            if t is None:
                continue
            k, v = t
            if self.seen[eng].get(k, 0) >= v:
                continue
            if waits.get(k, 0) < v:
                waits[k] = v
        for k, v in waits.items():
            self.seen[eng][k] = v
        return list(waits.items())

    def _deps(self, reads, writes):
        ts = []
        for r in reads:
            ts.append(self.lastw.get(r))
        for w in writes:
            ts.append(self.lastw.get(w))
            ts.extend(self.readers.get(w, ()))
        return ts

    def _commit(self, ticket, reads, writes):
        for r in reads:
            self.readers.setdefault(r, []).append(ticket)
        for w in writes:
            self.lastw[w] = ticket
            self.readers[w] = []

    def op(self, eng, fn, reads=(), writes=(), inc=True):
        deps = self._deps(reads, writes)
        if eng == "pe":
            deps = [t for t in deps if t is not None and t[0] != "pe"]
        waits = self._need(eng, deps)
        if inc:
            self.count[eng] += 1
            ticket = (eng, self.count[eng])
        else:
            ticket = (eng, self.count[eng] + 1)
        self.prog[eng].append((fn, waits, self.sem[eng] if inc else None, 1))
        self._commit(ticket, reads, writes)
        self.ninstr += 1
        return ticket

    def dma(self, q, fn, reads=(), writes=()):
        i = self.dq[q]
        self.dq[q] += 1
        slot, rnd = i % NDMASEM, i // NDMASEM
        key = (q, slot)
        deps = self._deps(reads, writes)
        if rnd > 0:
            deps.append((key, 16 * rnd))
        waits = self._need(q, deps)
        ticket = (key, 16 * (rnd + 1))
        self.prog[q].append((fn, waits, self.sem[key], 16))
        self._commit(ticket, reads, writes)
        self.ninstr += 1
        return ticket

    def all_tickets(self):
        ts = [(e, self.count[e]) for e in ("pe", "act", "dve", "pool") if self.count[e] > 0]
        for q in ("sp", "pool"):
            n = self.dq[q]
            for slot in range(NDMASEM):
                cnt = (n - slot + NDMASEM - 1) // NDMASEM if n > slot else 0
                if cnt > 0:
                    ts.append(((q, slot), 16 * cnt))
        return ts

    def barrier(self):
        ts = self.all_tickets()
        for e in self.ENG:
            waits = self._need(e, ts)
            if waits:
                self.prog[e].append((None, waits, None, 0))
        self.lastw = {}
        self.readers = {}

    def emit(self, block):
        def run(name):
            def body(e):
                for fn, waits, sem, n in self.prog[name]:
                    for k, v in waits:
                        e.wait_ge(self.sem[k], v)
                    if fn is None:
                        continue
                    ins = fn(e)
                    if sem is not None:
                        ins.then_inc(sem, n)
            return body

        block.tensor(run("pe"))
        block.scalar(run("act"))
        block.vector(run("dve"))
        block.gpsimd(run("pool"))
        block.sync(run("sp"))


class Arena:
    def __init__(self, ap, words):
        self.ap = ap
        self.words = words
        self.top = 0
        self.marks = []
        self.peak = 0

    def _take(self, words):
        a = self.ap[:, self.top:self.top + words]
        self.top += words
        self.peak = max(self.peak, self.top)
        assert self.top <= self.words, f"arena overflow {self.top} > {self.words}"
        return a

    def f32(self, cols):
        return self._take(cols)

    def i32(self, cols):
        return self._take(cols).bitcast(I32)

    def bf16(self, cols):
        w = (cols + 1) // 2
        return self._take(w).bitcast(BF16)[:, :cols]

    def push(self):
        self.marks.append(self.top)

    def pop(self):
        self.top = self.marks.pop()


class _Stop(Exception):
    pass


def build_program(dbg=None, stage=99):
    nc = bass.Bass("TRN2", target_bir_lowering=False)
    dram_in = lambda name, shape: nc.dram_tensor(name, list(shape), F32, kind="ExternalInput").ap()
    x_d = dram_in("x", [L, D])
    cb_d = dram_in("cb", [8, 128])
    pm_d = dram_in("pm", [128, 20 * 128])
    Wd = []
    for l in range(DEPTH):
        Wd.append(dict(
            pv=dram_in(f"pv{l}", [PVROWS, 128]),
            adaw=dram_in(f"adaw{l}", [D, 6 * D]),
            win=dram_in(f"win{l}", [D, 3072]),
            poolw=dram_in(f"poolw{l}", [4, 128, 128]),
            ssc=dram_in(f"ssc{l}", [128, 96]),
            ssb=dram_in(f"ssb{l}", [128, 2 * 16 * 16]),
            sscc=dram_in(f"sscc{l}", [128, 2 * 2 * 16 * 16]),
            gluw=dram_in(f"gluw{l}", [512, 512]),
            lrubd=dram_in(f"lrubd{l}", [128, 16 * 128]),
            gng=dram_in(f"gng{l}", [128, 512]),
            gws=dram_in(f"gws{l}", [128, 4 * 128]),
            gbs=dram_in(f"gbs{l}", [1, 512]),
            wbr=dram_in(f"wbr{l}", [4, 512, D]),
            wgate=dram_in(f"wgate{l}", [D, 4 * D]),
            wout=dram_in(f"wout{l}", [D, D]),
            wff1=dram_in(f"wff1{l}", [D, 4 * D]),
            wff2=dram_in(f"wff2{l}", [4 * D, D]),
        ))
    out_d = nc.dram_tensor("out", [L, D], F32, kind="ExternalOutput").ap()
    y_d = nc.dram_tensor("yscr", [4, 128, 4, L], BF16, kind="Internal").ap()
    dbg_d = {}
    if dbg:
        for name, shape in dbg.items():
            dbg_d[name] = nc.dram_tensor("dbg_" + name, list(shape), F32, kind="ExternalOutput").ap()

    with ExitStack() as st:
        S = Sched(nc, st)
        AW = 48600
        arena_t = st.enter_context(nc.sbuf_tensor("arena", [128, AW], F32))
        A = Arena(arena_t[:], AW)
        banks = [st.enter_context(nc.psum_tensor(f"ps{i}", [128, 512], F32)) for i in range(8)]
        block = st.enter_context(nc.Block())
        bank_state = {"i": 0, "free": list(range(8))}

        def PS():
            fr = bank_state["free"]
            b = fr[bank_state["i"] % len(fr)]
            bank_state["i"] += 1
            return banks[b][:], ("ps", b)

        def MM(out, lhsT, rhs, start, stop, R, W, inc):
            return S.op("pe", lambda e: e.matmul(out, lhsT=lhsT, rhs=rhs, start=start, stop=stop), R, W, inc)

        def TR(out, in_, ident, R, W):
            return S.op("pe", lambda e: e.transpose(out=out, in_=in_, identity=ident), R, W, True)

        def ACT(out, in_, func, R, W, bias=None, scale=None, accum=None):
            kw = {}
            if bias is not None:
                kw["bias"] = bias
            if scale is not None:
                kw["scale"] = scale
            if accum is not None:
                kw["accum_out"] = accum
            return S.op("act", lambda e: e.activation(out=out, in_=in_, func=func, **kw), R, W)

        def TT(eng, out, a, b, op, R, W):
            return S.op(eng, lambda e: e.tensor_tensor(out=out, in0=a, in1=b, op=op), R, W)

        def TS(eng, out, a, s1, s2, op0, op1, R, W):
            if s2 is None:
                return S.op(eng, lambda e: e.tensor_scalar(out=out, in0=a, scalar1=s1, scalar2=None, op0=op0), R, W)
            return S.op(eng, lambda e: e.tensor_scalar(out=out, in0=a, scalar1=s1, scalar2=s2, op0=op0, op1=op1), R, W)

        def STT(eng, out, a, s, b, op0, op1, R, W):
            eng = "dve"
            return S.op(eng, lambda e: e.scalar_tensor_tensor(out=out, in0=a, scalar=s, in1=b, op0=op0, op1=op1), R, W)

        def CP(eng, out, in_, R, W):
            if eng == "act":
                return S.op("act", lambda e: e.activation(out=out, in_=in_, func=AF.Copy), R, W)
            return S.op(eng, lambda e: e.tensor_copy(out=out, in_=in_), R, W)

        def MSET(eng, out, val, W):
            return S.op(eng, lambda e: e.memset(out, val), (), W)

        def SCAN(out, d0, d1, init, R, W):
            return S.op("dve", lambda e: e.tensor_tensor_scan(out=out, data0=d0, data1=d1, initial=init, op0=ALU.mult, op1=ALU.add), R, W)

        def RECIP(out, in_, R, W):
            return S.op("dve", lambda e: e.reciprocal(out=out, in_=in_), R, W)

        def DMA(q, out, in_, R, W):
            return S.dma(q, lambda e: e.dma_start(out=out, in_=in_), R, W)

        def DBG(name, src_ap, R):
            if name in dbg_d:
                DMA("pool", dbg_d[name].rearrange("p (c t) -> p c t", c=src_ap.shape[1]) if len(src_ap.shape) == 3 else dbg_d[name], src_ap, R, [("dbgout", name)])

        xT = A.f32(8 * L).rearrange("p (c t) -> p c t", c=8)
        ident = A.f32(128)
        onesb = A.bf16(128)
        pcol = [A.f32(PVROWS) for _ in range(DEPTH)]
        modt = [A.f32(48) for _ in range(DEPTH)]
        modA = [A.f32(16) for _ in range(DEPTH)]
        cond = A.f32(8)
        epsc = A.f32(1)
        halfpi = A.f32(1)
        onec = A.f32(1)
        magicc = A.f32(1)

        MSET("pool", ident, 1.0, ["ident"])
        S.op("pool", lambda e: e.affine_select(out=ident, in_=ident, pattern=[[-1, 128]], compare_op=ALU.is_equal,
                                                fill=0.0, base=0, channel_multiplier=1), ["ident"], ["ident"])
        MSET("pool", onesb, 1.0, ["onesb"])
        MSET("pool", epsc, EPS, ["epsc"])
        MSET("pool", halfpi, float(np.pi / 2), ["halfpi"])
        MSET("pool", onec, 1.0, ["onec"])
        MSET("pool", magicc, MAGIC, ["magicc"])

        A.push()
        for l in range(DEPTH):
            pvt = A.f32(2 * 128).rearrange("p (a b) -> p a b", a=2)
            DMA("sp", pvt[:, 0, :], Wd[l]["pv"][0:128, :], [], [("pvt", l)])
            DMA("sp", pvt[0:32, 1, :], Wd[l]["pv"][128:160, :], [], [("pvt", l)])
            ps, pk = PS()
            TR(ps[:, 0:128], pvt[:, 0, :], ident, [("pvt", l), "ident"], [pk])
            TR(ps[:, 128:160], pvt[0:32, 1, :], ident[0:32, 0:32], [("pvt", l), "ident"], [pk])
            CP("dve", pcol[l], ps[:, 0:PVROWS], [pk], [("pcol", l)])
        cbt = A.f32(128)
        DMA("sp", cbt[0:8, :], cb_d, [], ["cbt"])
        ps, pk = PS()
        TR(ps[:, 0:8], cbt[0:8, :], ident[0:8, 0:8], ["cbt", "ident"], [pk])
        ACT(cond, ps[:, 0:8], AF.Silu, [pk], ["cond"])
        def mod_piece_dma(l, kc, part, nparts, buf, bkey, q="sp"):
            Wc = 6144 // nparts
            c0 = part * Wc
            DMA(q, buf[:, 0:Wc], Wd[l]["adaw"][kc * 128:(kc + 1) * 128, c0:c0 + Wc], [], [bkey])

        def mod_piece(l, kc, part, nparts, buf, bkey, dma=True, defer_add=False, q="sp"):
            Wc = 6144 // nparts
            nj = Wc // 128
            j0 = part * nj
            if dma:
                mod_piece_dma(l, kc, part, nparts, buf, bkey, q)
            ps, pk = PS()
            for j in range(nj):
                MM(ps[:, j:j + 1], buf[:, j * 128:(j + 1) * 128], cond[:, kc:kc + 1], True, True, [bkey, "cond"], [pk], j == nj - 1)
            dst = modt[l][:, j0:j0 + nj]
            mk = ("modp", l, part)

            def do_add():
                if kc == 0:
                    TT("dve", dst, ps[:, 0:nj], pcol[l][:, R_ADAB + j0:R_ADAB + j0 + nj], ALU.add, [pk, ("pcol", l)], [mk])
                else:
                    TT("dve", dst, ps[:, 0:nj], dst, ALU.add, [pk, mk], [mk])
            if defer_add:
                return do_add
            do_add()

        def mod_finish(l, nparts):
            mks = [("modp", l, p) for p in range(nparts)]
            STT("dve", modA[l][:, 0:8], modt[l][:, 8:16], 1.0, pcol[l][:, R_N1G:R_N1G + 8], ALU.add, ALU.mult,
                mks + [("pcol", l)], [("modA", l), ("mod", l)])
            STT("dve", modA[l][:, 8:16], modt[l][:, 32:40], 1.0, pcol[l][:, R_N2G:R_N2G + 8], ALU.add, ALU.mult,
                mks + [("pcol", l)], [("modA", l), ("mod", l)])

        awb = [A.f32(3072) for _ in range(4)]
        xtok = [A.f32(D), A.f32(D)]

        def x_tile(n):
            xt = xtok[n % 2]
            DMA("sp" if n % 2 == 0 else "pool", xt, x_d[n * 128:(n + 1) * 128, :], [], [("xtok", n % 2)])
            for cg in range(2):
                ps, pk = PS()
                for c4 in range(4):
                    c = cg * 4 + c4
                    TR(ps[:, c4 * 128:(c4 + 1) * 128], xt[:, c * 128:(c + 1) * 128], ident, [("xtok", n % 2), "ident"], [pk])
                eng = "dve" if cg == 0 else "act"
                CP(eng, xT[:, cg * 4:(cg + 1) * 4, n * 128:(n + 1) * 128], ps.rearrange("p (c t) -> p c t", c=4), [pk],
                   [("x", c, n // 4) for c in range(cg * 4, cg * 4 + 4)])

        for kc in range(8):
            for part in range(2):
                bi = (kc * 2 + part) % 4
                mod_piece(0, kc, part, 2, awb[bi], ("aw", bi), q=("sp" if part == 0 else "pool"))
            x_tile(2 * kc)
            x_tile(2 * kc + 1)
        mod_finish(0, 2)
        S.barrier()
        A.pop()

        def rms_mod(l, which, tts, hT, hkey):
            A.push()
            sq = [A.bf16(8 * 512).rearrange("p (c t) -> p c t", c=8) for _ in range(2)]
            rt = [A.f32(512) for _ in range(2)]
            tmps = [A.f32(512) for _ in range(3)] + [tmp2]
            shc = 0 if which == 0 else 24
            tts = list(tts)
            pend = {}

            def stats_a(i):
                tt = tts[i]
                b = i % 2
                sl = slice(tt * 512, (tt + 1) * 512)
                ACT(sq[b], xT[:, :, sl], AF.Square, [("x", c, tt) for c in range(8)], [("sq", b)])
                ps, pk = PS()
                for c in range(8):
                    MM(ps, onesb, sq[b][:, c, :], c == 0, c == 7, [("sq", b), "onesb"], [pk], c == 7)
                pend[i] = (ps, pk)

            def stats_b(i):
                b = i % 2
                ps, pk = pend.pop(i)
                ACT(rt[b], ps, AF.Sqrt, [pk, "epsc"], [("rt", b)], bias=epsc[:, 0:1], scale=1.0 / D)
                RECIP(rt[b], rt[b], [("rt", b)], [("rt", b)])

            def apply(i):
                tt = tts[i]
                b = i % 2
                sl = slice(tt * 512, (tt + 1) * 512)
                for c in range(8):
                    tm = tmps[c % 4]
                    tk = ("tmp", c % 4)
                    STT("dve", tm, xT[:, c, sl], modA[l][:, which * 8 + c:which * 8 + c + 1], rt[b], ALU.mult, ALU.mult,
                        [("x", c, tt), ("rt", b), ("modA", l)], [tk])
                    dst = hT[:, c, sl] if hT.shape[2] == L else hT[:, c, (tt % 2) * 512:(tt % 2 + 1) * 512]
                    shcol = modt[l][:, shc + c:shc + c + 1]
                    if c % 2 == 0:
                        ACT(dst, tm, AF.Identity, [tk, ("mod", l)], [(hkey, c, tt)], bias=shcol)
                    else:
                        TS("dve", dst, tm, shcol, None, ALU.add, None, [tk, ("mod", l)], [(hkey, c, tt)])

            stats_a(0)
            stats_b(0)
            for i in range(len(tts)):
                if i + 1 < len(tts):
                    stats_a(i + 1)
                apply(i)
                if i + 1 < len(tts):
                    stats_b(i + 1)
            A.pop()

        tmp2 = A.f32(512)

        def load_w(q, dst, src, key):
            return DMA(q, dst, src, [], [key])

        def proj_fm(wt, wkey, ncol_chunks, hT, hkey, tts, sink, js=None):
            for j in (range(ncol_chunks) if js is None else js):
                for tt in tts:
                    ps, pk = PS()
                    for kc in range(8):
                        MM(ps, wt[:, kc, j * 128:(j + 1) * 128], hT[:, kc, tt * 512:(tt + 1) * 512], kc == 0, kc == 7,
                           [wkey, (hkey, kc, tt)], [pk], kc == 7)
                    sink(j, tt, ps, pk)

        persist_top = A.top

        def checkpoint(k):
            if stage == k:
                raise _Stop()
        try:
          checkpoint(0)
          for l in range(DEPTH):
                W = Wd[l]
                pc = pcol[l]
                pck = ("pcol", l)
                A.push()
                A.push()
                zsT = A.bf16(4 * L).rearrange("p (c t) -> p c t", c=4)
                ssc = A.f32(96)
                ssb = A.f32(512).rearrange("p (r q h) -> p r q h", r=2, q=16)
                scc = A.f32(1024).rearrange("p (r d q j) -> p r d q j", r=2, d=2, q=16)
                sp_ = A.f32(32 * 12)
                bbar = A.f32(2 * 2 * 256).rearrange("p (d r q h) -> p d r q h", d=2, r=2, q=16)
                bt1 = A.f32(256).rearrange("p (q h) -> p q h", q=16)
                bt2 = A.f32(256).rearrange("p (q h) -> p q h", q=16)
                Z = A.f32(4 * 128).rearrange("p (i c) -> p i c", i=4)
                BT = A.bf16(2 * 2 * 2 * 4 * 128).rearrange("p (s d r q c) -> p s d r q c", s=2, d=2, r=2, q=4)
                Cpad = A.bf16(2 * 2 * 3 * 4 * 128).rearrange("p (s d v q c) -> p s d v q c", s=2, d=2, v=3, q=4)
                A.push()
                hT = A.bf16(8 * L).rearrange("p (c t) -> p c t", c=8)
                wss = A.bf16(8 * 512).rearrange("p (k n) -> p k n", k=8)
                DMA("pool", wss, W["win"][:, 512:1024].rearrange("(k p) n -> p k n", p=128), [], ["wss"])
                DMA("sp", ssc, W["ssc"], [], ["ssc"])
                DMA("sp", ssb, W["ssb"].rearrange("p (r q h) -> p r q h", r=2, q=16), [], ["ssb"])
                DMA("sp", scc, W["sscc"].rearrange("p (r d q j) -> p r d q j", r=2, d=2, q=16), [], ["scc"])
                col = lambda i: sp_[:, i * 32:(i + 1) * 32]
                dt, mag, th, t1, t2, cs, sn, abr, abi, fre, fim, phi = [col(i) for i in range(12)]
                lr, li, ldt = ssc[:, 0:32], ssc[:, 32:64], ssc[:, 64:96]
                K = ["sprep"]
                ACT(dt, ldt, AF.Exp, ["ssc"], K)
                TT("dve", t1, lr, dt, ALU.mult, ["ssc"] + K, K)
                ACT(mag, t1, AF.Exp, K, K)
                TT("dve", th, li, dt, ALU.mult, ["ssc"] + K, K)
                TS("dve", t1, th, 1.0 / TWO_PI, MAGIC, ALU.mult, ALU.add, K, K)
                TS("dve", t2, th, 1.0 / TWO_PI, None, ALU.mult, None, K, K)
                STT("dve", phi, t1, MAGIC, t2, ALU.subtract, ALU.subtract, K, K)
                TS("dve", phi, phi, -1.0, None, ALU.mult, None, K, K)
                ACT(sn, phi, AF.Sin, K, K, scale=TWO_PI)
                ACT(t1, phi, AF.Abs, K, K)
                ACT(cs, t1, AF.Sin, K + ["halfpi"], K, scale=-TWO_PI, bias=halfpi[:, 0:1])
                TT("dve", abr, mag, cs, ALU.mult, K, K)
                TT("dve", abi, mag, sn, ALU.mult, K, K)
                TT("dve", t1, lr, lr, ALU.mult, ["ssc"] + K, K)
                TT("dve", t2, li, li, ALU.mult, ["ssc"] + K, K)
                TT("dve", t1, t1, t2, ALU.add, K, K)
                RECIP(t1, t1, K, K)
                TS("dve", abr, abr, -1.0, None, ALU.add, None, K, K)
                TT("dve", fre, abr, lr, ALU.mult, K, K)
                TT("dve", t2, abi, li, ALU.mult, K, K)
                TT("dve", fre, fre, t2, ALU.add, K, K)
                TT("dve", fre, fre, t1, ALU.mult, K, K)
                TT("dve", fim, abi, lr, ALU.mult, K, K)
                TT("dve", t2, abr, li, ALU.mult, K, K)
                TT("dve", fim, fim, t2, ALU.subtract, K, K)
                TT("dve", fim, fim, t1, ALU.mult, K, K)
                for d_ in range(2):
                    frb = fre[:, d_ * 16:(d_ + 1) * 16].unsqueeze(2).to_broadcast([128, 16, 16])
                    fib = fim[:, d_ * 16:(d_ + 1) * 16].unsqueeze(2).to_broadcast([128, 16, 16])
                    TT("dve", bt1, ssb[:, 0], frb, ALU.mult, ["ssb"] + K, ["bt1"])
                    TT("dve", bt2, ssb[:, 1], fib, ALU.mult, ["ssb"] + K, ["bt2"])
                    TT("dve", bbar[:, d_, 0], bt1, bt2, ALU.subtract, ["bt1", "bt2"], ["bbar"])
                    TT("dve", bt1, ssb[:, 1], frb, ALU.mult, ["ssb"] + K, ["bt1"])
                    TT("dve", bt2, ssb[:, 0], fib, ALU.mult, ["ssb"] + K, ["bt2"])
                    TT("dve", bbar[:, d_, 1], bt1, bt2, ALU.add, ["bt1", "bt2"], ["bbar"])
                MSET("pool", Z, 0.0, ["Z"])
                MSET("pool", Cpad, 0.0, [("Cpad", 0), ("Cpad", 1)])

                def bm_copy(blk, g):
                    d_, r_ = g // 2, g % 2
                    for qq in range(4):
                        q = blk * 4 + qq
                        for half in range(2):
                            c0 = 16 * (2 * qq + half)
                            CP("pool", Z[half * 64:(half + 1) * 64, qq, c0:c0 + 16], bbar[half * 64:(half + 1) * 64, d_, r_, q, :],
                               ["bbar"], ["Z"])

                def bm_tr(blk, g):
                    d_, r_ = g // 2, g % 2
                    st_ = blk % 2
                    ps, pk = PS()
                    for qq in range(4):
                        TR(ps[:, qq * 128:(qq + 1) * 128], Z[:, qq, :], ident, ["Z", "ident"], [pk])
                    CP("act", BT[:, st_, d_, r_, :, :], ps.rearrange("p (q c) -> p q c", q=4), [pk], [("BT", st_)])

                def bm_cpad(blk, d_):
                    st_ = blk % 2
                    for v in range(3):
                        r_ = 0 if v < 2 else 1
                        sgn = 1.0 if v == 0 else -1.0
                        for qq in range(4):
                            q = blk * 4 + qq
                            for half in range(2):
                                c0 = 16 * (2 * qq + half)
                                hs = slice(half * 64, (half + 1) * 64)
                                TS("pool", Cpad[hs, st_, d_, v, qq, c0:c0 + 16], scc[hs, r_, d_, q, :], sgn, None, ALU.mult, None,
                                   ["scc"], [("Cpad", st_)])

                bm_cpad(0, 0)
                bm_cpad(0, 1)
                bm_copy(0, 0)
                rms_mod(l, 0, range(4), hT, "h")
                if l == DBGL:
                    DBG("h", hT, [("h", c, tt) for c in range(8) for tt in range(4)])
                bm_tr(0, 0)
                bm_copy(0, 1)

                def sink_zs(j, tt, ps, pk):
                    CP("act" if (j + tt) % 2 == 0 else "dve", zsT[:, j, tt * 512:(tt + 1) * 512], ps, [pk], [("zs", j, tt)])
                proj_fm(wss, "wss", 4, hT, "h", range(4), sink_zs, js=[0, 1])
                bm_tr(0, 1)
                bm_copy(0, 2)
                proj_fm(wss, "wss", 4, hT, "h", range(4), sink_zs, js=[2, 3])
                bm_tr(0, 2)
                bm_copy(0, 3)
                bm_tr(0, 3)
                if l == DBGL:
                    DBG("zs", zsT, [("zs", c, tt) for c in range(4) for tt in range(4)])
                S.barrier()
                A.pop()
                checkpoint(1)
                wk = A.f32(L)
                iot_i = wk.bitcast(I32)
                iot = A.f32(L)
                S.op("pool", lambda e: e.iota(iot_i, pattern=[[1, L]], base=0, channel_multiplier=0), [], ["ioti"])
                CP("dve", iot, iot_i, ["ioti"], ["iot"])
                carr = A.f32(2 * 4 * 2).rearrange("p (d q r) -> p d q r", d=2, q=4)
                tA = [wk[:, 0:512], wk[:, 1024:1536]]
                tB = [wk[:, 512:1024], wk[:, 1536:2048]]
                m2 = [A.bf16(512) for _ in range(2)]
                m4 = [A.bf16(512) for _ in range(2)]
                sinT = [A.bf16(512) for _ in range(4)]
                cosT = [A.bf16(512) for _ in range(4)]
                xre = [A.bf16(512) for _ in range(2)]
                xim = [A.bf16(512) for _ in range(2)]
                m1 = [A.bf16(512) for _ in range(2)]
                m3 = [A.bf16(512) for _ in range(2)]
                Gre = [A.bf16(512) for _ in range(2)]
                Gim = [A.bf16(512) for _ in range(2)]
                Pp = [[A.bf16(512) for _ in range(4)] for _ in range(2)]
                ygt = A.f32(512)
                ygT = zsT
                ybst = [A.bf16(512), A.bf16(512)]
                mod_next = l + 1 < DEPTH
                if mod_next:
                    awq = [A.f32(1536), A.f32(1536)]
                    mod_jobs = [(kc, part) for kc in range(8) for part in range(4)]
                mod_di = 0
                mod_ci = 0
                mod_adds = []
                it = 0
                wgl = A.bf16(4 * 512).rearrange("p (k n) -> p k n", k=4)
                DMA("pool", wgl, W["gluw"].rearrange("(k p) n -> p k n", p=128), [], ["wgl"])
                for blk in range(4):
                    st_ = blk % 2
                    ybanks = [0, 1, 2, 3]
                    bank_state["free"] = [4, 5, 6, 7]
                    ykeys = [("ps", b) for b in ybanks]
                    iters = []
                    for d_ in range(2):
                        for step in range(4):
                            tt = step if d_ == 0 else 3 - step
                            for qq in range(4):
                                iters.append((d_, step, tt, qq, it % 2, it % 4))
                                it += 1

                    def stage_a(d_, step, tt, qq, b, b4):
                        k4 = lambda s_: (s_, b4)
                        sl = slice(tt * 512, (tt + 1) * 512)
                        q = blk * 4 + qq
                        kb = lambda s_: (s_, b)
                        phc = phi[:, d_ * 16 + q:d_ * 16 + q + 1]
                        ACT(tA[b], iot[:, sl], AF.Identity, ["iot", "magicc"] + K, [kb("tA")], scale=phc, bias=magicc[:, 0:1])
                        ACT(tB[b], iot[:, sl], AF.Copy, ["iot"] + K, [kb("tB")], scale=phc)
                        STT("dve", tA[b], tA[b], MAGIC, tB[b], ALU.subtract, ALU.subtract, [kb("tA"), kb("tB")], [kb("tA")])
                        ACT(sinT[b4], tA[b], AF.Sin, [kb("tA")], [k4("sinT")], scale=(-TWO_PI if d_ == 0 else TWO_PI))
                        ACT(tB[b], tA[b], AF.Abs, [kb("tA")], [kb("tB")])
                        ACT(cosT[b4], tB[b], AF.Sin, [kb("tB"), "halfpi"], [k4("cosT")], scale=-TWO_PI, bias=halfpi[:, 0:1])
                        psr, pkr = PS()
                        MM(psr, BT[:, st_, d_, 0, qq, :], zsT[:, blk, sl], True, True, [("BT", st_), ("zs", blk, tt)], [pkr], True)
                        psi, pki = PS()
                        MM(psi, BT[:, st_, d_, 1, qq, :], zsT[:, blk, sl], True, True, [("BT", st_), ("zs", blk, tt)], [pki], True)
                        CP("act", xre[b], psr, [pkr], [kb("xre")])
                        CP("act", xim[b], psi, [pki], [kb("xim")])

                    def stage_b1(d_, step, tt, qq, b, b4):
                        kb = lambda s_: (s_, b)
                        k4 = lambda s_: (s_, b4)
                        TT("dve", m1[b], cosT[b4], xre[b], ALU.mult, [k4("cosT"), kb("xre")], [kb("m1")])
                        TT("dve", m2[b], sinT[b4], xim[b], ALU.mult, [k4("sinT"), kb("xim")], [kb("m2")])
                        TT("dve", m3[b], cosT[b4], xim[b], ALU.mult, [k4("cosT"), kb("xim")], [kb("m3")])
                        TT("dve", m4[b], sinT[b4], xre[b], ALU.mult, [k4("sinT"), kb("xre")], [kb("m4")])
                        TT("dve", m3[b], m3[b], m4[b], ALU.subtract, [kb("m3"), kb("m4")], [kb("m3")])
                        TT("dve", m1[b], m1[b], m2[b], ALU.add, [kb("m1"), kb("m2")], [kb("m1")])

                    def stage_b2(d_, step, tt, qq, b, b4):
                        q = blk * 4 + qq
                        kb = lambda s_: (s_, b)
                        rho = mag[:, d_ * 16 + q:d_ * 16 + q + 1].to_broadcast([128, 512])
                        ck = ("carr", d_, qq)
                        if d_ == 0:
                            o_re, i_re, o_im, i_im = Gre[b], m1[b], Gim[b], m3[b]
                            last = slice(511, 512)
                        else:
                            o_re, i_re, o_im, i_im = Gre[b][:, ::-1], m1[b][:, ::-1], Gim[b][:, ::-1], m3[b][:, ::-1]
                            last = slice(0, 1)
                        init_re = 0.0 if step == 0 else carr[:, d_, qq, 0:1]
                        init_im = 0.0 if step == 0 else carr[:, d_, qq, 1:2]
                        SCAN(o_im, rho, i_im, init_im, [kb("m3"), ck] + K, [kb("Gim")])
                        SCAN(o_re, rho, i_re, init_re, [kb("m1"), ck] + K, [kb("Gre")])
                        if step < 3:
                            CP("pool", carr[:, d_, qq, 0:1], Gre[b][:, last], [kb("Gre")], [ck])
                            CP("pool", carr[:, d_, qq, 1:2], Gim[b][:, last], [kb("Gim")], [ck])

                    def stage_b3(d_, step, tt, qq, b, b4):
                        kb = lambda s_: (s_, b)
                        k4 = lambda s_: (s_, b4)
                        TT("dve", Pp[b][0], cosT[b4], Gre[b], ALU.mult, [k4("cosT"), kb("Gre")], [kb("P0")])
                        TT("dve", Pp[b][1], sinT[b4], Gim[b], ALU.mult, [k4("sinT"), kb("Gim")], [kb("P1")])
                        TT("dve", Pp[b][2], sinT[b4], Gre[b], ALU.mult, [k4("sinT"), kb("Gre")], [kb("P2")])
                        TT("dve", Pp[b][3], cosT[b4], Gim[b], ALU.mult, [k4("cosT"), kb("Gim")], [kb("P3")])
                        yps = banks[ybanks[tt]][:]
                        first = (d_ == 0 and qq == 0)
                        lastm = (d_ == 1 and qq == 3)
                        vs = [0, 1, 2, 2]
                        for m in range(4):
                            MM(yps, Cpad[:, st_, d_, vs[m], qq, :], Pp[b][m], first and m == 0, lastm and m == 3,
                               [("Cpad", st_), kb("P%d" % m)], [ykeys[tt]], m == 3)

                    n_it = len(iters)
                    for t_ in range(-2, n_it + 1):
                        if blk < 3:
                            if t_ in (2, 6, 10, 14):
                                bm_copy(blk + 1, (t_ - 2) // 4)
                            if t_ in (18, 22):
                                bm_cpad(blk + 1, (t_ - 18) // 4)
                        if mod_next and t_ >= 0 and t_ % 4 == 1 and mod_di < len(mod_jobs):
                            kc_, part_ = mod_jobs[mod_di]
                            bi_ = mod_di % 2
                            mod_piece_dma(l + 1, kc_, part_, 4, awq[bi_], ("awq", bi_))
                            mod_di += 1
                        if mod_next and t_ >= 0 and t_ % 4 == 0 and mod_adds:
                            mod_adds.pop(0)()
                            if mod_ci == len(mod_jobs) and not mod_adds:
                                mod_finish(l + 1, 4)
                        if 0 <= t_ + 2 < n_it:
                            stage_a(*iters[t_ + 2])
                        if blk < 3 and t_ in (4, 8, 12, 16):
                            bm_tr(blk + 1, (t_ - 4) // 4)
                        if mod_next and t_ >= 0 and t_ % 4 == 3 and mod_ci < mod_di:
                            kc_, part_ = mod_jobs[mod_ci]
                            bi_ = mod_ci % 2
                            mod_adds.append(mod_piece(l + 1, kc_, part_, 4, awq[bi_], ("awq", bi_), dma=False, defer_add=True))
                            mod_ci += 1
                        if 0 <= t_ + 1 < n_it:
                            stage_b1(*iters[t_ + 1])
                        if 0 <= t_ < n_it:
                            stage_b2(*iters[t_])
                        if 0 <= t_ - 1 < n_it:
                            stage_b3(*iters[t_ - 1])
                    for tt in range(4):
                        sl = slice(tt * 512, (tt + 1) * 512)
                        STT("dve", ygt, zsT[:, blk, sl], pc[:, R_SSD + blk:R_SSD + blk + 1], banks[ybanks[tt]][:], ALU.mult, ALU.add,
                            [("zs", blk, tt), ykeys[tt], pck], ["ygt"])
                        ACT(ygT[:, blk, sl], ygt, AF.Gelu, ["ygt"], [("yg", blk, tt)])
                    bank_state["free"] = list(range(8))
                while mod_next and mod_adds:
                    mod_adds.pop(0)()
                    if mod_ci == len(mod_jobs) and not mod_adds:
                        mod_finish(l + 1, 4)
                sg = ygt
                for ob in range(4):
                    for tt in range(4):
                        sl = slice(tt * 512, (tt + 1) * 512)
                        ps, pk = PS()
                        for kc in range(4):
                            MM(ps, wgl[:, kc, ob * 128:(ob + 1) * 128], ygT[:, kc, sl], kc == 0, kc == 3, ["wgl", ("yg", kc, tt)], [pk], kc == 3)
                        ACT(sg, ps, AF.Sigmoid, [pk, pck], ["ygt"], bias=pc[:, R_GLUB + ob:R_GLUB + ob + 1])
                        bb = (ob * 4 + tt) % 2
                        TT("dve", ybst[bb], sg, ygT[:, ob, sl], ALU.mult, ["ygt", ("yg", ob, tt)], [("ybst", bb)])
                        DMA("sp", y_d[1, :, ob, sl], ybst[bb], [("ybst", bb)], [("yD", 1, tt)])
                        if l == DBGL and "yb" in dbg_d:
                            DMA("pool", dbg_d["yb"][:, ob * L + tt * 512:ob * L + (tt + 1) * 512], ybst[bb], [("ybst", bb)], [("dbgo", "yb", ob, tt)])
                S.barrier()
                A.pop()
                checkpoint(2)
                hT = A.bf16(8 * L).rearrange("p (c t) -> p c t", c=8)
                A.push()
                wlr = A.bf16(8 * 1024).rearrange("p (k n) -> p k n", k=8)
                DMA("pool", wlr, W["win"][:, 1024:2048].rearrange("(k p) n -> p k n", p=128), [], ["wlr"])
                lbd = A.bf16(16 * 128).rearrange("p (i c) -> p i c", i=16)
                DMA("pool", lbd, W["lrubd"].rearrange("p (i c) -> p i c", i=16), [], ["lbd"])
                rms_mod(l, 0, range(4), hT, "h")
                S.barrier()
                cl = A.f32(8)
                lt1 = A.f32(8)
                lt2 = A.f32(8)
                lam = pc[:, R_LAM:R_LAM + 8]
                ACT(lt1, lam, AF.Abs, [pck], ["lt1"])
                ACT(lt1, lt1, AF.Exp, ["lt1"], ["lt1"], scale=-1.0)
                ACT(lt1, lt1, AF.Ln, ["lt1", "onec"], ["lt1"], bias=onec[:, 0:1])
                TS("dve", lt2, lam, -1.0, 0.0, ALU.mult, ALU.max, [pck], ["lt2"])
                TT("dve", cl, lt1, lt2, ALU.add, ["lt1", "lt2"], ["cl"])
                TS("dve", cl, cl, -8.0, None, ALU.mult, None, ["cl"], ["cl"])
                zrp = A.f32(L + 4)
                MSET("pool", zrp, 0.0, [("zrp", tt) for tt in range(4)])
                xc = A.f32(L)
                xcb = A.bf16(L)
                rr = A.f32(L)
                ii = A.f32(L)
                aa = A.f32(L)
                a2 = A.f32(L)
                hsum = A.f32(L)
                hh = rr
                gg = aa
                ycst = A.bf16(L)
                for j in range(4):
                    for tt in range(4):
                        ps, pk = PS()
                        for kc in range(8):
                            MM(ps, wlr[:, kc, j * 128:(j + 1) * 128], hT[:, kc, tt * 512:(tt + 1) * 512], kc == 0, kc == 7,
                               ["wlr", ("h", kc, tt)], [pk], kc == 7)
                        CP("act", zrp[:, 2 + tt * 512:2 + (tt + 1) * 512], ps, [pk], [("zrp", tt)])
                    zk = [("zrp", tt) for tt in range(4)]
                    cw = lambda k: pc[:, R_CW + k * 4 + j:R_CW + k * 4 + j + 1]
                    ACT(xc, zrp[:, 0:L], AF.Identity, zk + [pck], ["xc"], bias=pc[:, R_CB + j:R_CB + j + 1], scale=cw(0))
                    for k in range(1, 4):
                        STT("dve" if k != 2 else "pool", xc, zrp[:, k:k + L], cw(k), xc, ALU.mult, ALU.add, zk + ["xc", pck], ["xc"])
                    CP("act", xcb, xc, ["xc"], ["xcb"])
                    if l == DBGL and j == 0:
                        DBG("xc0", xc, ["xc"])
                        DBG("zrp0", zrp[:, 0:L], zk)
                    for d_ in range(2):
                        for (which, dst, dk, boff) in ((0, rr, "rr", R_BA), (1, ii, "ii", R_BX)):
                            for tt in range(4):
                                sl = slice(tt * 512, (tt + 1) * 512)
                                ps, pk = PS()
                                MM(ps, lbd[:, which * 8 + d_ * 4 + j, :], xcb[:, sl], True, True, ["lbd", "xcb"], [pk], True)
                                ACT(dst[:, sl], ps, AF.Sigmoid, [pk, pck], [dk], bias=pc[:, boff + d_ * 4 + j:boff + d_ * 4 + j + 1])
                        if l == DBGL and j == 0 and d_ == 0:
                            DBG("rr0", rr, ["rr"])
                            DBG("ii0", ii, ["ii"])
                            DBG("cl", cl, ["cl"])
                        ACT(aa, rr, AF.Exp, ["rr", "cl"], ["aa"], scale=cl[:, d_ * 4 + j:d_ * 4 + j + 1])
                        if l == DBGL and j == 0 and d_ == 0:
                            DBG("aa0", aa, ["aa"])
                        TT("dve", a2, aa, aa, ALU.mult, ["aa"], ["a2"])
                        ACT(a2, a2, AF.Sqrt, ["a2", "onec"], ["a2"], scale=-1.0, bias=onec[:, 0:1])
                        TT("dve", ii, ii, a2, ALU.mult, ["ii", "a2"], ["ii"])
                        TT("dve", ii, ii, xc, ALU.mult, ["ii", "xc"], ["ii"])
                        if d_ == 0:
                            SCAN(hsum, aa, ii, 0.0, ["aa", "ii"], ["hsum"])
                        else:
                            SCAN(hh[:, ::-1], aa[:, ::-1], ii[:, ::-1], 0.0, ["aa", "ii"], ["rr"])
                            TT("dve", hsum, hsum, hh, ALU.add, ["hsum", "rr"], ["hsum"])
                    for tt in range(4):
                        sl = slice(tt * 512, (tt + 1) * 512)
                        ps, pk = PS()
                        for kc in range(8):
                            MM(ps, wlr[:, kc, 512 + j * 128:512 + (j + 1) * 128], hT[:, kc, sl], kc == 0, kc == 7,
                               ["wlr", ("h", kc, tt)], [pk], kc == 7)
                        ACT(gg[:, sl], ps, AF.Gelu, [pk], ["aa"])
                    TT("dve", ycst, hsum, gg, ALU.mult, ["hsum", "aa"], ["ycst"])
                    DMA("sp", y_d[2, :, j, :], ycst, ["ycst"], [("yD", 2, tt) for tt in range(4)])
                    if l == DBGL and "yc" in dbg_d:
                        DMA("pool", dbg_d["yc"][:, j * L:(j + 1) * L], ycst, ["ycst"], [("dbgo", "yc", j)])
                S.barrier()
                A.pop()

                checkpoint(3)
                A.push()
                yast = [A.bf16(512), A.bf16(512)]
                ydst = [A.bf16(512).rearrange("p (g q) -> p g q", g=4), A.bf16(512).rearrange("p (g q) -> p g q", g=4)]
                wgm = A.bf16(8 * 1024).rearrange("p (k n) -> p k n", k=8)
                DMA("pool", wgm, W["win"][:, 2048:3072].rearrange("(k p) n -> p k n", p=128), [], ["wgm"])
                gng = A.f32(512)
                DMA("sp", gng, W["gng"], [], ["gng"])
                gws = A.bf16(4 * 128).rearrange("p (g q) -> p g q", g=4)
                DMA("pool", gws, W["gws"].rearrange("p (g q) -> p g q", g=4), [], ["gws"])
                gbs = A.bf16(512)
                DMA("pool", gbs[0:1, :], W["gbs"], [], ["gbs"])
                A.push()
                wpo = A.bf16(8 * 512).rearrange("p (k n) -> p k n", k=8)
                DMA("pool", wpo, W["win"][:, 0:512].rearrange("(k p) n -> p k n", p=128), [], ["wpo"])
                pmb = A.bf16(20 * 128).rearrange("p (g v t) -> p g v t", g=4, v=5)
                DMA("pool", pmb, pm_d.rearrange("p (g v t) -> p g v t", g=4, v=5), [], ["pmb"])
                pwb = A.bf16(4 * 128).rearrange("p (g d) -> p g d", g=4)
                DMA("pool", pwb, W["poolw"].rearrange("g c d -> c g d"), [], ["pwb"])
                utok = A.bf16(16 * 512).rearrange("p (n c) -> p n c", n=16)
                for n in range(16):
                    ps, pk = PS()
                    for kc in range(8):
                        MM(ps, hT[:, kc, n * 128:(n + 1) * 128], wpo[:, kc, :], kc == 0, kc == 7, ["wpo", ("h", kc, n // 4)], [pk], kc == 7)
                    CP("act" if n % 2 == 0 else "dve", utok[:, n, :], ps, [pk], [("utok", n)])
                    if l == DBGL and n == 1:
                        DBG("utok1", utok[:, n, :], [("utok", n)])
                pooled = A.bf16(L)
                for g in range(4):
                    for tt in range(4):
                        ps, pk = PS()
                        for n4 in range(4):
                            nt = tt * 4 + n4
                            srcs = [ns for ns in (nt - 1, nt, nt + 1) if 0 <= ns < 16]
                            for i, ns in enumerate(srcs):
                                if ns == nt - 1:
                                    v = 0
                                elif ns == nt + 1:
                                    v = 2
                                else:
                                    v = 3 if nt == 0 else (4 if nt == 15 else 1)
                                MM(ps[:, n4 * 128:(n4 + 1) * 128], utok[:, ns, g * 128:(g + 1) * 128], pmb[:, g, v, :], i == 0, i == len(srcs) - 1,
                                   [("utok", ns), "pmb"], [pk], (n4 == 3 and i == len(srcs) - 1))
                        CP("act", pooled[:, tt * 512:(tt + 1) * 512], ps, [pk], [("pooled", tt)])
                    if l == DBGL and g == 3:
                        DBG("pooled3", pooled, [("pooled", tt) for tt in range(4)])
                    for tt in range(4):
                        sl = slice(tt * 512, (tt + 1) * 512)
                        ps, pk = PS()
                        MM(ps, pwb[:, g, :], pooled[:, sl], True, True, ["pwb", ("pooled", tt)], [pk], True)
                        bb = (g * 4 + tt) % 2
                        ACT(yast[bb], ps, AF.Copy, [pk, pck], [("yast", bb)], scale=pc[:, R_PSC + g:R_PSC + g + 1])
                        DMA("sp", y_d[0, :, g, sl], yast[bb], [("yast", bb)], [("yD", 0, tt)])
                        if l == DBGL and "ya" in dbg_d:
                            DMA("pool", dbg_d["ya"][:, g * L + tt * 512:g * L + (tt + 1) * 512], yast[bb], [("yast", bb)], [("dbgo", "ya", g, tt)])
                S.barrier()
                A.pop()
                A.push()
                guT = A.bf16(4 * L).rearrange("p (c t) -> p c t", c=4)

                def sink_gu(j, tt, ps, pk):
                    ACT(guT[:, j, tt * 512:(tt + 1) * 512], ps, AF.Gelu, [pk], [("gu", j, tt)])
                proj_fm(wgm, "wgm", 4, hT, "h", range(4), sink_gu)
                gv = [A.f32(512), A.f32(512)]
                gsq = A.f32(512)
                ssq = [A.f32(1), A.f32(1)]
                vn = [A.bf16(512), A.bf16(512)]
                def gm_stage1(n):
                    b = n % 2
                    ps, pk = PS()
                    for kc in range(8):
                        MM(ps, hT[:, kc, n * 128:(n + 1) * 128], wgm[:, kc, 512:1024], kc == 0, kc == 7, ["wgm", ("h", kc, n // 4)], [pk], kc == 7)
                    ACT(gv[b], ps, AF.Gelu, [pk], [("gv", b)])
                    ACT(gsq, gv[b], AF.Square, [("gv", b)], ["gsq", ("ssq", b)], accum=ssq[b][:, 0:1])
                    ACT(ssq[b], ssq[b], AF.Sqrt, [("ssq", b), "epsc"], [("ssq", b)], bias=epsc[:, 0:1], scale=1.0 / 512)
                    RECIP(ssq[b], ssq[b], [("ssq", b)], [("ssq", b)])
                    STT("dve", vn[b], gv[b], ssq[b][:, 0:1], gng, ALU.mult, ALU.mult, [("gv", b), ("ssq", b), "gng"], [("vn", b)])
                    if l == DBGL and n == 1:
                        DBG("gv1", gv[b], [("gv", b)])
                        DBG("vn1", vn[b], [("vn", b)])
                        DBG("ssq1", ssq[b], [("ssq", b)])

                def gm_stage2(n):
                    b = n % 2
                    ps, pk = PS()
                    for g in range(4):
                        MM(ps[:, g * 128:(g + 1) * 128], vn[b][:, g * 128:(g + 1) * 128], gws[:, g, :], True, False, [("vn", b), "gws"], [pk], False)
                        MM(ps[:, g * 128:(g + 1) * 128], onesb[0:1, :], gbs[0:1, g * 128:(g + 1) * 128], False, True, ["onesb", "gbs"], [pk], g == 3)
                    TT("dve", ydst[b], ps.rearrange("p (g q) -> p g q", g=4), guT[:, :, n * 128:(n + 1) * 128], ALU.mult,
                       [pk] + [("gu", j, n // 4) for j in range(4)], [("ydst", b)])
                    DMA("sp", y_d[3, :, :, n * 128:(n + 1) * 128], ydst[b], [("ydst", b)], [("yD", 3, n // 4)])
                    if l == DBGL and "yd" in dbg_d:
                        DMA("pool", dbg_d["yd"].rearrange("p (g t) -> p g t", g=4)[:, :, n * 128:(n + 1) * 128], ydst[b], [("ydst", b)], [("dbgo", "yd", n)])

                gm_stage1(0)
                for n in range(16):
                    if n + 1 < 16:
                        gm_stage1(n + 1)
                    gm_stage2(n)
                A.pop()
                S.barrier()
                checkpoint(35)
                yh = [A.bf16(4 * 1024).rearrange("p (c t) -> p c t", c=4) for _ in range(4)]
                mrg = wgm
                wgb = [A.bf16(8 * 128).rearrange("p (k n) -> p k n", k=8) for _ in range(3)]
                wbb = [A.bf16(4 * 128).rearrange("p (k n) -> p k n", k=4) for _ in range(3)]
                wob = [A.bf16(8 * 128).rearrange("p (k n) -> p k n", k=8) for _ in range(2)]
                sgt = [gng, A.f32(512)]
                macc = [A.f32(512), A.f32(512)]
                mtmp = [A.f32(512), A.f32(512)]
                wi = 0
                yi = 0
                for hf in range(2):
                    for k in range(4):
                        DMA("sp", yh[k], y_d[k, :, :, hf * 1024:(hf + 1) * 1024], [("yD", k, hf * 2), ("yD", k, hf * 2 + 1)], [("yh", k)])
                    for dch in range(8):
                        for k in range(4):
                            b3 = wi % 3
                            wi += 1
                            DMA("pool", wgb[b3], W["wgate"][:, k * 1024 + dch * 128:k * 1024 + (dch + 1) * 128].rearrange("(kc p) n -> p kc n", p=128),
                                [], [("wgb", b3)])
                            DMA("pool", wbb[b3], W["wbr"][k, :, dch * 128:(dch + 1) * 128].rearrange("(kc p) n -> p kc n", p=128), [], [("wbb", b3)])
                            for t2 in range(2):
                                tt = hf * 2 + t2
                                sl = slice(tt * 512, (tt + 1) * 512)
                                ps, pk = PS()
                                for kc in range(8):
                                    MM(ps, wgb[b3][:, kc, :], hT[:, kc, sl], kc == 0, kc == 7, [("wgb", b3), ("h", kc, tt)], [pk], kc == 7)
                                ACT(sgt[t2], ps, AF.Sigmoid, [pk, pck], [("sgt", t2)], bias=pc[:, R_BG + k * 8 + dch:R_BG + k * 8 + dch + 1])
                                ps2, pk2 = PS()
                                for kc in range(4):
                                    MM(ps2, wbb[b3][:, kc, :], yh[k][:, kc, t2 * 512:(t2 + 1) * 512], kc == 0, kc == 3, [("wbb", b3), ("yh", k)], [pk2], kc == 3)
                                if k == 0:
                                    TT("dve", macc[t2], sgt[t2], ps2, ALU.mult, [("sgt", t2), pk2], [("macc", t2)])
                                elif k < 3:
                                    TT("dve", mtmp[t2], sgt[t2], ps2, ALU.mult, [("sgt", t2), pk2], [("mtmp", t2)])
                                    TT("dve", macc[t2], macc[t2], mtmp[t2], ALU.add, [("macc", t2), ("mtmp", t2)], [("macc", t2)])
                                else:
                                    TT("dve", mtmp[t2], sgt[t2], ps2, ALU.mult, [("sgt", t2), pk2], [("mtmp", t2)])
                                    TT("dve", mrg[:, dch, t2 * 512:(t2 + 1) * 512], macc[t2], mtmp[t2], ALU.add, [("macc", t2), ("mtmp", t2)],
                                       [("mrg", dch, t2)])
                    for do in range(8):
                        b2 = do % 2
                        DMA("pool", wob[b2], W["wout"][:, do * 128:(do + 1) * 128].rearrange("(kc p) n -> p kc n", p=128), [], [("wob", b2)])
                        for t2 in range(2):
                            tt = hf * 2 + t2
                            sl = slice(tt * 512, (tt + 1) * 512)
                            ps, pk = PS()
                            for kc in range(8):
                                MM(ps, wob[b2][:, kc, :], mrg[:, kc, t2 * 512:(t2 + 1) * 512], kc == 0, kc == 7, [("wob", b2), ("mrg", kc, t2)], [pk], kc == 7)
                            STT("dve", xT[:, do, sl], ps, modt[l][:, 16 + do:17 + do], xT[:, do, sl], ALU.mult, ALU.add,
                                [pk, ("mod", l), ("x", do, tt)], [("x", do, tt)])
                S.barrier()
                A.pop()
                A.pop()

                if l == DBGL:
                    DBG("x1", xT, [("x", c, tt) for c in range(8) for tt in range(4)])
                checkpoint(4)
                A.push()
                h2 = A.bf16(8 * 1024).rearrange("p (c t) -> p c t", c=8)
                w1b = [A.bf16(8 * 512).rearrange("p (k n) -> p k n", k=8) for _ in range(2)]
                w2b = [A.bf16(4 * 1024).rearrange("p (k n) -> p k n", k=4) for _ in range(2)]
                f1 = [A.bf16(4 * 1024).rearrange("p (c t) -> p c t", c=4) for _ in range(2)]
                fsq = [A.f32(512), A.f32(512)]
                fi = 0
                for hf in range(2):
                    rms_mod(l, 1, [hf * 2, hf * 2 + 1], h2, "h2")
                    for fg in range(8):
                        b = fi % 2
                        fi += 1
                        DMA("pool", w1b[b], W["wff1"][:, fg * 512:(fg + 1) * 512].rearrange("(kc p) n -> p kc n", p=128), [], [("w1b", b)])
                        DMA("pool", w2b[b], W["wff2"][fg * 512:(fg + 1) * 512, :].rearrange("(kc p) n -> p kc n", p=128), [], [("w2b", b)])
                        for fc in range(4):
                            for t2 in range(2):
                                tt = hf * 2 + t2
                                ps, pk = PS()
                                for kc in range(8):
                                    MM(ps, w1b[b][:, kc, fc * 128:(fc + 1) * 128], h2[:, kc, t2 * 512:(t2 + 1) * 512], kc == 0, kc == 7,
                                       [("w1b", b), ("h2", kc, tt)], [pk], kc == 7)
                                ACT(fsq[t2], ps, AF.Square, [pk], [("fsq", t2)])
                                STT("dve", f1[b][:, fc, t2 * 512:(t2 + 1) * 512], ps, 0.0, fsq[t2], ALU.is_gt, ALU.mult, [pk, ("fsq", t2)],
                                    [("f1", b, fc, t2)])
                        for do in range(8):
                            for t2 in range(2):
                                tt = hf * 2 + t2
                                sl = slice(tt * 512, (tt + 1) * 512)
                                ps, pk = PS()
                                for fc in range(4):
                                    MM(ps, w2b[b][:, fc, do * 128:(do + 1) * 128], f1[b][:, fc, t2 * 512:(t2 + 1) * 512], fc == 0, fc == 3,
                                       [("w2b", b), ("f1", b, fc, t2)], [pk], fc == 3)
                                STT("dve", xT[:, do, sl], ps, modt[l][:, 40 + do:41 + do], xT[:, do, sl], ALU.mult, ALU.add,
                                    [pk, ("mod", l), ("x", do, tt)], [("x", do, tt)])
                if l == DBGL:
                    DBG("x2", xT, [("x", c, tt) for c in range(8) for tt in range(4)])
                S.barrier()
                A.pop()
                checkpoint(5 + l)

        except _Stop:
            A.top = persist_top
            A.marks = []
            bank_state["free"] = list(range(8))
            S.barrier()
        A.push()
        sq = A.bf16(8 * 512).rearrange("p (c t) -> p c t", c=8)
        rt = A.f32(512)
        yn = A.f32(8 * 512).rearrange("p (c t) -> p c t", c=8)
        otok = [A.f32(D), A.f32(D)]
        pl = pcol[DEPTH - 1]
        outs = []
        for tt in range(4):
            sl = slice(tt * 512, (tt + 1) * 512)
            ACT(sq, xT[:, :, sl], AF.Square, [("x", c, tt) for c in range(8)], ["sq"])
            ps, pk = PS()
            for c in range(8):
                MM(ps, onesb, sq[:, c, :], c == 0, c == 7, ["sq", "onesb"], [pk], c == 7)
            ACT(rt, ps, AF.Sqrt, [pk, "epsc"], ["rt"], bias=epsc[:, 0:1], scale=1.0 / D)
            RECIP(rt, rt, ["rt"], ["rt"])
            for c in range(8):
                STT("dve" if c % 2 == 0 else "pool", yn[:, c, :], xT[:, c, sl], pl[:, R_FG + c:R_FG + c + 1], rt, ALU.mult, ALU.mult,
                    [("x", c, tt), "rt", ("pcol", DEPTH - 1)], [("yn", c)])
            for n4 in range(4):
                n = tt * 4 + n4
                ot = otok[n % 2]
                for cg in range(2):
                    ps, pk = PS()
                    for c4 in range(4):
                        c = cg * 4 + c4
                        TR(ps[:, c4 * 128:(c4 + 1) * 128], yn[:, c, n4 * 128:(n4 + 1) * 128], ident, [("yn", c), "ident"], [pk])
                    CP("act" if cg == 0 else "dve", ot[:, cg * 512:(cg + 1) * 512], ps, [pk], [("otok", n % 2, cg)])
                outs.append(DMA("sp", out_d[n * 128:(n + 1) * 128, :], ot, [("otok", n % 2, 0), ("otok", n % 2, 1)], [("out", n)]))
        A.pop()
        S.barrier()
        S.emit(block)
        print("instructions:", S.ninstr, "arena peak words:", A.peak)
    return nc


def _pool_mats():
    wins = (2, 4, 8, 16)
    Lm = 3 * 128
    out = np.zeros((128, 4, 5, 128), np.float32)
    for g, w in enumerate(wins):
        def blockfor(nt_kind):
            if nt_kind == "mid":
                Lv, t0 = 5 * 128, 2 * 128
            elif nt_kind == "first":
                Lv, t0 = 3 * 128, 0
            else:
                Lv, t0 = 3 * 128, 2 * 128
            Pm = np.zeros((Lv, Lv), np.float64)
            for t in range(Lv):
                lo = min(max(t - w // 2, 0), Lv)
                hi = min(max(t - w // 2 + w, 0), Lv)
                Pm[lo:hi, t] = 1.0 / (hi - lo)
                Pm[t, t] -= 1.0
            return Pm, t0
        Pm, t0 = blockfor("mid")
        out[:, g, 0, :] = Pm[t0 - 128:t0, t0:t0 + 128]
        out[:, g, 1, :] = Pm[t0:t0 + 128, t0:t0 + 128]
        out[:, g, 2, :] = Pm[t0 + 128:t0 + 256, t0:t0 + 128]
        Pm, t0 = blockfor("first")
        out[:, g, 3, :] = Pm[t0:t0 + 128, t0:t0 + 128]
        Pm, t0 = blockfor("last")
        out[:, g, 4, :] = Pm[t0:t0 + 128, t0:t0 + 128]
    return np.ascontiguousarray(out.reshape(128, 20 * 128))


def _prep_shared(inp):
    f = lambda a: np.ascontiguousarray(np.asarray(a, dtype=np.float32))
    sh = {"pm": _pool_mats()}
    for l in range(DEPTH):
        rows = [
            f(inp["ada_b"][l]).reshape(48, 128),
            f(inp["norm1_g"][l]).reshape(8, 128),
            f(inp["pool_scale"][l]).reshape(4, 128),
            f(inp["ssm_d"][l]).reshape(4, 128),
            f(inp["ssm_glu_b"][l]).reshape(4, 128),
            f(inp["lru_conv_w"][l]).reshape(16, 128),
            f(inp["lru_conv_b"][l]).reshape(4, 128),
            f(inp["lru_ba"][l]).reshape(8, 128),
            f(inp["lru_bx"][l]).reshape(8, 128),
            f(inp["lru_lam"][l]).reshape(8, 128),
            f(inp["b_gate"][l]).reshape(32, 128),
            f(inp["norm2_g"][l]).reshape(8, 128),
            f(inp["final_g"]).reshape(8, 128),
        ]
        sh[f"pv{l}"] = np.ascontiguousarray(np.concatenate(rows, axis=0))
        sh[f"adaw{l}"] = f(inp["ada_w"][l])
        sh[f"win{l}"] = f(inp["w_in"][l])
        sh[f"poolw{l}"] = f(inp["pool_w"][l])
        def sc(a):
            return a.reshape(2, 16, 2, 64).transpose(2, 3, 0, 1).reshape(128, 32)
        lre = f(inp["ssm_lam_re"][l])
        lim = f(inp["ssm_lam_im"][l])
        ldt = np.broadcast_to(f(inp["ssm_log_dt"][l])[:, :, None], (2, 32, 64))
        sh[f"ssc{l}"] = np.ascontiguousarray(np.concatenate([sc(lre), sc(lim), sc(ldt)], axis=1))
        def sb(a):
            return a.reshape(16, 2, 64, 16).transpose(1, 2, 0, 3).reshape(128, 16, 16)
        sh[f"ssb{l}"] = np.ascontiguousarray(np.stack([sb(f(inp["ssm_b_re"][l])), sb(f(inp["ssm_b_im"][l]))], axis=1).reshape(128, 512))
        def scm(a):
            return a.reshape(2, 16, 2, 16, 64).transpose(2, 4, 0, 1, 3).reshape(128, 2, 16, 16)
        sh[f"sscc{l}"] = np.ascontiguousarray(np.stack([scm(f(inp["ssm_c_re"][l])), scm(f(inp["ssm_c_im"][l]))], axis=1).reshape(128, 1024))
        sh[f"gluw{l}"] = f(inp["ssm_glu_w"][l])
        bd = np.zeros((128, 16, 128), np.float32)
        for which, nm in enumerate(("lru_wa", "lru_wx")):
            w = f(inp[nm][l])
            for d_ in range(2):
                for j in range(4):
                    i = which * 8 + d_ * 4 + j
                    bd[0:64, i, 0:64] = w[d_, 2 * j]
                    bd[64:128, i, 64:128] = w[d_, 2 * j + 1]
        sh[f"lrubd{l}"] = np.ascontiguousarray(bd.reshape(128, 16 * 128))
        sh[f"gng{l}"] = np.ascontiguousarray(np.broadcast_to(f(inp["gmlp_norm_g"][l])[None, :], (128, 512)))
        sh[f"gws{l}"] = np.ascontiguousarray(f(inp["gmlp_ws"][l]).transpose(2, 0, 1).reshape(128, 512))
        sh[f"gbs{l}"] = f(inp["gmlp_bs"][l]).reshape(1, 512)
        sh[f"wbr{l}"] = f(inp["w_branch"][l])
        sh[f"wgate{l}"] = f(inp["w_gate"][l])
        sh[f"wout{l}"] = f(inp["w_out"][l])
        sh[f"wff1{l}"] = f(inp["w_ff1"][l])
        sh[f"wff2{l}"] = f(inp["w_ff2"][l])
    return sh


_CACHE = {}


def kernel(**inputs):
    x = np.ascontiguousarray(np.asarray(inputs["x"], dtype=np.float32))
    c = np.ascontiguousarray(np.asarray(inputs["c"], dtype=np.float32))
    B = x.shape[0]
    sh = _prep_shared(inputs)
    if "nc" not in _CACHE:
        _CACHE["nc"] = build_program()
    nc = _CACHE["nc"]
    in_maps = []
    for b in range(B):
        m = dict(sh)
        m["x"] = x[b]
        m["cb"] = np.ascontiguousarray(c[b].reshape(8, 128))
        in_maps.append(m)
    res = run_bass_kernel_spmd(nc, in_maps, core_ids=list(range(B)))
    return np.stack([np.asarray(r["out"], dtype=np.float32) for r in res.results], axis=0)
```
